# Optimizing a Trainium2 kernel written in Bass

```python
import jax, jax.numpy as jnp
from jax import lax
import numpy as np

D_MODEL = 1024
BATCH = 4
SEQ = 4096
DEPTH = 1

NSA_HEADS = 8
NSA_KV_GROUPS = 2
NSA_HPG = NSA_HEADS // NSA_KV_GROUPS
NSA_HEAD_DIM = 64
NSA_WIDTH = NSA_HEADS * NSA_HEAD_DIM
NSA_KV_WIDTH = NSA_KV_GROUPS * NSA_HEAD_DIM
CMP_BLOCK = 32
CMP_STRIDE = 16
CMP_HIDDEN = 128
SLC_BLOCK = 64
SLC_TOPK = 16
WINDOW = 512
N_BRANCH = 3
FORCED_SCORE = 1.0e4
MLA_HEADS = 8
MLA_NOPE_DIM = 64
MLA_ROPE_DIM = 32
MLA_V_DIM = 64
MLA_WIDTH = MLA_HEADS * MLA_V_DIM
MLA_Q_RANK = 256
MLA_KV_RANK = 128
MIX_WIDTH = NSA_WIDTH + MLA_WIDTH
Q_BLOCK = 128
ROPE_THETA = 10000.0
NORM_EPS = 1e-6
NEG_INF = -1e30

IN_SIZES = (
    NSA_WIDTH,
    NSA_KV_WIDTH, NSA_KV_WIDTH,
    NSA_KV_WIDTH, NSA_KV_WIDTH,
    NSA_KV_WIDTH, NSA_KV_WIDTH,
    NSA_HEADS * N_BRANCH,
    NSA_WIDTH,
    MLA_Q_RANK,
    MLA_KV_RANK,
    MLA_ROPE_DIM,
    MLA_WIDTH,
)
IN_WIDTH = sum(IN_SIZES)
IN_OFFSETS = tuple(int(o) for o in np.cumsum(IN_SIZES)[:-1])

kernel_name = "hymba_nsa_mla_adaln_block"


def rms_norm(x, g):
    xf = x.astype(jnp.float32)
    y = xf * lax.rsqrt(jnp.mean(xf * xf, axis=-1, keepdims=True) + NORM_EPS)
    return (y * g.astype(jnp.float32)).astype(x.dtype)


def rope(x, pos):
    half = x.shape[-1] // 2
    inv = ROPE_THETA ** (-jnp.arange(half, dtype=jnp.float32) / half)
    ang = pos.astype(jnp.float32)[..., None] * inv
    cos = jnp.cos(ang)[:, :, None, :]
    sin = jnp.sin(ang)[:, :, None, :]
    xf = x.astype(jnp.float32)
    x1, x2 = xf[..., :half], xf[..., half:]
    out = jnp.concatenate([x1 * cos - x2 * sin, x2 * cos + x1 * sin], axis=-1)
    return out.astype(x.dtype)


def masked_softmax(s, mask):
    s = jnp.where(mask, s.astype(jnp.float32), NEG_INF)
    p = jax.nn.softmax(s, axis=-1)
    return jnp.where(mask, p, 0.0)


def compress(kv, cmp_pos, w1, w2):
    B, S, G, D = kv.shape
    r = CMP_BLOCK // CMP_STRIDE
    ns = S // CMP_STRIDE
    sub = kv.reshape(B, ns, CMP_STRIDE, G, D)
    nc = ns - r + 1
    blocks = jnp.concatenate([sub[:, j:j + nc] for j in range(r)], axis=2)
    blocks = blocks + cmp_pos[None, None, :, None, :]
    flat = blocks.transpose(0, 1, 3, 2, 4).reshape(B, nc, G, CMP_BLOCK * D)
    return jax.nn.silu(flat @ w1) @ w2


def cmp_to_slc_matrix(nc, nslc):
    start = np.arange(nc)[:, None] * CMP_STRIDE
    bstart = np.arange(nslc)[None, :] * SLC_BLOCK
    ov = np.minimum(start + CMP_BLOCK, bstart + SLC_BLOCK) - np.maximum(start, bstart)
    return (np.clip(ov, 0, None) / CMP_BLOCK).astype(np.float32)


def nsa_attention(q, k_cmp, v_cmp, k_slc, v_slc, k_win, v_win, gate_logits, positions,
                  cmp_pos, cmp_k_w1, cmp_k_w2, cmp_v_w1, cmp_v_w2):
    B, S = q.shape[:2]
    G, HPG, dk = NSA_KV_GROUPS, NSA_HPG, NSA_HEAD_DIM
    scale = dk ** -0.5
    nqb = S // Q_BLOCK
    t = np.arange(S)
    qg = rope(q.reshape(B, S, NSA_HEADS, dk), positions).reshape(B, S, G, HPG, dk)

    kc = compress(k_cmp.reshape(B, S, G, dk), cmp_pos, cmp_k_w1, cmp_k_w2)
    vc = compress(v_cmp.reshape(B, S, G, dk), cmp_pos, cmp_v_w1, cmp_v_w2)
    nc = kc.shape[1]
    cmp_end = np.arange(nc) * CMP_STRIDE + CMP_BLOCK - 1
    kc = rope(kc, positions[:, cmp_end])
    s_cmp = jnp.einsum('bsghd,bngd->bghsn', qg, kc).astype(jnp.float32) * scale
    p_cmp = masked_softmax(s_cmp, cmp_end[None, :] <= t[:, None])
    o_cmp = jnp.einsum('bghsn,bngd->bsghd', p_cmp.astype(vc.dtype), vc)

    nslc = S // SLC_BLOCK
    m = jnp.asarray(cmp_to_slc_matrix(nc, nslc))
    imp = jnp.einsum('bghsn,nj->bgsj', p_cmp, m)
    blk = np.arange(nslc)[None, :]
    cur = (t // SLC_BLOCK)[:, None]
    forced = (blk == 0) | (blk == cur) | (blk == cur - 1)
    imp = jnp.where(forced, FORCED_SCORE, jnp.where(blk > cur, -FORCED_SCORE, imp))
    n_sel = min(SLC_TOPK, nslc)
    _, sel_idx = lax.top_k(imp, n_sel)

    kb = rope(k_slc.reshape(B, S, G, dk), positions).reshape(B, nslc, SLC_BLOCK, G, dk).transpose(0, 3, 1, 2, 4)
    vb = v_slc.reshape(B, nslc, SLC_BLOCK, G, dk).transpose(0, 3, 1, 2, 4)
    q_chunks = qg.reshape(B, nqb, Q_BLOCK, G, HPG, dk).transpose(1, 0, 3, 2, 4, 5)
    idx_chunks = sel_idx.reshape(B, G, nqb, Q_BLOCK, n_sel).transpose(2, 0, 1, 3, 4)
    t_chunks = jnp.arange(S, dtype=jnp.int32).reshape(nqb, Q_BLOCK)
    gather = jax.vmap(jax.vmap(lambda blocks, ix: blocks[ix]))
    sb_offsets = jnp.arange(SLC_BLOCK, dtype=jnp.int32)

    def slc_block(args):
        qc, ic, tc = args
        kg = gather(kb, ic)
        vg = gather(vb, ic)
        s = jnp.einsum('bgqhd,bgqnkd->bgqhnk', qc, kg).astype(jnp.float32) * scale
        key_pos = ic[..., None] * SLC_BLOCK + sb_offsets
        mask = (key_pos <= tc[None, None, :, None, None]).reshape(B, G, Q_BLOCK, 1, n_sel * SLC_BLOCK)
        p = masked_softmax(s.reshape(B, G, Q_BLOCK, HPG, n_sel * SLC_BLOCK), mask)
        p = p.reshape(B, G, Q_BLOCK, HPG, n_sel, SLC_BLOCK)
        return jnp.einsum('bgqhnk,bgqnkd->bqghd', p.astype(vg.dtype), vg)

    o_slc = lax.map(slc_block, (q_chunks, idx_chunks, t_chunks))
    o_slc = o_slc.transpose(1, 0, 2, 3, 4, 5).reshape(B, S, G, HPG, dk)

    n_back = WINDOW // Q_BLOCK

    def band(a):
        ap = jnp.pad(a, ((0, 0), (WINDOW, 0), (0, 0), (0, 0)))
        ab = ap.reshape(B, nqb + n_back, Q_BLOCK, G, dk)
        return jnp.concatenate([ab[:, j:j + nqb] for j in range(n_back + 1)], axis=2)

    kband = band(rope(k_win.reshape(B, S, G, dk), positions))
    vband = band(v_win.reshape(B, S, G, dk))
    qw = qg.reshape(B, nqb, Q_BLOCK, G, HPG, dk)
    s_win = jnp.einsum('bnqghd,bnkgd->bnghqk', qw, kband).astype(jnp.float32) * scale
    q_idx = np.arange(nqb)[:, None] * Q_BLOCK + np.arange(Q_BLOCK)[None, :]
    k_idx = np.arange(nqb)[:, None] * Q_BLOCK - WINDOW + np.arange((n_back + 1) * Q_BLOCK)[None, :]
    diff = q_idx[:, :, None] - k_idx[:, None, :]
    win_mask = (diff >= 0) & (diff < WINDOW) & (k_idx[:, None, :] >= 0)
    p_win = masked_softmax(s_win, win_mask[None, :, None, None])
    o_win = jnp.einsum('bnghqk,bnkgd->bnqghd', p_win.astype(vband.dtype), vband).reshape(B, S, G, HPG, dk)

    g = jax.nn.sigmoid(gate_logits.astype(jnp.float32)).reshape(B, S, G, HPG, N_BRANCH).astype(q.dtype)
    o = g[..., 0:1] * o_cmp + g[..., 1:2] * o_slc + g[..., 2:3] * o_win
    return o.reshape(B, S, NSA_WIDTH)


def mla_attention(c_q, c_kv, k_rope, positions, q_norm_g, w_q_up, kv_norm_g, w_kv_up):
    B, S = c_q.shape[:2]
    H = MLA_HEADS
    nqb = S // Q_BLOCK
    q = (rms_norm(c_q, q_norm_g) @ w_q_up).reshape(B, S, H, MLA_NOPE_DIM + MLA_ROPE_DIM)
    q_nope = q[..., :MLA_NOPE_DIM]
    q_rot = rope(q[..., MLA_NOPE_DIM:], positions)
    kv = (rms_norm(c_kv, kv_norm_g) @ w_kv_up).reshape(B, S, H, MLA_NOPE_DIM + MLA_V_DIM)
    k_nope, v = kv[..., :MLA_NOPE_DIM], kv[..., MLA_NOPE_DIM:]
    k_rot = rope(k_rope[:, :, None, :], positions)[:, :, 0]
    scale = (MLA_NOPE_DIM + MLA_ROPE_DIM) ** -0.5
    qn_c = q_nope.reshape(B, nqb, Q_BLOCK, H, MLA_NOPE_DIM).transpose(1, 0, 2, 3, 4)
    qr_c = q_rot.reshape(B, nqb, Q_BLOCK, H, MLA_ROPE_DIM).transpose(1, 0, 2, 3, 4)
    t_chunks = jnp.arange(S, dtype=jnp.int32).reshape(nqb, Q_BLOCK)
    key_idx = jnp.arange(S, dtype=jnp.int32)

    def mla_block(args):
        qn, qr, tc = args
        s = (jnp.einsum('bqhd,bkhd->bhqk', qn, k_nope)
             + jnp.einsum('bqhd,bkd->bhqk', qr, k_rot)).astype(jnp.float32) * scale
        p = masked_softmax(s, key_idx[None, :] <= tc[:, None])
        return jnp.einsum('bhqk,bkhd->bqhd', p.astype(v.dtype), v)

    o = lax.map(mla_block, (qn_c, qr_c, t_chunks))
    return o.transpose(1, 0, 2, 3, 4).reshape(B, S, MLA_WIDTH)


def setup_inputs(seed: int = 0) -> dict:
    key = jax.random.key(seed)
    ks = jax.random.split(key, 20)
    f32 = jnp.float32
    L, D = DEPTH, D_MODEL

    def nrm(k, shape, fan_in, gain=1.0):
        return jax.random.normal(k, shape, f32) * (gain * fan_in ** -0.5)

    def gain(k, shape):
        return 1.0 + 0.02 * jax.random.normal(k, shape, f32)

    x = jax.random.normal(ks[0], (BATCH, SEQ, D), f32)
    c = jax.random.normal(ks[1], (BATCH, D), f32)
    offset = jax.random.randint(ks[2], (BATCH, 1), 0, 1024, dtype=jnp.int32)
    positions = offset + jnp.arange(SEQ, dtype=jnp.int32)[None, :]
    return {
        "x": x,
        "c": c,
        "positions": positions,
        "ada_w": nrm(ks[3], (L, D, 3 * D), D, 0.5),
        "ada_b": 0.01 * jax.random.normal(ks[4], (L, 3 * D), f32),
        "norm_g": gain(ks[5], (L, D)),
        "w_in": nrm(ks[6], (L, D, IN_WIDTH), D),
        "cmp_pos": 0.02 * jax.random.normal(ks[7], (L, CMP_BLOCK, NSA_HEAD_DIM), f32),
        "cmp_k_w1": nrm(ks[8], (L, CMP_BLOCK * NSA_HEAD_DIM, CMP_HIDDEN), CMP_BLOCK * NSA_HEAD_DIM),
        "cmp_k_w2": nrm(ks[9], (L, CMP_HIDDEN, NSA_HEAD_DIM), CMP_HIDDEN),
        "cmp_v_w1": nrm(ks[10], (L, CMP_BLOCK * NSA_HEAD_DIM, CMP_HIDDEN), CMP_BLOCK * NSA_HEAD_DIM),
        "cmp_v_w2": nrm(ks[11], (L, CMP_HIDDEN, NSA_HEAD_DIM), CMP_HIDDEN),
        "q_norm_g": gain(ks[12], (L, MLA_Q_RANK)),
        "w_q_up": nrm(ks[13], (L, MLA_Q_RANK, MLA_HEADS * (MLA_NOPE_DIM + MLA_ROPE_DIM)), MLA_Q_RANK),
        "kv_norm_g": gain(ks[14], (L, MLA_KV_RANK)),
        "w_kv_up": nrm(ks[15], (L, MLA_KV_RANK, MLA_HEADS * (MLA_NOPE_DIM + MLA_V_DIM)), MLA_KV_RANK),
        "w_out": nrm(ks[16], (L, MIX_WIDTH, D), MIX_WIDTH),
        "final_norm_g": gain(ks[17], (D,)),
    }


def reference(x, c, positions, ada_w, ada_b, norm_g, w_in, cmp_pos, cmp_k_w1, cmp_k_w2,
              cmp_v_w1, cmp_v_w2, q_norm_g, w_q_up, kv_norm_g, w_kv_up, w_out, final_norm_g):
    for l in range(DEPTH):
        mod = jax.nn.silu(c) @ ada_w[l] + ada_b[l]
        shift, scl, gate = jnp.split(mod, 3, axis=-1)
        h = rms_norm(x, norm_g[l]) * (1.0 + scl[:, None, :]) + shift[:, None, :]

        proj = h @ w_in[l]
        (q_n, kc_n, vc_n, ks_n, vs_n, kw_n, vw_n, gl_n, z_nsa,
         cq_m, ckv_m, kr_m, z_mla) = jnp.split(proj, IN_OFFSETS, axis=-1)

        o_nsa = nsa_attention(q_n, kc_n, vc_n, ks_n, vs_n, kw_n, vw_n, gl_n, positions,
                              cmp_pos[l], cmp_k_w1[l], cmp_k_w2[l], cmp_v_w1[l], cmp_v_w2[l])
        o_mla = mla_attention(cq_m, ckv_m, kr_m, positions,
                              q_norm_g[l], w_q_up[l], kv_norm_g[l], w_kv_up[l])

        mixed = jnp.concatenate([o_nsa * jax.nn.silu(z_nsa), o_mla * jax.nn.silu(z_mla)], axis=-1)
        x = x + gate[:, None, :] * (mixed @ w_out[l])
    return rms_norm(x, final_norm_g)
```

```python
import contextlib
import numpy as np
import ml_dtypes
import concourse.bass as bass
import concourse.mybir as mybir
from concourse.bass_utils import run_bass_kernel_spmd

F32 = mybir.dt.float32
BF16 = mybir.dt.bfloat16
I32 = mybir.dt.int32
ALU = mybir.AluOpType
AF = mybir.ActivationFunctionType

S_LEN = 4096
D = 1024
NT = 32
NOWN = 16
IN_W = 2744
PI = float(np.pi)

ENGS = ("pe", "act", "dve", "pool", "sp")
SAME_ENG_SYNC = {"pe": False, "act": True, "dve": True, "pool": True, "sp": False}
N_DMA_SLOTS = 10


class Res:
    __slots__ = ("name", "w", "r", "excl", "multi", "ws")

    def __init__(self, name, excl=False, multi=False):
        self.name = name
        self.w = None
        self.r = {}
        self.excl = excl
        self.multi = multi
        self.ws = {}


class Sched:
    def __init__(self, nc, stack):
        self.nc = nc
        self.sems = {}
        for e in ENGS:
            self.sems[e] = stack.enter_context(nc.semaphore("s_" + e))
        self.dq = ("sp", "pool")
        for q in self.dq:
            for i in range(N_DMA_SLOTS):
                self.sems[("d", q, i)] = stack.enter_context(nc.semaphore(f"d_{q}{i}"))
        self.cnt = {k: 0 for k in self.sems}
        self.ops = {e: [] for e in ENGS}
        self.seen = {e: {} for e in ENGS}
        self.dslot = {q: 0 for q in self.dq}
        self.dead = False

    def _wait(self, eng, key, val):
        if self.dead:
            return
        if self.seen[eng].get(key, 0) >= val:
            return
        self.seen[eng][key] = val
        sem = self.sems[key]
        self.ops[eng].append(lambda e, sem=sem, val=val: e.wait_ge(sem, val))

    def _deps(self, eng, reads, writes, extra=()):
        deps = {}

        def add(tok):
            if tok is None:
                return
            k, v = tok
            if deps.get(k, 0) < v:
                deps[k] = v
        for r in reads:
            add(r.w)
            if r.multi:
                for k, v in r.ws.items():
                    add((k, v))
            if r.excl:
                for k, v in r.r.items():
                    if k != eng:
                        add((k, v))
        for w in writes:
            if w.multi:
                continue
            add(w.w)
            for k, v in w.r.items():
                add((k, v))
        for t in extra:
            add(t)
        for k, v in deps.items():
            if k == eng and not SAME_ENG_SYNC[eng]:
                continue
            self._wait(eng, k, v)

    def _mark(self, tok, reads, writes):
        k, v = tok
        for r in reads:
            if r.r.get(k, 0) < v:
                r.r[k] = v
        for w in writes:
            if w.multi:
                if w.ws.get(k, 0) < v:
                    w.ws[k] = v
                continue
            w.w = tok
            w.r = {}

    def op(self, eng, meth, kw, reads=(), writes=()):
        if self.dead:
            return None
        self._deps(eng, reads, writes)
        sem = self.sems[eng]
        self.cnt[eng] += 1
        tok = (eng, self.cnt[eng])
        self.ops[eng].append(lambda e, meth=meth, kw=kw, sem=sem: getattr(e, meth)(**kw).then_inc(sem, 1))
        self._mark(tok, reads, writes)
        return tok

    def dma(self, q, out, in_, reads=(), writes=(), **kw):
        if self.dead:
            return None
        slot = self.dslot[q]
        self.dslot[q] = (slot + 1) % N_DMA_SLOTS
        key = ("d", q, slot)
        prev = (key, self.cnt[key]) if self.cnt[key] > 0 else None
        self._deps(q, reads, writes, extra=(prev,))
        self.cnt[key] += 16
        tok = (key, self.cnt[key])
        sem = self.sems[key]
        self.ops[q].append(lambda e, out=out, in_=in_, sem=sem, kw=kw:
                           e.dma_start(out=out, in_=in_, **kw).then_inc(sem, 16))
        self._mark(tok, reads, writes)
        return tok

    def wait_all_dma(self, eng):
        for q in self.dq:
            for i in range(N_DMA_SLOTS):
                k = ("d", q, i)
                if self.cnt[k]:
                    self._wait(eng, k, self.cnt[k])

    def flush(self):
        if not any(self.ops[e] for e in ENGS):
            return
        nc = self.nc
        ops = self.ops
        with nc.Block() as block:
            @block.tensor
            def _(e):
                for f in ops["pe"]:
                    f(e)

            @block.scalar
            def _(e):
                for f in ops["act"]:
                    f(e)

            @block.vector
            def _(e):
                for f in ops["dve"]:
                    f(e)

            @block.gpsimd
            def _(e):
                for f in ops["pool"]:
                    f(e)

            @block.sync
            def _(e):
                for f in ops["sp"]:
                    f(e)
        self.ops = {e: [] for e in ENGS}


O_Q, O_KC, O_VC, O_KS, O_VS, O_KW, O_VW, O_GL, O_ZN, O_CQ, O_CKV, O_KR, O_ZM = (
    0, 512, 640, 768, 896, 1024, 1152, 1280, 1304, 1816, 2072, 2200, 2232)
NSA_COLS = 1816
MLA_COLS = IN_W - NSA_COLS


class _Stop(Exception):
    pass


def build(dbg_names=(), stop=99):
    nc = bass.Bass("TRN2", target_bir_lowering=False)
    dram = {}

    def din(name, shape, dt=F32):
        dram[name] = nc.dram_tensor(name, list(shape), dt, kind="ExternalInput").ap()
        return dram[name]

    xs = din("xs", [S_LEN, D])
    pos_i = din("pos_i", [S_LEN], I32)
    valid_d = din("valid", [S_LEN])
    c_b = din("c_b", [D])
    ada_w = din("ada_w", [D, 3 * D]); ada_b = din("ada_b", [3 * D]); norm_g = din("norm_g", [D])
    w_in = din("w_in", [D, IN_W]); cmp_pos = din("cmp_pos", [32, 64])
    ckw1 = din("cmp_k_w1", [2048, 128]); ckw2 = din("cmp_k_w2", [128, 64])
    cvw1 = din("cmp_v_w1", [2048, 128]); cvw2 = din("cmp_v_w2", [128, 64])
    q_norm_g = din("q_norm_g", [256]); w_q_up = din("w_q_up", [256, 768])
    kv_norm_g = din("kv_norm_g", [128]); w_kv_up = din("w_kv_up", [128, 1024])
    w_out = din("w_out", [D, D]); fin_g = din("final_norm_g", [D])
    inv32_d = din("inv32", [32]); inv16_d = din("inv16", [16])
    invF_d = din("invF", [128, 1]); sgnF_d = din("sgnF", [128, 1])
    E_d = din("E_aug", [64, S_LEN]); ident_d = din("ident", [128, 128])
    trile_d = din("tri_le", [128, 128]); trigt_d = din("tri_gt", [128, 128])
    cmpmask_d = din("cmpmask", [256, NOWN * 128])
    impkeep_d = din("impkeep", [NOWN * 128, 64]); impforce_d = din("impforce", [NOWN * 128, 64])
    m1_d = din("m1", [256, 64])
    valid_pt = din("valid_pt", [128, NT]); pos_pt = din("pos_pt", [128, NT], I32)
    c_col = din("c_col", [128, 8]); g_col = din("g_col", [128, 8]); gq_col = din("gq_col", [128, 2]); gkv_col = din("gkv_col", [128, 1])
    posT_d = din("cmp_posT", [64, 32]); validc_pt = din("validc_pt", [128, 2]); cpos_pt = din("cpos_pt", [128, 2], I32)
    out_d = nc.dram_tensor("out", [NOWN * 128, D], F32, kind="ExternalOutput").ap()
    dbg_out = {}

    with contextlib.ExitStack() as gst:
        S = Sched(nc, gst)
        ckpt_n = [0]

        def checkpoint():
            ckpt_n[0] += 1
            if ckpt_n[0] == stop:
                S.wait_all_dma("sp")
                S.flush()
                S.dead = True

        uniq = [0]

        def sb(st, name, shape, dt=F32):
            uniq[0] += 1
            return st.enter_context(nc.sbuf_tensor(f"t{uniq[0]}_{name}", list(shape), dt))

        def dbg(name, ap, shape, dt=F32, reads=()):
            if name in dbg_names:
                d = nc.dram_tensor("dbg_" + name, list(shape), dt, kind="ExternalOutput").ap()
                dbg_out[name] = d
                S.dma("sp", d, ap, reads=reads)

        banks = [gst.enter_context(nc.psum_tensor(f"bank{i}", [128, 512], F32)) for i in range(8)]
        rbank = [Res(f"bank{i}", excl=True) for i in range(8)]

        def bk(i):
            return banks[i]

        def bk16(i):
            return banks[i][:].bitcast(BF16)

        ident = sb(gst, "ident", [128, 128]); identb = sb(gst, "identb", [128, 128], BF16)
        trile = sb(gst, "trile", [128, 128], BF16); trigt = sb(gst, "trigt", [128, 128], BF16)
        ones_f = sb(gst, "ones_f", [128, 128])
        scl1 = sb(gst, "scl1", [128, 8]); shf = sb(gst, "shf", [128, 8])
        gate_bc = sb(gst, "gate_bc", [128, D]); fing_bc = sb(gst, "fing_bc", [128, D])
        valid = sb(gst, "validc", [128, NT]); posf = sb(gst, "posf", [128, NT])
        rstd_all = sb(gst, "rstd_all", [128, NT]); rrstd = Res("rstd")
        mixA = sb(gst, "mixA", [128, NOWN, 512], BF16); mixB = sb(gst, "mixB", [128, NOWN, 512], BF16)
        rconst = Res("const", multi=True); rmod = Res("mod"); rmixA = [Res(f"mixA{i}") for i in range(NOWN)]
        rmixB = Res("mixB"); rmixBt = [Res(f"mixBt{i}") for i in range(NOWN)]

        S.dma("sp", ident[:], ident_d[:, :], writes=[rconst])
        S.dma("pool", identb[:], ident_d[:, :], writes=[rconst])
        S.dma("pool", trile[:], trile_d[:, :], writes=[rconst])
        S.dma("pool", trigt[:], trigt_d[:, :], writes=[rconst])
        S.dma("sp", valid[:], valid_pt[:, :], writes=[rconst])
        S.dma("sp", fing_bc[:], fin_g.partition_broadcast(128), writes=[rconst])
        S.op("dve", "memset", dict(ap=ones_f[:], constant=1.0), writes=[rconst])

        with contextlib.ExitStack() as st:
            ccol = sb(st, "ccol", [128, 8]); scol = sb(st, "scol", [128, 8], BF16)
            gcol = sb(st, "gcol", [128, 8]); posi = sb(st, "posi", [128, NT], I32)
            adab = sb(st, "adab", [1, 3 * D]); modrow = sb(st, "modrow", [1, 3 * D])
            awb = [sb(st, f"awb{i}", [128, 8, 512], BF16) for i in range(4)]
            rawb = [Res(f"awb{i}") for i in range(4)]; rc = Res("ccol", multi=True); rrow = Res("modrow")
            S.dma("sp", ccol[:], c_col[:, :], writes=[rc])
            S.dma("sp", gcol[:], g_col[:, :], writes=[rc])
            S.dma("sp", posi[:], pos_pt[:, :], writes=[rc])
            S.dma("sp", adab[:], ada_b.rearrange("(o n) -> o n", o=1), writes=[rc])
            S.op("act", "activation", dict(out=scol[:], in_=ccol[:], func=AF.Silu), reads=[rc], writes=[rc])
            S.op("dve", "tensor_copy", dict(out=posf[:], in_=posi[:]), reads=[rc], writes=[rconst])
            aw_v = ada_w.rearrange("(c p) n -> p c n", p=128)
            for n in range(6):
                b = n % 4
                S.dma("pool", awb[b][:], aw_v[:, :, n * 512:(n + 1) * 512], writes=[rawb[b]])
                for c in range(8):
                    S.op("pe", "matmul", dict(out=bk(n % 2)[0:1, :], lhsT=scol[:, c:c + 1], rhs=awb[b][:, c, :],
                                                              start=(c == 0), stop=(c == 7)),
                         reads=[rc, rawb[b]], writes=[rbank[n % 2]])
                S.op("dve", "tensor_tensor", dict(out=modrow[:, n * 512:(n + 1) * 512], in0=bk(n % 2)[0:1, :],
                                                        in1=adab[:, n * 512:(n + 1) * 512], op=ALU.add),
                     reads=[rbank[n % 2], rc], writes=[rrow])
            for j in range(16):
                S.op("pe", "matmul", dict(out=bk(2)[:, j:j + 1], lhsT=modrow[:, j * 128:(j + 1) * 128], rhs=ones_f[0:1, 0:1],
                                                start=True, stop=True), reads=[rrow, rconst], writes=[rbank[2]])
            S.op("dve", "tensor_copy", dict(out=shf[:], in_=bk(2)[:, 0:8]), reads=[rbank[2]], writes=[rmod])
            S.op("dve", "scalar_tensor_tensor", dict(out=scl1[:], in0=bk(2)[:, 8:16], scalar=1.0, in1=gcol[:],
                                                         op0=ALU.add, op1=ALU.mult), reads=[rbank[2], rc], writes=[rmod])
            for hh in range(2):
                S.op("pe", "matmul", dict(out=bk(3)[:, :], lhsT=ones_f[0:1, :], rhs=modrow[:, 2048 + hh * 512:2048 + (hh + 1) * 512],
                                                  start=True, stop=True), reads=[rrow, rconst], writes=[rbank[3]])
                S.op("dve", "tensor_copy", dict(out=gate_bc[:, hh * 512:(hh + 1) * 512], in_=bk(3)[:, :]),
                     reads=[rbank[3]], writes=[rmod])
            xpre = [sb(st, f"xpre{i}", [128, D]) for i in range(6)]; rxpre = [Res(f"xpre{i}") for i in range(6)]
            jpre = sb(st, "jpre", [128, D], BF16)
            for t in range(NT):
                b3 = t % 6
                S.dma("sp", xpre[b3][:], xs[t * 128:(t + 1) * 128, :], writes=[rxpre[b3]])
                S.op("act", "activation", dict(out=jpre[:], in_=xpre[b3][:], func=AF.Square, accum_out=rstd_all[:, t:t + 1]),
                     reads=[rxpre[b3]], writes=[rrstd])
            S.op("dve", "tensor_scalar", dict(out=rstd_all[:], in0=rstd_all[:], scalar1=1.0 / D, scalar2=1e-6, op0=ALU.mult, op1=ALU.add),
                 reads=[rrstd], writes=[rrstd])
            S.op("act", "activation", dict(out=rstd_all[:], in_=rstd_all[:], func=AF.Sqrt), reads=[rrstd], writes=[rrstd])
            S.op("dve", "reciprocal", dict(out=rstd_all[:], in_=rstd_all[:]), reads=[rrstd], writes=[rrstd])
            dbg("scl1", scl1[:], [128, 8], reads=[rmod]); dbg("shf", shf[:], [128, 8], reads=[rmod])
            dbg("gate", gate_bc[0:1, :], [1, D], reads=[rmod])
            S.flush()
            checkpoint()

        def rope_tables(st, half, name):
            cosT = sb(st, name + "cos", [128, NT, half]); sinT = sb(st, name + "sin", [128, NT, half])
            with contextlib.ExitStack() as t2:
                invb = sb(t2, name + "inv", [128, half]); ang = sb(t2, name + "ang", [128, NT, half])
                ki = sb(t2, name + "ki", [128, NT, half], I32); kf = sb(t2, name + "kf", [128, NT, half])
                rr = Res(name + "tmp"); rt = Res(name + "tab")
                S.dma("sp", invb[:], (inv32_d if half == 32 else inv16_d).partition_broadcast(128), writes=[rr])
                S.op("dve", "tensor_tensor", dict(out=ang[:], in0=posf[:].unsqueeze(2).to_broadcast([128, NT, half]),
                                                      in1=invb[:].unsqueeze(1).to_broadcast([128, NT, half]), op=ALU.mult),
                     reads=[rr, rconst], writes=[rr])
                for tab, off in ((sinT, 0.0), (cosT, PI / 2)):
                    S.op("dve", "tensor_scalar", dict(out=ki[:], in0=ang[:], scalar1=off, scalar2=1.0 / (2 * PI),
                                                                 op0=ALU.add, op1=ALU.mult), reads=[rr], writes=[rr])
                    S.op("dve", "tensor_copy", dict(out=kf[:], in_=ki[:]), reads=[rr], writes=[rr])
                    S.op("dve", "scalar_tensor_tensor", dict(out=kf[:], in0=kf[:], scalar=-2 * PI, in1=ang[:],
                                                                 op0=ALU.mult, op1=ALU.add), reads=[rr], writes=[rr])
                    S.op("dve", "tensor_scalar", dict(out=kf[:], in0=kf[:], scalar1=off, scalar2=PI,
                                                                 op0=ALU.add, op1=ALU.min), reads=[rr], writes=[rr])
                    S.op("dve", "tensor_scalar", dict(out=kf[:], in0=kf[:], scalar1=-PI, scalar2=None, op0=ALU.max),
                         reads=[rr], writes=[rr])
                    S.op("act", "activation", dict(out=tab[:], in_=kf[:], func=AF.Sin), reads=[rr], writes=[rr, rt])
                S.flush()
                checkpoint()
            return cosT, sinT, rt

        def rope_tm(src, dst, nh, half, cosv, sinv, tmp, reads, writes):
            n = nh * half
            npart = src.shape[0]
            cb = cosv.unsqueeze(1).to_broadcast([npart, nh, half]); sbv = sinv.unsqueeze(1).to_broadcast([npart, nh, half])
            x1 = src[:, :, 0:half]; x2 = src[:, :, half:2 * half]
            t1 = tmp[:, 0:n].rearrange("p (h d) -> p h d", h=nh); t2 = tmp[:, n:2 * n].rearrange("p (h d) -> p h d", h=nh)
            S.op("dve", "tensor_tensor", dict(out=t1, in0=x1, in1=cb, op=ALU.mult), reads=reads, writes=[rtmp_rope])
            S.op("dve", "tensor_tensor", dict(out=t2, in0=x2, in1=sbv, op=ALU.mult), reads=reads, writes=[rtmp_rope])
            S.op("dve", "tensor_tensor", dict(out=dst[:, :, 0:half], in0=t1, in1=t2, op=ALU.subtract),
                 reads=[rtmp_rope], writes=writes)
            S.op("dve", "tensor_tensor", dict(out=t1, in0=x2, in1=cb, op=ALU.mult), reads=reads, writes=[rtmp_rope])
            S.op("dve", "tensor_tensor", dict(out=t2, in0=x1, in1=sbv, op=ALU.mult), reads=reads, writes=[rtmp_rope])
            S.op("dve", "tensor_tensor", dict(out=dst[:, :, half:2 * half], in0=t1, in1=t2, op=ALU.add),
                 reads=[rtmp_rope], writes=writes)

        rtmp_rope = Res("ropetmp")

        def make_hT_a(t, xt, xn, rxt, rxn):
            b = t % 2
            S.dma("sp", xt[b][:], xs[t * 128:(t + 1) * 128, :], writes=[rxt[b]])
            S.op("act", "activation", dict(out=xn[b][:], in_=xt[b][:], func=AF.Copy, scale=rstd_all[:, t:t + 1]),
                 reads=[rxt[b], rrstd], writes=[rxn[b]])

        def make_hT_b(t, xn, hT, rxn, rhT, n_dve=4):
            b = t % 2
            hb = (t // 4) % 2
            for c in range(8):
                tb = 5 + c // 4
                S.op("pe", "transpose", dict(out=bk16(tb)[:, (c % 4) * 128:(c % 4 + 1) * 128], in_=xn[b][:, c * 128:(c + 1) * 128],
                                             identity=identb[:]), reads=[rxn[b], rconst], writes=[rbank[tb]])
            col = (t % 4) * 128
            for c in range(8):
                tb = 5 + c // 4
                src = bk16(tb)[:, (c % 4) * 128:(c % 4 + 1) * 128]
                if c < n_dve:
                    S.op("dve", "tensor_scalar", dict(out=hT[hb][:, c, col:col + 128], in0=src,
                                                      scalar1=scl1[:, c:c + 1], scalar2=shf[:, c:c + 1], op0=ALU.mult, op1=ALU.add),
                         reads=[rbank[tb], rmod], writes=[rhT[hb]])
                else:
                    S.op("act", "activation", dict(out=hT[hb][:, c, col:col + 128], in_=src,
                                                   func=AF.Identity, bias=shf[:, c:c + 1], scale=scl1[:, c:c + 1]),
                         reads=[rbank[tb], rmod], writes=[rhT[hb]])

        def make_hT(t, xt, xn, hT, rxt, rxn, rhT):
            make_hT_a(t, xt, xn, rxt, rxn)
            make_hT_b(t, xn, hT, rxn, rhT)

        junk_s = sb(gst, "junk_s", [128, 8]); rjunk = Res("junk")
        rvalid_dummy = None

        def idle(n):
            for _ in range(n):
                yield

        def run_staggered(gens, lag, max_active=2, gate_key=None):
            pending = list(enumerate(gens))
            active = []
            while pending or active:
                if pending and len(active) < max_active and (not active or active[-1]["steps"] >= lag):
                    k, gen = pending.pop(0)
                    active.append({"k": k, "gen": gen, "steps": 0, "parked": False, "main": False})
                for ent in list(active):
                    if ent["parked"]:
                        key = gate_key(ent["k"])
                        if any(o["main"] and gate_key(o["k"]) == key for o in active if o is not ent):
                            continue
                        ent["parked"] = False
                        ent["main"] = True
                    try:
                        r = next(ent["gen"])
                        ent["steps"] += 1
                        if r == "GATE":
                            ent["parked"] = True
                        elif r == "RELEASE":
                            ent["main"] = False
                    except StopIteration:
                        active.remove(ent)

        def own_index(t):
            blk, r = divmod(t, 4)
            if blk % 2 == 1:
                return (blk // 2) * 4 + r
            return None

        with contextlib.ExitStack() as nsa:
            QT = sb(nsa, "QT", [128, 8, NOWN * 128], BF16)
            KE = sb(nsa, "KE", [128, 2, S_LEN], BF16)
            KW = sb(nsa, "KW", [64, 2, S_LEN], BF16)
            VS = sb(nsa, "VS", [128, NT, 2, 65], BF16); VW = sb(nsa, "VW", [128, NT, 2, 65], BF16)
            GS = sb(nsa, "GS", [128, NOWN, 24])
            mixBf = mixB[:].rearrange("p a b -> p (a b)")
            kcmpT = mixBf[:, 0:S_LEN]; vcmpT = mixBf[:, S_LEN:2 * S_LEN]
            rQT = [Res(f"QT{i}") for i in range(NOWN)]; rQS = [[Res(f"QS{i}_{g}") for g in range(2)] for i in range(NOWN)]
            rKE = Res("KE"); rKW = Res("KW"); rVS = Res("VS"); rVW = Res("VW"); rGS = Res("GS")
            for g in range(2):
                S.dma("pool", KE[64:128, g, :], E_d[:, :], writes=[rKE])
            for V, rV in ((VS, rVS), (VW, rVW)):
                S.op("dve", "tensor_copy", dict(out=V[:, :, :, 64], in_=valid[:].unsqueeze(2).to_broadcast([128, NT, 2])),
                     reads=[rconst], writes=[rV])

            W1k = sb(nsa, "W1k", [128, 32, 128], BF16)
            W2 = [sb(nsa, f"W2{j}", [128, 64], BF16) for j in range(2)]
            posT = sb(nsa, "posT", [128, 32], BF16)
            rW = Res("W1", multi=True)
            with contextlib.ExitStack() as st:
                cos32, sin32, rtab = rope_tables(st, 32, "r32")
                wn = sb(st, "wn", [128, 8, NSA_COLS], BF16); rwn = Res("wn", multi=True)
                w_v = w_in.rearrange("(c p) n -> p c n", p=128)
                W_KC, W_VC, W_KS, W_KW, W_VS, W_VW, W_GL, W_ZN = 512, 640, 768, 896, 1024, 1152, 1280, 1304
                segs = ((0, O_Q, 512), (W_KC, O_KC, 128), (W_VC, O_VC, 128), (W_KS, O_KS, 128), (W_KW, O_KW, 128),
                        (W_VS, O_VS, 128), (W_VW, O_VW, 128), (W_GL, O_GL, 24), (W_ZN, O_ZN, 512))
                for (d0, s0, n_) in segs:
                    S.dma("pool", wn[:, :, d0:d0 + n_], w_v[:, :, s0:s0 + n_], writes=[rwn])
                vk_ = ckw1.rearrange("(l d) j -> d l j", d=64)
                S.dma("pool", W1k[0:64, :, :], vk_, writes=[rW]); S.dma("pool", W1k[64:128, :, :], vk_, writes=[rW])
                S.dma("pool", W2[0][:], ckw2[:, :], writes=[rW]); S.dma("pool", W2[1][:], cvw2[:, :], writes=[rW])
                for hh in range(2):
                    S.dma("pool", posT[hh * 64:(hh + 1) * 64, :], posT_d[:, :], writes=[rW])
                xt = [sb(st, f"xt{i}", [128, D]) for i in range(2)]; rxt = [Res("xt0"), Res("xt1")]
                xn = [sb(st, f"xn{i}", [128, D], BF16) for i in range(2)]; rxn = [Res("xn0"), Res("xn1")]
                hT = [sb(st, f"hT{i}", [128, 8, 512], BF16) for i in range(2)]; rhT = [Res("hT0"), Res("hT1")]
                rtmp = sb(st, "ropetmp", [128, 1024]); ktm = sb(st, "ktm", [128, 256], BF16); rktm = Res("ktm")
                qtm = sb(st, "qtm", [128, 512], BF16); rqtm = Res("qtm")
                for r in range(4):
                    make_hT(r, xt, xn, hT, rxt, rxn, rhT)
                for blk in range(8):
                    hb = blk % 2
                    for j, (off, dstT) in enumerate(((W_KC, kcmpT), (W_VC, vcmpT))):
                        for c in range(8):
                            S.op("pe", "matmul", dict(out=bk(j)[:, :], lhsT=wn[:, c, off:off + 128], rhs=hT[hb][:, c, :],
                                                      start=(c == 0), stop=(c == 7)), reads=[rwn, rhT[hb]], writes=[rbank[j]])
                        S.op("act", "activation", dict(out=dstT[:, blk * 512:(blk + 1) * 512], in_=bk(j)[:, :], func=AF.Copy),
                             reads=[rbank[j]], writes=[rmixB])
                    for r in range(4):
                        t = blk * 4 + r
                        if blk + 1 < 8:
                            make_hT_a((blk + 1) * 4 + r, xt, xn, rxt, rxn)
                        for c in range(8):
                            S.op("pe", "matmul", dict(out=bk(2)[:, :], lhsT=hT[hb][:, c, r * 128:(r + 1) * 128], rhs=wn[:, c, W_KS:W_KS + 512],
                                                      start=(c == 0), stop=(c == 7)), reads=[rwn, rhT[hb]], writes=[rbank[2]])
                        ps = bk(2)
                        i = own_index(t)
                        if i is not None:
                            for c in range(8):
                                S.op("pe", "matmul", dict(out=bk(3)[:, :], lhsT=hT[hb][:, c, r * 128:(r + 1) * 128], rhs=wn[:, c, 0:512],
                                                          start=(c == 0), stop=(c == 7)), reads=[rwn, rhT[hb]], writes=[rbank[3]])
                            for c in range(8):
                                S.op("pe", "matmul", dict(out=bk(4)[:, :], lhsT=hT[hb][:, c, r * 128:(r + 1) * 128], rhs=wn[:, c, W_ZN:W_ZN + 512],
                                                          start=(c == 0), stop=(c == 7)), reads=[rwn, rhT[hb]], writes=[rbank[4]])
                            for c in range(8):
                                S.op("pe", "matmul", dict(out=bk(1)[:, 0:24], lhsT=hT[hb][:, c, r * 128:(r + 1) * 128], rhs=wn[:, c, W_GL:W_GL + 24],
                                                          start=(c == 0), stop=(c == 7)), reads=[rwn, rhT[hb]], writes=[rbank[1]])
                        for (o2, V, rV) in ((256, VS, rVS), (384, VW, rVW)):
                            src = ps[:, o2:o2 + 128].rearrange("p (g d) -> p g d", g=2)
                            if t < 4:
                                S.op("dve", "tensor_scalar", dict(out=V[:, t, :, 0:64], in0=src, scalar1=valid[:, t:t + 1], scalar2=None, op0=ALU.mult),
                                     reads=[rbank[2], rconst], writes=[rV])
                            else:
                                S.op("dve", "tensor_copy", dict(out=V[:, t, :, 0:64], in_=src), reads=[rbank[2]], writes=[rV])
                        rope_tm(ps[:, 0:256].rearrange("p (g d) -> p g d", g=4), ktm[:, :].rearrange("p (g d) -> p g d", g=4),
                                4, 32, cos32[:, t, :], sin32[:, t, :], rtmp, [rbank[2], rtab], [rktm])
                        if i is not None:
                            rope_tm(bk(3)[:, :].rearrange("p (h d) -> p h d", h=8), qtm[:].rearrange("p (h d) -> p h d", h=8),
                                    8, 32, cos32[:, t, :], sin32[:, t, :], rtmp, [rbank[3], rtab], [rqtm])
                            S.op("act", "activation", dict(out=mixA[:, i, :], in_=bk(4)[:, :], func=AF.Silu), reads=[rbank[4]], writes=[rmixA[i]])
                            S.op("act", "activation", dict(out=GS[:, i, :], in_=bk(1)[:, 0:24], func=AF.Sigmoid), reads=[rbank[1]], writes=[rGS])
                        if blk + 1 < 8:
                            make_hT_b((blk + 1) * 4 + r, xn, hT, rxn, rhT, n_dve=0)
                        for idx in range(2):
                            S.op("pe", "transpose", dict(out=bk16(7)[:, idx * 128:(idx + 1) * 128], in_=ktm[:, idx * 128:(idx + 1) * 128], identity=identb[:]),
                                 reads=[rktm, rconst], writes=[rbank[7]])
                        for idx, (KT, rK) in enumerate(((KE, rKE), (KW, rKW))):
                            for g in range(2):
                                S.op("dve", "tensor_copy", dict(out=KT[0:64, g, t * 128:(t + 1) * 128],
                                                                in_=bk16(7)[g * 64:(g + 1) * 64, idx * 128:(idx + 1) * 128]),
                                     reads=[rbank[7]], writes=[rK])
                        if i is None:
                            continue
                        for pr in range(4):
                            S.op("pe", "transpose", dict(out=bk16(3)[:, pr * 128:(pr + 1) * 128], in_=qtm[:, pr * 128:(pr + 1) * 128],
                                                         identity=identb[:]), reads=[rqtm, rconst], writes=[rbank[3]])
                        qsrc = bk16(3)[:, 0:512].rearrange("p (a q) -> p a q", a=4)
                        for hf in range(2):
                            S.op("dve", "tensor_copy", dict(out=QT[0:64, hf:8:2, i * 128:(i + 1) * 128], in_=qsrc[hf * 64:(hf + 1) * 64, :, :]),
                                 reads=[rbank[3]], writes=[rQT[i]])
                dbg("QT", QT[0:64, :, :], [64, 8, NOWN * 128], BF16, reads=rQT)
                dbg("KE", KE[:, :, :], [128, 2, S_LEN], BF16, reads=[rKE])
                dbg("KW", KW[:, :, :], [64, 2, S_LEN], BF16, reads=[rKW])
                dbg("VS", VS[:, :, :, :], [128, NT, 2, 65], BF16, reads=[rVS])
                dbg("kcmpT", kcmpT, [128, S_LEN], BF16, reads=[rmixB])
                dbg("GS", GS[:, :, :], [128, NOWN, 24], reads=[rGS])
                S.flush()
                checkpoint()

            with contextlib.ExitStack() as st:
                W1 = [W1k, sb(st, "W1v", [128, 32, 128], BF16)]
                cst = sb(st, "cst", [128, 2])
                hid = sb(st, "hid", [128, 256], BF16); ctmp = sb(st, "ctmp", [128, 64])
                kcT = sb(st, "kcT", [64, 2, 256], BF16)
                RC = sb(st, "RC", [128, 2, 2, 128], BF16)
                m1 = sb(st, "m1", [128, 2, 64]); vcv = sb(st, "vcv", [128, 2]); cposi = sb(st, "cposi", [128, 2], I32)
                cposf = sb(st, "cposf", [128, 2]); kctm = sb(st, "kctm", [128, 64], BF16)
                rcst = Res("cst"); rhid = Res("hid"); rkcT = Res("kcT"); rRC = Res("RC"); rk2 = Res("kctm")
                vv_ = cvw1.rearrange("(l d) j -> d l j", d=64)
                S.dma("pool", W1[1][0:64, :, :], vv_, writes=[rW]); S.dma("pool", W1[1][64:128, :, :], vv_, writes=[rW])
                S.dma("sp", m1[:, 0, :], m1_d[0:128, :], writes=[rW]); S.dma("sp", m1[:, 1, :], m1_d[128:256, :], writes=[rW])
                v16 = valid_d.rearrange("(n s) -> n s", s=16); p16 = pos_i.rearrange("(n s) -> n s", s=16)
                S.dma("sp", vcv[:, :], validc_pt[:, :], writes=[rW])
                S.dma("sp", cposi[:, :], cpos_pt[:, :], writes=[rW])
                S.op("dve", "tensor_copy", dict(out=cposf[:], in_=cposi[:]), reads=[rW], writes=[rW])
                ccos = sb(st, "ccos", [128, 2, 32]); csin = sb(st, "csin", [128, 2, 32])
                cinv = sb(st, "cinv", [128, 32]); cang = sb(st, "cang", [128, 2, 32]); cki = sb(st, "cki", [128, 2, 32], I32)
                ckf = sb(st, "ckf", [128, 2, 32])
                S.dma("sp", cinv[:], inv32_d.partition_broadcast(128), writes=[rW])
                S.op("dve", "tensor_tensor", dict(out=cang[:], in0=cposf[:].unsqueeze(2).to_broadcast([128, 2, 32]),
                                                      in1=cinv[:].unsqueeze(1).to_broadcast([128, 2, 32]), op=ALU.mult), reads=[rW], writes=[rW])
                for tab, off in ((csin, 0.0), (ccos, PI / 2)):
                    S.op("dve", "tensor_scalar", dict(out=cki[:], in0=cang[:], scalar1=off, scalar2=1.0 / (2 * PI), op0=ALU.add, op1=ALU.mult),
                         reads=[rW], writes=[rW])
                    S.op("dve", "tensor_copy", dict(out=ckf[:], in_=cki[:]), reads=[rW], writes=[rW])
                    S.op("dve", "scalar_tensor_tensor", dict(out=ckf[:], in0=ckf[:], scalar=-2 * PI, in1=cang[:], op0=ALU.mult, op1=ALU.add),
                         reads=[rW], writes=[rW])
                    S.op("dve", "tensor_scalar", dict(out=ckf[:], in0=ckf[:], scalar1=off, scalar2=PI, op0=ALU.add, op1=ALU.min),
                         reads=[rW], writes=[rW])
                    S.op("dve", "tensor_scalar", dict(out=ckf[:], in0=ckf[:], scalar1=-PI, scalar2=None, op0=ALU.max), reads=[rW], writes=[rW])
                    S.op("act", "activation", dict(out=tab[:], in_=ckf[:], func=AF.Sin), reads=[rW], writes=[rW])
                for j in range(2):
                    for l in range(32):
                        S.op("pe", "matmul", dict(out=bk(0)[:, j:j + 1], lhsT=W1[j][0:64, l, :], rhs=posT[0:64, l:l + 1],
                                                             start=(l == 0), stop=(l == 31)), reads=[rW], writes=[rbank[0]])
                S.op("dve", "tensor_copy", dict(out=cst[:], in_=bk(0)[:, 0:2]), reads=[rbank[0]], writes=[rcst])
                S.op("dve", "memset", dict(ap=RC[:], constant=0.0), writes=[rRC])
                S.op("dve", "memset", dict(ap=kcT[:], constant=0.0), writes=[rkcT])
                cmpD = sb(st, "cmpD", [128, 2, 16, 256], BF16); rcmpD = Res("cmpD")
                for j, srcT in enumerate((kcmpT, vcmpT)):
                    S.op("dve", "tensor_copy", dict(out=cmpD[:, j, :, :], in_=srcT.rearrange("p (n s) -> p s n", s=16)), reads=[rmixB], writes=[rcmpD])
                HB = {(0, 0): 1, (0, 1): 3, (1, 0): 4, (1, 1): 5}
                hid2 = [hid, sb(st, "hid_b", [128, 256], BF16)]; rhid2 = [rhid, Res("hid_b")]
                for g in range(2):
                    rows = slice(g * 64, (g + 1) * 64)
                    for j in range(2):
                        hbk = HB[(g, j)]
                        for l in range(32):
                            S.op("pe", "matmul", dict(out=bk(hbk)[:, 0:255], lhsT=W1[j][rows, l, :],
                                                      rhs=cmpD[rows, j, l % 16, l // 16:l // 16 + 255],
                                                      start=(l == 0), stop=(l == 31)),
                                 reads=[rW, rcmpD], writes=[rbank[hbk]])
                for g in range(2):
                    for j in range(2):
                        hbk = HB[(g, j)]
                        hx = (2 * g + j) % 2
                        S.op("act", "activation", dict(out=hid2[hx][:, 0:255], in_=bk(hbk)[:, 0:255], func=AF.Silu, bias=cst[:, j:j + 1]),
                             reads=[rbank[hbk], rcst], writes=[rhid2[hx]])
                        for ch in range(2):
                            nn = 128 if ch == 0 else 127
                            ob = 2 if ch == 0 else 6
                            S.op("pe", "matmul", dict(out=bk(ob)[0:nn, 0:64], lhsT=hid2[hx][:, ch * 128:ch * 128 + nn], rhs=W2[j][:, :],
                                                      start=True, stop=True), reads=[rhid2[hx], rW], writes=[rbank[ob]])
                            if j == 0:
                                rope_tm(bk(ob)[0:nn, 0:64].rearrange("p (h d) -> p h d", h=1), kctm[0:nn, :].rearrange("p (h d) -> p h d", h=1),
                                        1, 32, ccos[0:nn, ch, :], csin[0:nn, ch, :], ctmp[0:nn, :], [rbank[ob], rW], [rk2])
                                S.op("pe", "transpose", dict(out=bk16(7)[0:64, 0:nn], in_=kctm[0:nn, :], identity=identb[0:nn, 0:nn]),
                                     reads=[rk2, rconst], writes=[rbank[7]])
                                S.op("dve", "tensor_copy", dict(out=kcT[:, g, ch * 128:ch * 128 + nn], in_=bk16(7)[0:64, 0:nn]),
                                     reads=[rbank[7]], writes=[rkcT])
                            else:
                                S.op("dve", "tensor_scalar", dict(out=RC[0:nn, ch, g, 0:64], in0=bk(ob)[0:nn, 0:64],
                                                                  scalar1=vcv[0:nn, ch:ch + 1], scalar2=None, op0=ALU.mult),
                                     reads=[rbank[ob], rW], writes=[rRC])
                for g in range(2):
                    for ch in range(2):
                        S.op("dve", "tensor_copy", dict(out=RC[:, ch, g, 64:65], in_=vcv[:, ch:ch + 1]), reads=[rW], writes=[rRC])
                        S.op("dve", "tensor_scalar", dict(out=RC[:, ch, g, 65:128], in0=m1[:, ch, 0:63], scalar1=vcv[:, ch:ch + 1],
                                                                       scalar2=None, op0=ALU.mult), reads=[rW], writes=[rRC])
                dbg("kcT", kcT[:, :, :], [64, 2, 256], BF16, reads=[rkcT])
                dbg("RC", RC[:, :, :, :], [128, 2, 2, 128], BF16, reads=[rRC])
                S.flush()
                checkpoint()

                cmask = sb(st, "cmask", [128, 2, NOWN * 128], BF16)
                iforce = sb(st, "iforce", [128, NOWN, 64])
                rcm = Res("cmask", multi=True)
                for ch in range(2):
                    S.dma("pool", cmask[:, ch, :], cmpmask_d[ch * 128:(ch + 1) * 128, :], writes=[rcm])
                S.dma("sp", iforce[:], impforce_d.rearrange("(i p) j -> p i j", p=128), writes=[rcm])
                Pt = [sb(st, f"Pt{i}", [128, 512], BF16) for i in range(6)]; rPt = [Res(f"Pt{i}") for i in range(6)]
                accS = [sb(st, f"accS{i}", [65, 512]) for i in range(2)]; raccS = [Res("accS0"), Res("accS1")]
                NPS = 4
                cmpS = [sb(st, f"cmpS{i}", [128, 4, 128]) for i in range(NPS)]; rcmpS = [Res(f"cmpS{i}") for i in range(NPS)]
                sm = [sb(st, f"sm{i}", [128, 16]) for i in range(NPS)]; rsm = [Res(f"sm{i}") for i in range(NPS)]
                imp = [sb(st, f"imp{i}", [128, 64]) for i in range(NPS)]; imp2 = [sb(st, f"imp2{i}", [128, 64]) for i in range(NPS)]
                imp3 = [sb(st, f"imp3{i}", [128, 64]) for i in range(NPS)]
                m16 = [sb(st, f"m16{i}", [128, 16]) for i in range(NPS)]
                selb = [sb(st, f"selb{i}", [128, 64], BF16) for i in range(NPS)]; rimp = [Res(f"imp{i}") for i in range(NPS)]
                ocmp = [sb(st, f"ocmp{i}", [128, 4, 64]) for i in range(NPS)]; rocmp = [Res(f"ocmp{i}") for i in range(NPS)]
                oacc = [sb(st, f"oacc{i}", [128, 4, 64]) for i in range(NPS)]; roacc = [Res(f"oacc{i}") for i in range(NPS)]
                wts = [sb(st, f"wts{i}", [128, 3, 4]) for i in range(NPS)]
                Pp = [sb(st, f"Pp{i}", [128, 512], BF16) for i in range(2)]; rPp = [Res("Pp0"), Res("Pp1")]
                for x_ in range(NPS):
                    S.op("dve", "memset", dict(ap=imp[x_][:], constant=0.0), writes=[rimp[x_]])

                def attn_units(units, sx, accb, group_starts=(0,), before_pv=None):
                    n = len(units)
                    before_pv = before_pv or {}

                    def emit_score(u):
                        b = 2 * sx + (u % 2)
                        S.op("pe", "matmul", dict(out=bk(b)[:, :], lhsT=units[u][0], rhs=units[u][1], start=True, stop=True),
                             reads=units[u][2], writes=[rbank[b]])
                    emit_score(0)
                    if n > 1:
                        emit_score(1)
                    for u in range(n):
                        _, _, _, maskt, scale, vl, vrd = units[u]
                        b = 2 * sx + (u % 2)
                        p = 3 * sx + (u % 3)
                        S.op("act", "activation", dict(out=Pt[p][:, :], in_=bk(b)[:, :], func=AF.Exp, scale=scale),
                             reads=[rbank[b]], writes=[rPt[p]])
                        if maskt is not None:
                            pv = Pt[p][:, :].rearrange("p (h q) -> p h q", h=4)
                            S.op("dve", "tensor_tensor", dict(out=pv, in0=pv, in1=maskt[:].unsqueeze(1).to_broadcast([128, 4, 128]), op=ALU.mult),
                                 reads=[rPt[p], rconst], writes=[rPt[p]])
                        if u + 2 < n:
                            emit_score(u + 2)
                        if u in before_pv:
                            before_pv[u]()
                        S.op("pe", "matmul", dict(out=bk(accb)[0:65, :], lhsT=vl, rhs=Pt[p][:, :],
                                                  start=(u in group_starts), stop=(u + 1 in group_starts or u == n - 1)),
                             reads=[rPt[p]] + vrd, writes=[rbank[accb]])
                        yield

                def finalize_T(accb, si):
                    S.op("dve", "tensor_copy", dict(out=accS[si][:, :], in_=bk(accb)[0:65, :]), reads=[rbank[accb]], writes=[raccS[si]])
                    for h in range(4):
                        S.op("pe", "transpose", dict(out=bk(6)[:, h * 65:(h + 1) * 65], in_=accS[si][:, h * 128:(h + 1) * 128],
                                                     identity=ident[0:65, 0:65]), reads=[raccS[si], rconst], writes=[rbank[6]])

                b7lock = {"owner": None}

                def nsa_stream(k, i, g):
                    sx = k % 2
                    ps = k % NPS
                    qt = 8 * (i // 4) + 4 + (i % 4)
                    qs = slice(i * 128, (i + 1) * 128)
                    hs = slice(4 * g, 4 * g + 4)
                    accb = 4 + sx
                    SM, IMP, IMP2, IMP3, M16, SELB = sm[ps], imp[ps], imp2[ps], imp3[ps], m16[ps], selb[ps]
                    OC, OA, WT, CS = ocmp[ps], oacc[ps], wts[ps], cmpS[ps]
                    rSM, rIMP, rOC, rOA, rCS = rsm[ps], rimp[ps], rocmp[ps], roacc[ps], rcmpS[ps]
                    nch = 2 if 8 * qt + 6 >= 128 else 1
                    while b7lock["owner"] not in (None, k):
                        yield
                    b7lock["owner"] = k
                    for ch in range(nch):
                        S.op("pe", "matmul", dict(out=bk(7)[:, :], lhsT=kcT[:, g, ch * 128:(ch + 1) * 128], rhs=QT[0:64, hs, qs],
                                                  start=True, stop=True), reads=[rkcT, rQT[i]], writes=[rbank[7]])
                        yield from idle(3)
                        S.op("act", "activation", dict(out=Pp[ch][:, :], in_=bk(7)[:, :], func=AF.Exp, scale=0.125),
                             reads=[rbank[7]], writes=[rPp[ch]])
                        yield from idle(3)
                        pv = Pp[ch][:, :].rearrange("p (h q) -> p h q", h=4)
                        S.op("dve", "tensor_tensor", dict(out=pv, in0=pv, in1=cmask[:, ch, qs].unsqueeze(1).to_broadcast([128, 4, 128]), op=ALU.mult),
                             reads=[rPp[ch], rcm], writes=[rPp[ch]])
                        yield
                    yield from idle(2)
                    for h in range(4):
                        for ch in range(nch):
                            S.op("pe", "matmul", dict(out=bk(7)[:, h * 128:(h + 1) * 128], lhsT=Pp[ch][:, h * 128:(h + 1) * 128], rhs=RC[:, ch, g, :],
                                                      start=(ch == 0), stop=(ch == nch - 1)), reads=[rPp[ch], rRC], writes=[rbank[7]])
                    yield from idle(3)
                    S.op("dve", "tensor_copy", dict(out=CS[:, :, :], in_=bk(7)[:, :].rearrange("p (h c) -> p h c", h=4)), reads=[rbank[7]], writes=[rCS])
                    b7lock["owner"] = None
                    yield
                    S.op("dve", "tensor_scalar", dict(out=SM[:, 0:4], in0=CS[:, :, 64], scalar1=1e-30, scalar2=None, op0=ALU.max), reads=[rCS], writes=[rSM])
                    S.op("dve", "reciprocal", dict(out=SM[:, 4:8], in_=SM[:, 0:4]), reads=[rSM], writes=[rSM])
                    glv0 = GS[:, i, 12 * g:12 * g + 12].rearrange("p (h b) -> p b h", b=3)[:, 0, :]
                    S.op("dve", "tensor_tensor", dict(out=WT[:, 0, :], in0=glv0, in1=SM[:, 4:8], op=ALU.mult), reads=[rGS, rSM], writes=[rSM])
                    S.op("dve", "tensor_tensor", dict(out=OA[:], in0=CS[:, :, 0:64], in1=WT[:, 0, :].unsqueeze(2).to_broadcast([128, 4, 64]), op=ALU.mult),
                         reads=[rCS, rSM], writes=[rOA])
                    yield
                    for h in range(4):
                        if h == 0:
                            S.op("dve", "tensor_scalar", dict(out=IMP[:, 0:63], in0=CS[:, h, 65:128], scalar1=SM[:, 4:5], scalar2=None, op0=ALU.mult),
                                 reads=[rCS, rSM], writes=[rIMP])
                        else:
                            S.op("dve", "scalar_tensor_tensor", dict(out=IMP[:, 0:63], in0=CS[:, h, 65:128], scalar=SM[:, 4 + h:5 + h], in1=IMP[:, 0:63],
                                                                     op0=ALU.mult, op1=ALU.add), reads=[rCS, rSM, rIMP], writes=[rIMP])
                    yield
                    S.op("dve", "tensor_tensor", dict(out=IMP2[:], in0=IMP[:], in1=iforce[:, i, :], op=ALU.max), reads=[rIMP, rcm], writes=[rIMP])
                    S.op("dve", "max", dict(out=M16[:, 0:8], in_=IMP2[:]), reads=[rIMP], writes=[rIMP])
                    S.op("dve", "match_replace", dict(out=IMP3[:], in_to_replace=M16[:, 0:8], in_values=IMP2[:], imm_value=-1e30),
                         reads=[rIMP], writes=[rIMP])
                    yield
                    S.op("dve", "max", dict(out=M16[:, 8:16], in_=IMP3[:]), reads=[rIMP], writes=[rIMP])
                    S.op("dve", "tensor_scalar", dict(out=SELB[:], in0=IMP2[:], scalar1=M16[:, 15:16], scalar2=-30000.0, op0=ALU.is_lt, op1=ALU.mult),
                         reads=[rIMP], writes=[rIMP])
                    yield from idle(6)
                    while b7lock["owner"] not in (None, k):
                        yield
                    S.op("pe", "transpose", dict(out=bk16(7)[0:64, 0:128], in_=SELB[:, :], identity=identb[:]), reads=[rIMP, rconst], writes=[rbank[7]])
                    S.op("dve", "tensor_copy", dict(out=QT[64:128, hs, qs], in_=bk16(7)[0:64, 0:128].unsqueeze(1).to_broadcast([64, 4, 128])),
                         reads=[rbank[7]], writes=[rQS[i][g]])
                    yield "GATE"
                    units = []
                    for kt in range(qt + 1):
                        ks = slice(kt * 128, (kt + 1) * 128)
                        units.append((KE[:, g, ks], QT[:, hs, qs], [rKE, rQT[i], rQS[i][g]],
                                      trile if kt == qt else None, 0.125, VS[:, kt, g, :], [rVS]))
                    n_slc = len(units)
                    glv = GS[:, i, 12 * g:12 * g + 12].rearrange("p (h b) -> p b h", b=3)

                    def slc_finalize_rest():
                        for h_ in range(4):
                            S.op("pe", "transpose", dict(out=bk(6)[:, h_ * 65:(h_ + 1) * 65], in_=accS[sx][:, h_ * 128:(h_ + 1) * 128],
                                                         identity=ident[0:65, 0:65]), reads=[raccS[sx], rconst], writes=[rbank[6]])
                        S.op("dve", "reciprocal", dict(out=SM[:, 12:16], in_=bk(6)[:, 64:260:65]), reads=[rbank[6]], writes=[rSM])
                        S.op("dve", "tensor_tensor", dict(out=WT[:, 1, :], in0=glv[:, 1, :], in1=SM[:, 12:16], op=ALU.mult), reads=[rGS, rSM], writes=[rSM])
                        S.op("dve", "tensor_tensor", dict(out=OC[:], in0=bk(6)[:, 0:260].rearrange("p (h d) -> p h d", h=4)[:, :, 0:64],
                                                          in1=WT[:, 1, :].unsqueeze(2).to_broadcast([128, 4, 64]), op=ALU.mult),
                             reads=[rbank[6], rSM, rOA], writes=[rOC])
                        S.op("dve", "tensor_tensor", dict(out=OA[:], in0=OA[:], in1=OC[:], op=ALU.add), reads=[rOC, rOA], writes=[rOA])

                    for kt in range(qt - 4, qt + 1):
                        ks = slice(kt * 128, (kt + 1) * 128)
                        mk = trile if kt == qt else (trigt if kt == qt - 4 else None)
                        units.append((KW[:, g, ks], QT[0:64, hs, qs], [rKW, rQT[i]], mk, 0.125, VW[:, kt, g, :], [rVW]))

                    def copy_out():
                        S.op("dve", "tensor_copy", dict(out=accS[sx][:, :], in_=bk(accb)[0:65, :]), reads=[rbank[accb]], writes=[raccS[sx]])

                    for n_, _ in enumerate(attn_units(units, sx, accb, group_starts=(0, n_slc), before_pv={n_slc: copy_out})):
                        yield
                        if n_ == n_slc + 1:
                            slc_finalize_rest()
                            yield
                    S.op("dve", "tensor_copy", dict(out=accS[sx][:, :], in_=bk(accb)[0:65, :]), reads=[rbank[accb]], writes=[raccS[sx]])
                    yield "RELEASE"
                    yield from idle(3)
                    for h_ in range(4):
                        S.op("pe", "transpose", dict(out=bk(6)[:, h_ * 65:(h_ + 1) * 65], in_=accS[sx][:, h_ * 128:(h_ + 1) * 128],
                                                     identity=ident[0:65, 0:65]), reads=[raccS[sx], rconst], writes=[rbank[6]])
                    S.op("dve", "reciprocal", dict(out=SM[:, 12:16], in_=bk(6)[:, 64:260:65]), reads=[rbank[6]], writes=[rSM])
                    S.op("dve", "tensor_tensor", dict(out=WT[:, 2, :], in0=glv[:, 2, :], in1=SM[:, 12:16], op=ALU.mult), reads=[rGS, rSM], writes=[rSM])
                    S.op("dve", "tensor_tensor", dict(out=OC[:], in0=bk(6)[:, 0:260].rearrange("p (h d) -> p h d", h=4)[:, :, 0:64],
                                                      in1=WT[:, 2, :].unsqueeze(2).to_broadcast([128, 4, 64]), op=ALU.mult),
                         reads=[rbank[6], rSM, rOA], writes=[rOC])
                    S.op("dve", "tensor_tensor", dict(out=OA[:], in0=OA[:], in1=OC[:], op=ALU.add), reads=[rOC, rOA], writes=[rOA])
                    mv = mixA[:, i, 256 * g:256 * g + 256].rearrange("p (h d) -> p h d", h=4)
                    S.op("dve", "tensor_tensor", dict(out=mv, in0=mv, in1=OA[:], op=ALU.mult), reads=[rOA, rmixA[i]], writes=[rmixA[i]])
                    yield

                order = [x for j in range(8) for g in range(2) for x in ((j, g), (j + 8, g))]
                run_staggered([nsa_stream(k, i, g) for k, (i, g) in enumerate(order)], 3, max_active=4, gate_key=lambda k: k % 2)
                dbg("mixA", mixA[:, :, :], [128, NOWN, 512], BF16, reads=rmixA)
                S.flush()
                checkpoint()

        wo = sb(gst, "wo", [128, 8, D], BF16); rwo = Res("wo", multi=True)
        wo_v = w_out.rearrange("(c p) n -> p c n", p=128)
        with contextlib.ExitStack() as st:
            cos16, sin16, rtab16 = rope_tables(st, 16, "r16")
            wm = sb(st, "wm", [128, 8, MLA_COLS], BF16); rwm = Res("wm", multi=True)
            w_v = w_in.rearrange("(c p) n -> p c n", p=128)
            for c in range(8):
                S.dma("pool", wm[:, c, :], w_v[:, c, NSA_COLS:IN_W], writes=[rwm])
            for c in range(8):
                S.dma("pool", wo[:, c, :], wo_v[:, c, :], writes=[rwo])
            cqnT = sb(st, "cqnT", [128, 2, NOWN * 128], BF16); ckvnT = sb(st, "ckvnT", [128, S_LEN], BF16)
            KH = [sb(st, f"KH{i}", [96, S_LEN], BF16) for i in range(2)]
            VH = [sb(st, f"VH{i}", [128, NT, 65], BF16) for i in range(2)]
            rcqn = Res("cqnT"); rckv = Res("ckvnT"); rKHrot = [Res("KHrot0"), Res("KHrot1")]; rKH = [Res("KH0"), Res("KH1")]
            rVH = [Res("VH0"), Res("VH1")]
            wq = sb(st, "wq", [128, 2, 768], BF16); wqp = sb(st, "wqp", [128, 2, 8, 96], BF16); wkv = sb(st, "wkv", [128, 1024], BF16)
            wqf = sb(st, "wqf", [128, 2, 768]); wkvf = sb(st, "wkvf", [128, 1024]); gq = sb(st, "gq", [128, 2]); gkv = sb(st, "gkv", [128, 1])
            rwq = Res("wq", multi=True)
            S.dma("sp", wqf[:], w_q_up.rearrange("(c p) n -> p c n", p=128), writes=[rwq])
            S.dma("sp", wkvf[:], w_kv_up[:, :], writes=[rwq])
            S.dma("sp", gq[:], gq_col[:, :], writes=[rwq])
            S.dma("sp", gkv[:], gkv_col[:, :], writes=[rwq])
            for c in range(2):
                S.op("dve", "tensor_scalar", dict(out=wq[:, c, :], in0=wqf[:, c, :], scalar1=gq[:, c:c + 1], scalar2=None, op0=ALU.mult),
                     reads=[rwq], writes=[rwq])
                wv4 = wq[:, c, :].rearrange("p (h d) -> p h d", h=8)
                S.op("dve", "tensor_copy", dict(out=wqp[:, c, :, 0:64], in_=wv4[:, :, 0:64]), reads=[rwq], writes=[rwq])
                S.op("dve", "tensor_copy", dict(out=wqp[:, c, :, 64:80], in_=wv4[:, :, 80:96]), reads=[rwq], writes=[rwq])
                S.op("dve", "tensor_copy", dict(out=wqp[:, c, :, 80:96], in_=wv4[:, :, 64:80]), reads=[rwq], writes=[rwq])
            S.op("dve", "tensor_scalar", dict(out=wkv[:], in0=wkvf[:], scalar1=gkv[:, 0:1], scalar2=None, op0=ALU.mult), reads=[rwq], writes=[rwq])
            for hb in range(2):
                S.op("dve", "tensor_copy", dict(out=VH[hb][:, :, 64], in_=valid[:]), reads=[rconst], writes=[rVH[hb]])

            cosF = sb(st, "cosF", [128, NOWN * 128]); sinF = sb(st, "sinF", [128, NOWN * 128])
            rF = Res("ropeF")
            with contextlib.ExitStack() as t2:
                pFi = sb(t2, "pFi", [128, NOWN * 128], I32); angF = sb(t2, "angF", [128, NOWN * 128])
                kFi = sb(t2, "kFi", [128, NOWN * 128], I32); invF = sb(t2, "invF", [128, 1]); sgnF = sb(t2, "sgnF", [128, 1])
                rq = Res("pF", multi=True)
                for s_ in range(4):
                    S.dma("sp", pFi[64:96, s_ * 512:(s_ + 1) * 512],
                          pos_i[(2 * s_ + 1) * 512:(2 * s_ + 2) * 512].partition_broadcast(32), writes=[rq])
                S.dma("sp", invF[:], invF_d[:, :], writes=[rq]); S.dma("sp", sgnF[:], sgnF_d[:, :], writes=[rq])
                R = slice(64, 96)
                S.op("dve", "tensor_copy", dict(out=angF[R, :], in_=pFi[R, :]), reads=[rq], writes=[rq])
                S.op("dve", "tensor_scalar", dict(out=angF[R, :], in0=angF[R, :], scalar1=invF[R, 0:1], scalar2=None, op0=ALU.mult), reads=[rq], writes=[rq])
                for tab, off in ((sinF, 0.0), (cosF, PI / 2)):
                    S.op("dve", "tensor_scalar", dict(out=kFi[R, :], in0=angF[R, :], scalar1=off, scalar2=1.0 / (2 * PI), op0=ALU.add, op1=ALU.mult),
                         reads=[rq], writes=[rq])
                    S.op("dve", "tensor_copy", dict(out=tab[R, :], in_=kFi[R, :]), reads=[rq], writes=[rq, rF])
                    S.op("dve", "scalar_tensor_tensor", dict(out=tab[R, :], in0=tab[R, :], scalar=-2 * PI, in1=angF[R, :], op0=ALU.mult, op1=ALU.add),
                         reads=[rq], writes=[rq, rF])
                    S.op("dve", "tensor_scalar", dict(out=tab[R, :], in0=tab[R, :], scalar1=off, scalar2=PI, op0=ALU.add, op1=ALU.min),
                         reads=[rq], writes=[rq, rF])
                    S.op("dve", "tensor_scalar", dict(out=tab[R, :], in0=tab[R, :], scalar1=-PI, scalar2=None, op0=ALU.max), reads=[rq], writes=[rq, rF])
                    S.op("act", "activation", dict(out=tab[R, :], in_=tab[R, :], func=AF.Sin), reads=[rq], writes=[rq, rF])
                S.op("dve", "tensor_scalar", dict(out=sinF[R, :], in0=sinF[R, :], scalar1=sgnF[R, 0:1], scalar2=None, op0=ALU.mult), reads=[rq], writes=[rq, rF])
                S.flush()
                checkpoint()

            with contextlib.ExitStack() as t1:
                xt = [sb(t1, f"xt{i}", [128, D]) for i in range(2)]; rxt = [Res("xt0"), Res("xt1")]
                xn = [sb(t1, f"xn{i}", [128, D], BF16) for i in range(2)]; rxn = [Res("xn0"), Res("xn1")]
                hT = [sb(t1, f"hT{i}", [128, 8, 512], BF16) for i in range(2)]; rhT = [Res("hT0"), Res("hT1")]
                junk = sb(t1, "junk", [128, 256], BF16)
                rtmp = sb(t1, "ropetmp", [128, 64]); latn = [sb(t1, f"latn{i}", [128, 416], BF16) for i in range(2)]; rlat = [Res("latn0"), Res("latn1")]
                st3 = [sb(t1, f"st3{i}", [128, 2, 2]) for i in range(2)]; rst3 = [Res("st30"), Res("st31")]
                LBS = ((0, 1), (2, 4))
                for r in range(4):
                    make_hT(r, xt, xn, hT, rxt, rxn, rhT)

                def stage1(hh):
                    sq = st3[hh % 2]; rsq = rst3[hh % 2]
                    for j in range(2):
                        t = 2 * hh + j
                        blk, r = divmod(t, 4)
                        hb = blk % 2
                        i = own_index(t)
                        lbk = LBS[hh % 2][j]
                        if blk + 1 < 8:
                            make_hT_a(t + 4, xt, xn, rxt, rxn)
                        for c in range(8):
                            S.op("pe", "matmul", dict(out=bk(lbk)[:, 0:416], lhsT=hT[hb][:, c, r * 128:(r + 1) * 128], rhs=wm[:, c, 0:416],
                                                      start=(c == 0), stop=(c == 7)), reads=[rwm, rhT[hb]], writes=[rbank[lbk]])
                        for k_, (a_, n_) in enumerate(((0, 256), (256, 128))):
                            S.op("act", "activation", dict(out=junk[:, 0:n_], in_=bk(lbk)[:, a_:a_ + n_], func=AF.Square, accum_out=sq[:, k_, j:j + 1]),
                                 reads=[rbank[lbk]], writes=[rsq])
                        if i is not None:
                            for c in range(8):
                                S.op("pe", "matmul", dict(out=bk(3)[:, :], lhsT=hT[hb][:, c, r * 128:(r + 1) * 128], rhs=wm[:, c, 416:928],
                                                          start=(c == 0), stop=(c == 7)), reads=[rwm, rhT[hb]], writes=[rbank[3]])
                            S.op("act", "activation", dict(out=mixB[:, i, :], in_=bk(3)[:, :], func=AF.Silu), reads=[rbank[3]], writes=[rmixB, rmixBt[i]])
                        if blk + 1 < 8:
                            make_hT_b(t + 4, xn, hT, rxn, rhT, n_dve=8)

                def stats(hh):
                    sq = st3[hh % 2]; rsq = rst3[hh % 2]
                    for k_, n_ in enumerate((256, 128)):
                        S.op("dve", "tensor_scalar", dict(out=sq[:, k_, :], in0=sq[:, k_, :], scalar1=1.0 / n_, scalar2=1e-6, op0=ALU.mult, op1=ALU.add),
                             reads=[rsq], writes=[rsq])
                    S.op("act", "activation", dict(out=sq[:, :, :], in_=sq[:, :, :], func=AF.Sqrt), reads=[rsq], writes=[rsq])
                    S.op("dve", "reciprocal", dict(out=sq[:, :, :], in_=sq[:, :, :]), reads=[rsq], writes=[rsq])

                def stage2(hh):
                    sq = st3[hh % 2]; rsq = rst3[hh % 2]
                    for j in range(2):
                        t = 2 * hh + j
                        i = own_index(t)
                        lb = j
                        lbk = LBS[hh % 2][j]
                        ps = bk(lbk)
                        if i is not None:
                            S.op("dve", "tensor_scalar", dict(out=latn[lb][:, 0:256], in0=ps[:, 0:256], scalar1=sq[:, 0, j:j + 1], scalar2=None, op0=ALU.mult),
                                 reads=[rbank[lbk], rsq], writes=[rlat[lb]])
                        S.op("dve", "tensor_scalar", dict(out=latn[lb][:, 256:384], in0=ps[:, 256:384], scalar1=sq[:, 1, j:j + 1], scalar2=None, op0=ALU.mult),
                             reads=[rbank[lbk], rsq], writes=[rlat[lb]])
                        rope_tm(ps[:, 384:416].rearrange("p (h d) -> p h d", h=1), latn[lb][:, 384:416].rearrange("p (h d) -> p h d", h=1),
                                1, 16, cos16[:, t, :], sin16[:, t, :], rtmp, [rbank[lbk], rtab16], [rlat[lb]])
                        S.op("pe", "transpose", dict(out=bk16(7)[:, 0:128], in_=latn[lb][:, 256:384], identity=identb[:]), reads=[rlat[lb], rconst], writes=[rbank[7]])
                        S.op("pe", "transpose", dict(out=bk16(7)[0:32, 128:256], in_=latn[lb][:, 384:416], identity=identb[:]), reads=[rlat[lb], rconst], writes=[rbank[7]])
                        if i is not None:
                            for c in range(2):
                                S.op("pe", "transpose", dict(out=bk16(7)[:, 256 + c * 128:384 + c * 128], in_=latn[lb][:, c * 128:(c + 1) * 128], identity=identb[:]),
                                     reads=[rlat[lb], rconst], writes=[rbank[7]])
                        S.op("dve", "tensor_copy", dict(out=ckvnT[:, t * 128:(t + 1) * 128], in_=bk16(7)[:, 0:128]), reads=[rbank[7]], writes=[rckv])
                        for hb2 in range(2):
                            S.op("dve", "tensor_copy", dict(out=KH[hb2][64:96, t * 128:(t + 1) * 128], in_=bk16(7)[0:32, 128:256]),
                                 reads=[rbank[7]], writes=[rKHrot[hb2]])
                        if i is not None:
                            S.op("dve", "tensor_copy", dict(out=cqnT[:, :, i * 128:(i + 1) * 128], in_=bk16(7)[:, 256:512].rearrange("p (c q) -> p c q", c=2)),
                                 reads=[rbank[7]], writes=[rcqn])

                for hh in range(16):
                    stage1(hh)
                    if hh >= 1:
                        stage2(hh - 1)
                    stats(hh)
                stage2(15)
                dbg("ckvnT", ckvnT[:, :], [128, S_LEN], BF16, reads=[rckv])
                dbg("cqnT", cqnT[:, :, :], [128, 2, NOWN * 128], BF16, reads=[rcqn])
                dbg("krot", KH[0][64:96, :], [32, S_LEN], BF16, reads=[rKHrot[0]])
                S.flush()
                checkpoint()

            with contextlib.ExitStack() as t1:
                Pt = [sb(t1, f"Pt{i}", [128, 512], BF16) for i in range(6)]; rPt = [Res(f"Pt{i}") for i in range(6)]
                accS = [sb(t1, f"accS{i}", [65, 512]) for i in range(2)]; raccS = [Res("accS0"), Res("accS1")]
                qh2 = [[sb(t1, f"qh{i}_{j}", [96, 512], BF16) for j in range(2)] for i in range(2)]; rqh2 = [[Res(f"qh{i}_{j}") for j in range(2)] for i in range(2)]
                qtmp = [sb(t1, f"qtmp{i}", [128, 512]) for i in range(2)]; rqtmp = [Res("qtmp0"), Res("qtmp1")]
                qtmp2 = [sb(t1, f"qtmpb{i}", [128, 512]) for i in range(2)]; rqtmp2 = [Res("qtmpb0"), Res("qtmpb1")]
                sm = [sb(t1, f"sm{i}", [128, 16]) for i in range(2)]; rsm = [Res("sm0"), Res("sm1")]
                otmp = [sb(t1, f"otmp{i}", [128, 4, 64]) for i in range(2)]; rot = [Res("otmp0"), Res("otmp1")]
                SC = float(96 ** -0.5)
                R = slice(64, 96)

                def mla_stream(h):
                    hb = h % 2
                    accb = 4 + hb
                    for blk in range(8):
                        gb = 7 - (blk % 2)
                        S.op("pe", "matmul", dict(out=bk(gb)[0:64, :], lhsT=wkv[:, h * 128:h * 128 + 64], rhs=ckvnT[:, blk * 512:(blk + 1) * 512],
                                                  start=True, stop=True), reads=[rwq, rckv], writes=[rbank[gb]])
                        S.op("dve", "tensor_copy", dict(out=KH[hb][0:64, blk * 512:(blk + 1) * 512], in_=bk(gb)[0:64, :]),
                             reads=[rbank[gb]], writes=[rKH[hb]])
                        yield
                    for t8 in range(4):
                        gb = 7 - (t8 % 2)
                        for tt in range(8):
                            t = t8 * 8 + tt
                            S.op("pe", "matmul", dict(out=bk(gb)[:, tt * 64:(tt + 1) * 64], lhsT=ckvnT[:, t * 128:(t + 1) * 128],
                                                      rhs=wkv[:, h * 128 + 64:h * 128 + 128], start=True, stop=True),
                                 reads=[rwq, rckv], writes=[rbank[gb]])
                        if t8 == 0:
                            for tt in range(4):
                                S.op("dve", "tensor_scalar", dict(out=VH[hb][:, tt, 0:64], in0=bk(gb)[:, tt * 64:(tt + 1) * 64], scalar1=valid[:, tt:tt + 1],
                                                                  scalar2=None, op0=ALU.mult), reads=[rbank[gb], rconst], writes=[rVH[hb]])
                            S.op("dve", "tensor_copy", dict(out=VH[hb][:, 4:8, 0:64], in_=bk(gb)[:, 256:512].rearrange("p (t d) -> p t d", t=4)),
                                 reads=[rbank[gb]], writes=[rVH[hb]])
                        else:
                            S.op("dve", "tensor_copy", dict(out=VH[hb][:, t8 * 8:(t8 + 1) * 8, 0:64], in_=bk(gb)[:, :].rearrange("p (t d) -> p t d", t=8)),
                                 reads=[rbank[gb]], writes=[rVH[hb]])
                        yield
                    def qgen(s_):
                        cs = slice(s_ * 512, (s_ + 1) * 512)
                        qd = qh2[hb][s_ % 2]; rqd = rqh2[hb][s_ % 2]
                        for c in range(2):
                            S.op("pe", "matmul", dict(out=bk(7)[0:96, :], lhsT=wq[:, c, h * 96:(h + 1) * 96], rhs=cqnT[:, c, cs],
                                                      start=(c == 0), stop=(c == 1)), reads=[rwq, rcqn], writes=[rbank[7]])
                        for c in range(2):
                            S.op("pe", "matmul", dict(out=bk(6)[0:96, :], lhsT=wqp[:, c, h, :], rhs=cqnT[:, c, cs],
                                                      start=(c == 0), stop=(c == 1)), reads=[rwq, rcqn], writes=[rbank[6]])
                        S.op("dve", "tensor_copy", dict(out=qd[0:64, :], in_=bk(7)[0:64, :]), reads=[rbank[7]], writes=[rqd])
                        S.op("dve", "tensor_tensor", dict(out=qtmp[hb][R, :], in0=bk(7)[R, :], in1=cosF[R, cs], op=ALU.mult), reads=[rbank[7], rF], writes=[rqtmp[hb]])
                        S.op("dve", "tensor_tensor", dict(out=qtmp2[hb][R, :], in0=bk(6)[R, :], in1=sinF[R, cs], op=ALU.mult), reads=[rbank[6], rF], writes=[rqtmp2[hb]])
                        S.op("dve", "tensor_tensor", dict(out=qd[R, :], in0=qtmp[hb][R, :], in1=qtmp2[hb][R, :], op=ALU.add),
                             reads=[rqtmp[hb], rqtmp2[hb]], writes=[rqd])

                    def fin_rest(s_):
                        for r in range(4):
                            S.op("pe", "transpose", dict(out=bk(6)[:, r * 65:(r + 1) * 65], in_=accS[hb][:, r * 128:(r + 1) * 128], identity=ident[0:65, 0:65]),
                                 reads=[raccS[hb], rconst], writes=[rbank[6]])
                        S.op("dve", "reciprocal", dict(out=sm[hb][:, 4:8], in_=bk(6)[:, 64:260:65]), reads=[rbank[6]], writes=[rsm[hb]])
                        S.op("dve", "tensor_tensor", dict(out=otmp[hb][:], in0=bk(6)[:, 0:260].rearrange("p (r d) -> p r d", r=4)[:, :, 0:64],
                                                          in1=sm[hb][:, 4:8].unsqueeze(2).to_broadcast([128, 4, 64]), op=ALU.mult),
                             reads=[rbank[6], rsm[hb]], writes=[rot[hb]])
                        mvb = mixB[:, s_ * 4:(s_ + 1) * 4, h * 64:(h + 1) * 64]
                        rmv = [rmixBt[s_ * 4 + r] for r in range(4)]
                        S.op("dve", "tensor_tensor", dict(out=mvb, in0=mvb, in1=otmp[hb][:, :, :], op=ALU.mult), reads=[rot[hb]] + rmv, writes=rmv)

                    qgen(0)
                    yield from idle(4)
                    flat = [(s_, u) for s_ in range(4) for u in range(8 * s_ + 8)]
                    nflat = len(flat)

                    def emit_score(gi):
                        s_, u = flat[gi]
                        nk = 8 * s_ + 8
                        b = 2 * hb + (gi % 2)
                        c0 = max(0, u - (nk - 4)) * 128
                        qd = qh2[hb][s_ % 2]; rqd = rqh2[hb][s_ % 2]
                        S.op("pe", "matmul", dict(out=bk(b)[:, c0:512], lhsT=KH[hb][:, u * 128:(u + 1) * 128], rhs=qd[:, c0:512],
                                                  start=True, stop=True), reads=[rKH[hb], rKHrot[hb], rqd], writes=[rbank[b]])
                    emit_score(0)
                    emit_score(1)
                    for gi in range(nflat):
                        s_, u = flat[gi]
                        nk = 8 * s_ + 8
                        b = 2 * hb + (gi % 2)
                        p = 3 * hb + (gi % 3)
                        c0 = max(0, u - (nk - 4)) * 128
                        S.op("act", "activation", dict(out=Pt[p][:, c0:512], in_=bk(b)[:, c0:512], func=AF.Exp, scale=SC),
                             reads=[rbank[b]], writes=[rPt[p]])
                        if u >= nk - 4:
                            S.op("dve", "tensor_tensor", dict(out=Pt[p][:, c0:c0 + 128], in0=Pt[p][:, c0:c0 + 128], in1=trile[:], op=ALU.mult),
                                 reads=[rPt[p], rconst], writes=[rPt[p]])
                        if gi + 2 < nflat:
                            emit_score(gi + 2)
                        if u == 0 and s_ > 0:
                            S.op("dve", "tensor_copy", dict(out=accS[hb][:, :], in_=bk(accb)[0:65, :]), reads=[rbank[accb]], writes=[raccS[hb]])
                        S.op("pe", "matmul", dict(out=bk(accb)[0:65, c0:512], lhsT=VH[hb][:, u, :], rhs=Pt[p][:, c0:512],
                                                  start=(u == 0), stop=(u == nk - 1)), reads=[rPt[p], rVH[hb]], writes=[rbank[accb]])
                        yield
                        if u == 2 and s_ > 0:
                            fin_rest(s_ - 1)
                            yield
                        if u == 3 and s_ + 1 < 4:
                            qgen(s_ + 1)
                            yield
                    S.op("dve", "tensor_copy", dict(out=accS[hb][:, :], in_=bk(accb)[0:65, :]), reads=[rbank[accb]], writes=[raccS[hb]])
                    yield from idle(3)
                    fin_rest(3)
                    yield

                run_staggered([mla_stream(h) for h in range(8)], 44, max_active=2, gate_key=lambda k: k)
                dbg("mixB", mixB[:, :, :], [128, NOWN, 512], BF16, reads=rmixBt)
                S.flush()
                checkpoint()

        with contextlib.ExitStack() as st:
            xt = [sb(st, f"xo{i}", [128, D]) for i in range(2)]; rxt = [Res("xo0"), Res("xo1")]
            mT = [sb(st, f"mT{i}", [128, 8, 128], BF16) for i in range(2)]; rmT = [Res("mT0"), Res("mT1")]
            yo = [sb(st, f"yo{i}", [128, D]) for i in range(2)]; ryo = [Res("yo0"), Res("yo1")]
            junk = sb(st, "junk5", [128, D], BF16); st5 = [sb(st, f"st5{i}", [128, 4]) for i in range(2)]; rst5 = [Res("st50"), Res("st51")]

            def p5_front(i):
                b = i % 2
                t = 8 * (i // 4) + 4 + (i % 4)
                tb = 7 - b
                S.dma("sp", xt[b][:], xs[t * 128:(t + 1) * 128, :], writes=[rxt[b]])
                for c in range(8):
                    src = mixA[:, i, c * 128:(c + 1) * 128] if c < 4 else mixB[:, i, (c - 4) * 128:(c - 3) * 128]
                    S.op("pe", "transpose", dict(out=bk16(tb)[:, c * 128:(c + 1) * 128], in_=src, identity=identb[:]),
                         reads=[rmixA[i], rmixBt[i], rconst], writes=[rbank[tb]])
                S.op("act", "activation", dict(out=mT[b][:, :, :], in_=bk16(tb)[:, :].rearrange("p (c q) -> p c q", c=8), func=AF.Copy),
                     reads=[rbank[tb]], writes=[rmT[b]])

            p5_front(0)
            for i in range(NOWN):
                b = i % 2
                if i + 1 < NOWN:
                    p5_front(i + 1)
                for hh in range(2):
                    ob = 2 * b + hh
                    for c in range(8):
                        S.op("pe", "matmul", dict(out=bk(ob)[:, :], lhsT=mT[b][:, c, :], rhs=wo[:, c, hh * 512:(hh + 1) * 512],
                                                  start=(c == 0), stop=(c == 7)), reads=[rmT[b], rwo], writes=[rbank[ob]])
                    hsl = slice(hh * 512, (hh + 1) * 512)
                    S.op("dve", "tensor_tensor", dict(out=yo[b][:, hsl], in0=bk(ob)[:, :], in1=gate_bc[:, hsl], op=ALU.mult),
                         reads=[rbank[ob], rmod], writes=[ryo[b]])
                S.op("dve", "tensor_tensor", dict(out=yo[b][:, :], in0=yo[b][:, :], in1=xt[b][:, :], op=ALU.add), reads=[ryo[b], rxt[b]], writes=[ryo[b]])
                S.op("act", "activation", dict(out=junk[:], in_=yo[b][:], func=AF.Square, accum_out=st5[b][:, 0:1]), reads=[ryo[b]], writes=[rst5[b]])
                S.op("dve", "tensor_scalar", dict(out=st5[b][:, 1:2], in0=st5[b][:, 0:1], scalar1=1.0 / D, scalar2=1e-6, op0=ALU.mult, op1=ALU.add), reads=[rst5[b]], writes=[rst5[b]])
                S.op("act", "activation", dict(out=st5[b][:, 2:3], in_=st5[b][:, 1:2], func=AF.Sqrt), reads=[rst5[b]], writes=[rst5[b]])
                S.op("dve", "reciprocal", dict(out=st5[b][:, 2:3], in_=st5[b][:, 2:3]), reads=[rst5[b]], writes=[rst5[b]])
                S.op("dve", "scalar_tensor_tensor", dict(out=yo[b][:, :], in0=yo[b][:, :], scalar=st5[b][:, 2:3], in1=fing_bc[:, :], op0=ALU.mult, op1=ALU.mult),
                     reads=[ryo[b], rst5[b], rconst], writes=[ryo[b]])
                S.dma("sp", out_d[i * 128:(i + 1) * 128, :], yo[b][:, :], reads=[ryo[b]])
            S.wait_all_dma("sp")
            S.flush()
            checkpoint()
    return nc, dbg_out


def _constants(par):
    delta = 1 - par
    c = {}
    c["inv32"] = (10000.0 ** (-np.arange(32, dtype=np.float32) / 32)).astype(np.float32)
    c["inv16"] = (10000.0 ** (-np.arange(16, dtype=np.float32) / 16)).astype(np.float32)
    invF = np.zeros((128, 1), np.float32); sgnF = np.zeros((128, 1), np.float32)
    for r in range(64, 96):
        invF[r, 0] = c["inv16"][(r - 64) % 16]
        sgnF[r, 0] = -1.0 if r < 80 else 1.0
    c["invF"] = invF; c["sgnF"] = sgnF
    k = np.arange(S_LEN)
    c["E_aug"] = (k[None, :] // 64 == np.arange(64)[:, None]).astype(np.float32)
    c["ident"] = np.eye(128, dtype=np.float32)
    kk = np.arange(128)[:, None]; qq = np.arange(128)[None, :]
    c["tri_le"] = (kk <= qq).astype(np.float32)
    c["tri_gt"] = (kk > qq).astype(np.float32)
    own_t = np.concatenate([np.arange((2 * s + 1) * 512, (2 * s + 2) * 512) for s in range(4)])
    n = np.arange(256)[:, None]
    c["cmpmask"] = ((16 * n + 31 <= own_t[None, :]) & (n < 255)).astype(np.float32)
    j = np.arange(64)[None, :]
    cur = (own_t // 64)[:, None]
    jr = j - 8 * delta; curr = cur - 8 * delta
    forced = (jr == 0) | (jr == curr) | (jr == curr - 1)
    keep = np.ones((NOWN * 128, 64), np.float32); force = np.zeros((NOWN * 128, 64), np.float32)
    fut = jr > curr
    dummy = jr < 0
    force[forced & ~dummy & ~fut] = 1.0e4
    c["impkeep"] = keep; c["impforce"] = force
    start = np.arange(256)[:, None] * 16; bstart = np.arange(64)[None, :] * 64
    ov = np.minimum(start + 32, bstart + 64) - np.maximum(start, bstart)
    m1 = (np.clip(ov, 0, None) / 32).astype(np.float32); m1[255] = 0.0
    c["m1"] = m1
    return c


_CACHE = {}


def kernel(x, c, positions, ada_w, ada_b, norm_g, w_in, cmp_pos, cmp_k_w1, cmp_k_w2, cmp_v_w1, cmp_v_w2,
           q_norm_g, w_q_up, kv_norm_g, w_kv_up, w_out, final_norm_g, _dbg=(), _stop=99):
    x = np.asarray(x, np.float32); c = np.asarray(c, np.float32); positions = np.asarray(positions, np.int32)
    key = (tuple(_dbg), _stop)
    if key not in _CACHE:
        _CACHE[key] = build(_dbg, _stop)
    nc, dbg_out = _CACHE[key]
    shared = {
        "ada_w": np.asarray(ada_w, np.float32)[0], "ada_b": np.asarray(ada_b, np.float32)[0], "norm_g": np.asarray(norm_g, np.float32)[0],
        "w_in": np.asarray(w_in, np.float32)[0], "cmp_pos": np.asarray(cmp_pos, np.float32)[0],
        "cmp_k_w1": np.asarray(cmp_k_w1, np.float32)[0], "cmp_k_w2": np.asarray(cmp_k_w2, np.float32)[0],
        "cmp_v_w1": np.asarray(cmp_v_w1, np.float32)[0], "cmp_v_w2": np.asarray(cmp_v_w2, np.float32)[0],
        "q_norm_g": np.asarray(q_norm_g, np.float32)[0], "w_q_up": np.asarray(w_q_up, np.float32)[0],
        "kv_norm_g": np.asarray(kv_norm_g, np.float32)[0], "w_kv_up": np.asarray(w_kv_up, np.float32)[0],
        "w_out": np.asarray(w_out, np.float32)[0], "final_norm_g": np.asarray(final_norm_g, np.float32),
    }
    shared["g_col"] = np.ascontiguousarray(shared["norm_g"].reshape(8, 128).T)
    shared["gq_col"] = np.ascontiguousarray(shared["q_norm_g"].reshape(2, 128).T)
    shared["gkv_col"] = np.ascontiguousarray(shared["kv_norm_g"].reshape(1, 128).T)
    shared["cmp_posT"] = np.ascontiguousarray(shared["cmp_pos"].T)
    consts = [_constants(0), _constants(1)]
    in_maps = []
    for core in range(8):
        b, par = divmod(core, 2)
        if par == 1:
            xs = x[b]; ps = positions[b]; valid = np.ones(S_LEN, np.float32)
        else:
            xs = np.concatenate([np.zeros((512, D), np.float32), x[b, :S_LEN - 512]], axis=0)
            ps = np.concatenate([np.zeros(512, np.int32), positions[b, :S_LEN - 512]])
            valid = np.concatenate([np.zeros(512, np.float32), np.ones(S_LEN - 512, np.float32)])
        m = {"xs": np.ascontiguousarray(xs), "pos_i": np.ascontiguousarray(ps), "valid": valid, "c_b": np.ascontiguousarray(c[b])}
        m["valid_pt"] = np.ascontiguousarray(valid.reshape(NT, 128).T)
        m["pos_pt"] = np.ascontiguousarray(ps.reshape(NT, 128).T)
        m["c_col"] = np.ascontiguousarray(c[b].reshape(8, 128).T)
        nidx = np.minimum(np.arange(256), 254)
        vc_ = valid[16 * nidx].copy(); vc_[255] = 0.0
        cp_ = ps[16 * nidx + 31].copy(); cp_[255] = 0
        m["validc_pt"] = np.ascontiguousarray(vc_.reshape(2, 128).T.astype(np.float32))
        m["cpos_pt"] = np.ascontiguousarray(cp_.reshape(2, 128).T.astype(np.int32))
        m.update(shared)
        m.update(consts[par])
        in_maps.append(m)
    res = run_bass_kernel_spmd(nc, in_maps, core_ids=list(range(8)))
    out = np.zeros((4, S_LEN, D), np.float32)
    for core in range(8):
        b, par = divmod(core, 2)
        o = np.asarray(res.results[core]["out"], np.float32)
        for s in range(4):
            qb = 2 * s + par
            out[b, qb * 512:(qb + 1) * 512, :] = o[s * 512:(s + 1) * 512, :]
    if _dbg:
        kernel.last_dbg = [{k: np.asarray(res.results[core]["dbg_" + k]) for k in dbg_out} for core in range(8)]
    return out
```

```python
import contextlib
import numpy as np
import ml_dtypes
import concourse.bass as bass
import concourse.mybir as mybir
from concourse.bass_utils import run_bass_kernel_spmd

F32 = mybir.dt.float32
BF16 = mybir.dt.bfloat16
I32 = mybir.dt.int32
ALU = mybir.AluOpType
AF = mybir.ActivationFunctionType

S_LEN = 4096
D = 1024
NT = 32
NOWN = 16
IN_W = 2744
PI = float(np.pi)

ENGS = ("pe", "act", "dve", "pool", "sp")
SAME_ENG_SYNC = {"pe": False, "act": True, "dve": True, "pool": True, "sp": False}
N_DMA_SLOTS = 10


class Res:
    __slots__ = ("name", "w", "r", "excl", "multi", "ws")

    def __init__(self, name, excl=False, multi=False):
        self.name = name
        self.w = None
        self.r = {}
        self.excl = excl
        self.multi = multi
        self.ws = {}


class Sched:
    def __init__(self, nc, stack):
        self.nc = nc
        self.sems = {}
        for e in ENGS:
            self.sems[e] = stack.enter_context(nc.semaphore("s_" + e))
        self.dq = ("sp", "pool")
        for q in self.dq:
            for i in range(N_DMA_SLOTS):
                self.sems[("d", q, i)] = stack.enter_context(nc.semaphore(f"d_{q}{i}"))
        self.cnt = {k: 0 for k in self.sems}
        self.ops = {e: [] for e in ENGS}
        self.seen = {e: {} for e in ENGS}
        self.dslot = {q: 0 for q in self.dq}
        self.dead = False

    def _wait(self, eng, key, val):
        if self.dead:
            return
        if self.seen[eng].get(key, 0) >= val:
            return
        self.seen[eng][key] = val
        sem = self.sems[key]
        self.ops[eng].append(lambda e, sem=sem, val=val: e.wait_ge(sem, val))

    def _deps(self, eng, reads, writes, extra=()):
        deps = {}

        def add(tok):
            if tok is None:
                return
            k, v = tok
            if deps.get(k, 0) < v:
                deps[k] = v
        for r in reads:
            add(r.w)
            if r.multi:
                for k, v in r.ws.items():
                    add((k, v))
            if r.excl:
                for k, v in r.r.items():
                    if k != eng:
                        add((k, v))
        for w in writes:
            if w.multi:
                continue
            add(w.w)
            for k, v in w.r.items():
                add((k, v))
        for t in extra:
            add(t)
        for k, v in deps.items():
            if k == eng and not SAME_ENG_SYNC[eng]:
                continue
            self._wait(eng, k, v)

    def _mark(self, tok, reads, writes):
        k, v = tok
        for r in reads:
            if r.r.get(k, 0) < v:
                r.r[k] = v
        for w in writes:
            if w.multi:
                if w.ws.get(k, 0) < v:
                    w.ws[k] = v
                continue
            w.w = tok
            w.r = {}

    def op(self, eng, meth, kw, reads=(), writes=()):
        if self.dead:
            return None
        self._deps(eng, reads, writes)
        sem = self.sems[eng]
        self.cnt[eng] += 1
        tok = (eng, self.cnt[eng])
        self.ops[eng].append(lambda e, meth=meth, kw=kw, sem=sem: getattr(e, meth)(**kw).then_inc(sem, 1))
        self._mark(tok, reads, writes)
        return tok

    def dma(self, q, out, in_, reads=(), writes=(), **kw):
        if self.dead:
            return None
        slot = self.dslot[q]
        self.dslot[q] = (slot + 1) % N_DMA_SLOTS
        key = ("d", q, slot)
        prev = (key, self.cnt[key]) if self.cnt[key] > 0 else None
        self._deps(q, reads, writes, extra=(prev,))
        self.cnt[key] += 16
        tok = (key, self.cnt[key])
        sem = self.sems[key]
        self.ops[q].append(lambda e, out=out, in_=in_, sem=sem, kw=kw:
                           e.dma_start(out=out, in_=in_, **kw).then_inc(sem, 16))
        self._mark(tok, reads, writes)
        return tok

    def wait_all_dma(self, eng):
        for q in self.dq:
            for i in range(N_DMA_SLOTS):
                k = ("d", q, i)
                if self.cnt[k]:
                    self._wait(eng, k, self.cnt[k])

    def flush(self):
        if not any(self.ops[e] for e in ENGS):
            return
        nc = self.nc
        ops = self.ops
        with nc.Block() as block:
            @block.tensor
            def _(e):
                for f in ops["pe"]:
                    f(e)

            @block.scalar
            def _(e):
                for f in ops["act"]:
                    f(e)

            @block.vector
            def _(e):
                for f in ops["dve"]:
                    f(e)

            @block.gpsimd
            def _(e):
                for f in ops["pool"]:
                    f(e)

            @block.sync
            def _(e):
                for f in ops["sp"]:
                    f(e)
        self.ops = {e: [] for e in ENGS}


O_Q, O_KC, O_VC, O_KS, O_VS, O_KW, O_VW, O_GL, O_ZN, O_CQ, O_CKV, O_KR, O_ZM = (
    0, 512, 640, 768, 896, 1024, 1152, 1280, 1304, 1816, 2072, 2200, 2232)
NSA_COLS = 1816
MLA_COLS = IN_W - NSA_COLS


class _Stop(Exception):
    pass


def build(dbg_names=(), stop=99):
    nc = bass.Bass("TRN2", target_bir_lowering=False)
    dram = {}

    def din(name, shape, dt=F32):
        dram[name] = nc.dram_tensor(name, list(shape), dt, kind="ExternalInput").ap()
        return dram[name]

    xs = din("xs", [S_LEN, D])
    pos_i = din("pos_i", [S_LEN], I32)
    valid_d = din("valid", [S_LEN])
    c_b = din("c_b", [D])
    ada_w = din("ada_w", [D, 3 * D]); ada_b = din("ada_b", [3 * D]); norm_g = din("norm_g", [D])
    w_in = din("w_in", [D, IN_W]); cmp_pos = din("cmp_pos", [32, 64])
    ckw1 = din("cmp_k_w1", [2048, 128]); ckw2 = din("cmp_k_w2", [128, 64])
    cvw1 = din("cmp_v_w1", [2048, 128]); cvw2 = din("cmp_v_w2", [128, 64])
    q_norm_g = din("q_norm_g", [256]); w_q_up = din("w_q_up", [256, 768])
    kv_norm_g = din("kv_norm_g", [128]); w_kv_up = din("w_kv_up", [128, 1024])
    w_out = din("w_out", [D, D]); fin_g = din("final_norm_g", [D])
    inv32_d = din("inv32", [32]); inv16_d = din("inv16", [16])
    invF_d = din("invF", [128, 1]); sgnF_d = din("sgnF", [128, 1])
    E_d = din("E_aug", [64, S_LEN]); ident_d = din("ident", [128, 128])
    trile_d = din("tri_le", [128, 128]); trigt_d = din("tri_gt", [128, 128])
    cmpmask_d = din("cmpmask", [256, NOWN * 128])
    impkeep_d = din("impkeep", [NOWN * 128, 64]); impforce_d = din("impforce", [NOWN * 128, 64])
    m1_d = din("m1", [256, 64])
    valid_pt = din("valid_pt", [128, NT]); pos_pt = din("pos_pt", [128, NT], I32)
    c_col = din("c_col", [128, 8]); g_col = din("g_col", [128, 8]); gq_col = din("gq_col", [128, 2]); gkv_col = din("gkv_col", [128, 1])
    posT_d = din("cmp_posT", [64, 32]); validc_pt = din("validc_pt", [128, 2]); cpos_pt = din("cpos_pt", [128, 2], I32)
    out_d = nc.dram_tensor("out", [NOWN * 128, D], F32, kind="ExternalOutput").ap()
    dbg_out = {}

    with contextlib.ExitStack() as gst:
        S = Sched(nc, gst)
        ckpt_n = [0]

        def checkpoint():
            ckpt_n[0] += 1
            if ckpt_n[0] == stop:
                S.wait_all_dma("sp")
                S.flush()
                S.dead = True

        uniq = [0]

        def sb(st, name, shape, dt=F32):
            uniq[0] += 1
            return st.enter_context(nc.sbuf_tensor(f"t{uniq[0]}_{name}", list(shape), dt))

        def dbg(name, ap, shape, dt=F32, reads=()):
            if name in dbg_names:
                d = nc.dram_tensor("dbg_" + name, list(shape), dt, kind="ExternalOutput").ap()
                dbg_out[name] = d
                S.dma("sp", d, ap, reads=reads)

        banks = [gst.enter_context(nc.psum_tensor(f"bank{i}", [128, 512], F32)) for i in range(8)]
        rbank = [Res(f"bank{i}", excl=True) for i in range(8)]

        def bk(i):
            return banks[i]

        def bk16(i):
            return banks[i][:].bitcast(BF16)

        ident = sb(gst, "ident", [128, 128]); identb = sb(gst, "identb", [128, 128], BF16)
        trile = sb(gst, "trile", [128, 128], BF16); trigt = sb(gst, "trigt", [128, 128], BF16)
        ones_f = sb(gst, "ones_f", [128, 128])
        scl1 = sb(gst, "scl1", [128, 8]); shf = sb(gst, "shf", [128, 8])
        gate_bc = sb(gst, "gate_bc", [128, D]); fing_bc = sb(gst, "fing_bc", [128, D])
        valid = sb(gst, "validc", [128, NT]); posf = sb(gst, "posf", [128, NT])
        rstd_all = sb(gst, "rstd_all", [128, NT]); rrstd = Res("rstd")
        mixA = sb(gst, "mixA", [128, NOWN, 512], BF16); mixB = sb(gst, "mixB", [128, NOWN, 512], BF16)
        rconst = Res("const", multi=True); rmod = Res("mod"); rmixA = [Res(f"mixA{i}") for i in range(NOWN)]
        rmixB = Res("mixB"); rmixBt = [Res(f"mixBt{i}") for i in range(NOWN)]

        S.dma("sp", ident[:], ident_d[:, :], writes=[rconst])
        S.dma("pool", identb[:], ident_d[:, :], writes=[rconst])
        S.dma("pool", trile[:], trile_d[:, :], writes=[rconst])
        S.dma("pool", trigt[:], trigt_d[:, :], writes=[rconst])
        S.dma("sp", valid[:], valid_pt[:, :], writes=[rconst])
        S.dma("sp", fing_bc[:], fin_g.partition_broadcast(128), writes=[rconst])
        S.op("dve", "memset", dict(ap=ones_f[:], constant=1.0), writes=[rconst])

        with contextlib.ExitStack() as st:
            ccol = sb(st, "ccol", [128, 8]); scol = sb(st, "scol", [128, 8], BF16)
            gcol = sb(st, "gcol", [128, 8]); posi = sb(st, "posi", [128, NT], I32)
            adab = sb(st, "adab", [1, 3 * D]); modrow = sb(st, "modrow", [1, 3 * D])
            awb = [sb(st, f"awb{i}", [128, 8, 512], BF16) for i in range(2)]
            rawb = [Res("awb0"), Res("awb1")]; rc = Res("ccol", multi=True); rrow = Res("modrow")
            S.dma("sp", ccol[:], c_col[:, :], writes=[rc])
            S.dma("sp", gcol[:], g_col[:, :], writes=[rc])
            S.dma("sp", posi[:], pos_pt[:, :], writes=[rc])
            S.dma("sp", adab[:], ada_b.rearrange("(o n) -> o n", o=1), writes=[rc])
            S.op("act", "activation", dict(out=scol[:], in_=ccol[:], func=AF.Silu), reads=[rc], writes=[rc])
            S.op("dve", "tensor_copy", dict(out=posf[:], in_=posi[:]), reads=[rc], writes=[rconst])
            aw_v = ada_w.rearrange("(c p) n -> p c n", p=128)
            for n in range(6):
                b = n % 2
                S.dma("pool", awb[b][:], aw_v[:, :, n * 512:(n + 1) * 512], writes=[rawb[b]])
                for c in range(8):
                    S.op("pe", "matmul", dict(out=bk(n % 2)[0:1, :], lhsT=scol[:, c:c + 1], rhs=awb[b][:, c, :],
                                                              start=(c == 0), stop=(c == 7)),
                         reads=[rc, rawb[b]], writes=[rbank[n % 2]])
                S.op("dve", "tensor_tensor", dict(out=modrow[:, n * 512:(n + 1) * 512], in0=bk(n % 2)[0:1, :],
                                                        in1=adab[:, n * 512:(n + 1) * 512], op=ALU.add),
                     reads=[rbank[n % 2], rc], writes=[rrow])
            for j in range(16):
                S.op("pe", "matmul", dict(out=bk(2)[:, j:j + 1], lhsT=modrow[:, j * 128:(j + 1) * 128], rhs=ones_f[0:1, 0:1],
                                                start=True, stop=True), reads=[rrow, rconst], writes=[rbank[2]])
            S.op("dve", "tensor_copy", dict(out=shf[:], in_=bk(2)[:, 0:8]), reads=[rbank[2]], writes=[rmod])
            S.op("dve", "scalar_tensor_tensor", dict(out=scl1[:], in0=bk(2)[:, 8:16], scalar=1.0, in1=gcol[:],
                                                         op0=ALU.add, op1=ALU.mult), reads=[rbank[2], rc], writes=[rmod])
            for hh in range(2):
                S.op("pe", "matmul", dict(out=bk(3)[:, :], lhsT=ones_f[0:1, :], rhs=modrow[:, 2048 + hh * 512:2048 + (hh + 1) * 512],
                                                  start=True, stop=True), reads=[rrow, rconst], writes=[rbank[3]])
                S.op("dve", "tensor_copy", dict(out=gate_bc[:, hh * 512:(hh + 1) * 512], in_=bk(3)[:, :]),
                     reads=[rbank[3]], writes=[rmod])
            xpre = [sb(st, f"xpre{i}", [128, D]) for i in range(3)]; rxpre = [Res(f"xpre{i}") for i in range(3)]
            jpre = sb(st, "jpre", [128, D], BF16)
            for t in range(NT):
                b3 = t % 3
                S.dma("sp", xpre[b3][:], xs[t * 128:(t + 1) * 128, :], writes=[rxpre[b3]])
                S.op("act", "activation", dict(out=jpre[:], in_=xpre[b3][:], func=AF.Square, accum_out=rstd_all[:, t:t + 1]),
                     reads=[rxpre[b3]], writes=[rrstd])
            S.op("dve", "tensor_scalar", dict(out=rstd_all[:], in0=rstd_all[:], scalar1=1.0 / D, scalar2=1e-6, op0=ALU.mult, op1=ALU.add),
                 reads=[rrstd], writes=[rrstd])
            S.op("act", "activation", dict(out=rstd_all[:], in_=rstd_all[:], func=AF.Sqrt), reads=[rrstd], writes=[rrstd])
            S.op("dve", "reciprocal", dict(out=rstd_all[:], in_=rstd_all[:]), reads=[rrstd], writes=[rrstd])
            dbg("scl1", scl1[:], [128, 8], reads=[rmod]); dbg("shf", shf[:], [128, 8], reads=[rmod])
            dbg("gate", gate_bc[0:1, :], [1, D], reads=[rmod])
            S.flush()
            checkpoint()

        def rope_tables(st, half, name):
            cosT = sb(st, name + "cos", [128, NT, half]); sinT = sb(st, name + "sin", [128, NT, half])
            with contextlib.ExitStack() as t2:
                invb = sb(t2, name + "inv", [128, half]); ang = sb(t2, name + "ang", [128, NT, half])
                ki = sb(t2, name + "ki", [128, NT, half], I32); kf = sb(t2, name + "kf", [128, NT, half])
                rr = Res(name + "tmp"); rt = Res(name + "tab")
                S.dma("sp", invb[:], (inv32_d if half == 32 else inv16_d).partition_broadcast(128), writes=[rr])
                S.op("dve", "tensor_tensor", dict(out=ang[:], in0=posf[:].unsqueeze(2).to_broadcast([128, NT, half]),
                                                      in1=invb[:].unsqueeze(1).to_broadcast([128, NT, half]), op=ALU.mult),
                     reads=[rr, rconst], writes=[rr])
                for tab, off in ((sinT, 0.0), (cosT, PI / 2)):
                    S.op("dve", "tensor_scalar", dict(out=ki[:], in0=ang[:], scalar1=off, scalar2=1.0 / (2 * PI),
                                                                 op0=ALU.add, op1=ALU.mult), reads=[rr], writes=[rr])
                    S.op("dve", "tensor_copy", dict(out=kf[:], in_=ki[:]), reads=[rr], writes=[rr])
                    S.op("dve", "scalar_tensor_tensor", dict(out=kf[:], in0=kf[:], scalar=-2 * PI, in1=ang[:],
                                                                 op0=ALU.mult, op1=ALU.add), reads=[rr], writes=[rr])
                    S.op("dve", "tensor_scalar", dict(out=kf[:], in0=kf[:], scalar1=off, scalar2=PI,
                                                                 op0=ALU.add, op1=ALU.min), reads=[rr], writes=[rr])
                    S.op("dve", "tensor_scalar", dict(out=kf[:], in0=kf[:], scalar1=-PI, scalar2=None, op0=ALU.max),
                         reads=[rr], writes=[rr])
                    S.op("act", "activation", dict(out=tab[:], in_=kf[:], func=AF.Sin), reads=[rr], writes=[rr, rt])
                S.flush()
                checkpoint()
            return cosT, sinT, rt

        def rope_tm(src, dst, nh, half, cosv, sinv, tmp, reads, writes):
            n = nh * half
            npart = src.shape[0]
            cb = cosv.unsqueeze(1).to_broadcast([npart, nh, half]); sbv = sinv.unsqueeze(1).to_broadcast([npart, nh, half])
            x1 = src[:, :, 0:half]; x2 = src[:, :, half:2 * half]
            t1 = tmp[:, 0:n].rearrange("p (h d) -> p h d", h=nh); t2 = tmp[:, n:2 * n].rearrange("p (h d) -> p h d", h=nh)
            S.op("dve", "tensor_tensor", dict(out=t1, in0=x1, in1=cb, op=ALU.mult), reads=reads, writes=[rtmp_rope])
            S.op("dve", "tensor_tensor", dict(out=t2, in0=x2, in1=sbv, op=ALU.mult), reads=reads, writes=[rtmp_rope])
            S.op("dve", "tensor_tensor", dict(out=dst[:, :, 0:half], in0=t1, in1=t2, op=ALU.subtract),
                 reads=[rtmp_rope], writes=writes)
            S.op("dve", "tensor_tensor", dict(out=t1, in0=x2, in1=cb, op=ALU.mult), reads=reads, writes=[rtmp_rope])
            S.op("dve", "tensor_tensor", dict(out=t2, in0=x1, in1=sbv, op=ALU.mult), reads=reads, writes=[rtmp_rope])
            S.op("dve", "tensor_tensor", dict(out=dst[:, :, half:2 * half], in0=t1, in1=t2, op=ALU.add),
                 reads=[rtmp_rope], writes=writes)

        rtmp_rope = Res("ropetmp")

        def make_hT_a(t, xt, xn, rxt, rxn):
            b = t % 2
            S.dma("sp", xt[b][:], xs[t * 128:(t + 1) * 128, :], writes=[rxt[b]])
            S.op("act", "activation", dict(out=xn[b][:], in_=xt[b][:], func=AF.Copy, scale=rstd_all[:, t:t + 1]),
                 reads=[rxt[b], rrstd], writes=[rxn[b]])

        def make_hT_b(t, xn, hT, rxn, rhT, n_dve=4):
            b = t % 2
            hb = (t // 4) % 2
            for c in range(8):
                tb = 5 + c // 4
                S.op("pe", "transpose", dict(out=bk16(tb)[:, (c % 4) * 128:(c % 4 + 1) * 128], in_=xn[b][:, c * 128:(c + 1) * 128],
                                             identity=identb[:]), reads=[rxn[b], rconst], writes=[rbank[tb]])
            col = (t % 4) * 128
            for c in range(8):
                tb = 5 + c // 4
                src = bk16(tb)[:, (c % 4) * 128:(c % 4 + 1) * 128]
                if c < n_dve:
                    S.op("dve", "tensor_scalar", dict(out=hT[hb][:, c, col:col + 128], in0=src,
                                                      scalar1=scl1[:, c:c + 1], scalar2=shf[:, c:c + 1], op0=ALU.mult, op1=ALU.add),
                         reads=[rbank[tb], rmod], writes=[rhT[hb]])
                else:
                    S.op("act", "activation", dict(out=hT[hb][:, c, col:col + 128], in_=src,
                                                   func=AF.Identity, bias=shf[:, c:c + 1], scale=scl1[:, c:c + 1]),
                         reads=[rbank[tb], rmod], writes=[rhT[hb]])

        def make_hT(t, xt, xn, hT, rxt, rxn, rhT):
            make_hT_a(t, xt, xn, rxt, rxn)
            make_hT_b(t, xn, hT, rxn, rhT)

        junk_s = sb(gst, "junk_s", [128, 8]); rjunk = Res("junk")
        rvalid_dummy = None

        def idle(n):
            for _ in range(n):
                yield

        def run_staggered(gens, lag, max_active=2, gate_key=None):
            pending = list(enumerate(gens))
            active = []
            while pending or active:
                if pending and len(active) < max_active and (not active or active[-1]["steps"] >= lag):
                    k, gen = pending.pop(0)
                    active.append({"k": k, "gen": gen, "steps": 0, "parked": False, "main": False})
                for ent in list(active):
                    if ent["parked"]:
                        key = gate_key(ent["k"])
                        if any(o["main"] and gate_key(o["k"]) == key for o in active if o is not ent):
                            continue
                        ent["parked"] = False
                        ent["main"] = True
                    try:
                        r = next(ent["gen"])
                        ent["steps"] += 1
                        if r == "GATE":
                            ent["parked"] = True
                        elif r == "RELEASE":
                            ent["main"] = False
                    except StopIteration:
                        active.remove(ent)

        def own_index(t):
            blk, r = divmod(t, 4)
            if blk % 2 == 1:
                return (blk // 2) * 4 + r
            return None

        with contextlib.ExitStack() as nsa:
            QT = sb(nsa, "QT", [128, 8, NOWN * 128], BF16)
            KE = sb(nsa, "KE", [128, 2, S_LEN], BF16)
            KW = sb(nsa, "KW", [64, 2, S_LEN], BF16)
            VS = sb(nsa, "VS", [128, NT, 2, 65], BF16); VW = sb(nsa, "VW", [128, NT, 2, 65], BF16)
            GS = sb(nsa, "GS", [128, NOWN, 24])
            mixBf = mixB[:].rearrange("p a b -> p (a b)")
            kcmpT = mixBf[:, 0:S_LEN]; vcmpT = mixBf[:, S_LEN:2 * S_LEN]
            rQT = [Res(f"QT{i}") for i in range(NOWN)]; rQS = [[Res(f"QS{i}_{g}") for g in range(2)] for i in range(NOWN)]
            rKE = Res("KE"); rKW = Res("KW"); rVS = Res("VS"); rVW = Res("VW"); rGS = Res("GS")
            for g in range(2):
                S.dma("pool", KE[64:128, g, :], E_d[:, :], writes=[rKE])
            for V, rV in ((VS, rVS), (VW, rVW)):
                S.op("dve", "tensor_copy", dict(out=V[:, :, :, 64], in_=valid[:].unsqueeze(2).to_broadcast([128, NT, 2])),
                     reads=[rconst], writes=[rV])

            W1k = sb(nsa, "W1k", [128, 32, 128], BF16)
            W2 = [sb(nsa, f"W2{j}", [128, 64], BF16) for j in range(2)]
            posT = sb(nsa, "posT", [128, 32], BF16)
            rW = Res("W1", multi=True)
            with contextlib.ExitStack() as st:
                cos32, sin32, rtab = rope_tables(st, 32, "r32")
                wn = sb(st, "wn", [128, 8, NSA_COLS], BF16); rwn = Res("wn", multi=True)
                w_v = w_in.rearrange("(c p) n -> p c n", p=128)
                W_KC, W_VC, W_KS, W_KW, W_VS, W_VW, W_GL, W_ZN = 512, 640, 768, 896, 1024, 1152, 1280, 1304
                segs = ((0, O_Q, 512), (W_KC, O_KC, 128), (W_VC, O_VC, 128), (W_KS, O_KS, 128), (W_KW, O_KW, 128),
                        (W_VS, O_VS, 128), (W_VW, O_VW, 128), (W_GL, O_GL, 24), (W_ZN, O_ZN, 512))
                for (d0, s0, n_) in segs:
                    S.dma("pool", wn[:, :, d0:d0 + n_], w_v[:, :, s0:s0 + n_], writes=[rwn])
                vk_ = ckw1.rearrange("(l d) j -> d l j", d=64)
                S.dma("pool", W1k[0:64, :, :], vk_, writes=[rW]); S.dma("pool", W1k[64:128, :, :], vk_, writes=[rW])
                S.dma("pool", W2[0][:], ckw2[:, :], writes=[rW]); S.dma("pool", W2[1][:], cvw2[:, :], writes=[rW])
                for hh in range(2):
                    S.dma("pool", posT[hh * 64:(hh + 1) * 64, :], posT_d[:, :], writes=[rW])
                xt = [sb(st, f"xt{i}", [128, D]) for i in range(2)]; rxt = [Res("xt0"), Res("xt1")]
                xn = [sb(st, f"xn{i}", [128, D], BF16) for i in range(2)]; rxn = [Res("xn0"), Res("xn1")]
                hT = [sb(st, f"hT{i}", [128, 8, 512], BF16) for i in range(2)]; rhT = [Res("hT0"), Res("hT1")]
                rtmp = sb(st, "ropetmp", [128, 1024]); ktm = sb(st, "ktm", [128, 256], BF16); rktm = Res("ktm")
                qtm = sb(st, "qtm", [128, 512], BF16); rqtm = Res("qtm")
                for r in range(4):
                    make_hT(r, xt, xn, hT, rxt, rxn, rhT)
                for blk in range(8):
                    hb = blk % 2
                    for j, (off, dstT) in enumerate(((W_KC, kcmpT), (W_VC, vcmpT))):
                        for c in range(8):
                            S.op("pe", "matmul", dict(out=bk(j)[:, :], lhsT=wn[:, c, off:off + 128], rhs=hT[hb][:, c, :],
                                                      start=(c == 0), stop=(c == 7)), reads=[rwn, rhT[hb]], writes=[rbank[j]])
                        S.op("act", "activation", dict(out=dstT[:, blk * 512:(blk + 1) * 512], in_=bk(j)[:, :], func=AF.Copy),
                             reads=[rbank[j]], writes=[rmixB])
                    for r in range(4):
                        t = blk * 4 + r
                        if blk + 1 < 8:
                            make_hT_a((blk + 1) * 4 + r, xt, xn, rxt, rxn)
                        for c in range(8):
                            S.op("pe", "matmul", dict(out=bk(2)[:, :], lhsT=hT[hb][:, c, r * 128:(r + 1) * 128], rhs=wn[:, c, W_KS:W_KS + 512],
                                                      start=(c == 0), stop=(c == 7)), reads=[rwn, rhT[hb]], writes=[rbank[2]])
                        ps = bk(2)
                        i = own_index(t)
                        if i is not None:
                            for c in range(8):
                                S.op("pe", "matmul", dict(out=bk(3)[:, :], lhsT=hT[hb][:, c, r * 128:(r + 1) * 128], rhs=wn[:, c, 0:512],
                                                          start=(c == 0), stop=(c == 7)), reads=[rwn, rhT[hb]], writes=[rbank[3]])
                            for c in range(8):
                                S.op("pe", "matmul", dict(out=bk(4)[:, :], lhsT=hT[hb][:, c, r * 128:(r + 1) * 128], rhs=wn[:, c, W_ZN:W_ZN + 512],
                                                          start=(c == 0), stop=(c == 7)), reads=[rwn, rhT[hb]], writes=[rbank[4]])
                            for c in range(8):
                                S.op("pe", "matmul", dict(out=bk(1)[:, 0:24], lhsT=hT[hb][:, c, r * 128:(r + 1) * 128], rhs=wn[:, c, W_GL:W_GL + 24],
                                                          start=(c == 0), stop=(c == 7)), reads=[rwn, rhT[hb]], writes=[rbank[1]])
                        for (o2, V, rV) in ((256, VS, rVS), (384, VW, rVW)):
                            src = ps[:, o2:o2 + 128].rearrange("p (g d) -> p g d", g=2)
                            if t < 4:
                                S.op("dve", "tensor_scalar", dict(out=V[:, t, :, 0:64], in0=src, scalar1=valid[:, t:t + 1], scalar2=None, op0=ALU.mult),
                                     reads=[rbank[2], rconst], writes=[rV])
                            else:
                                S.op("dve", "tensor_copy", dict(out=V[:, t, :, 0:64], in_=src), reads=[rbank[2]], writes=[rV])
                        rope_tm(ps[:, 0:256].rearrange("p (g d) -> p g d", g=4), ktm[:, :].rearrange("p (g d) -> p g d", g=4),
                                4, 32, cos32[:, t, :], sin32[:, t, :], rtmp, [rbank[2], rtab], [rktm])
                        if i is not None:
                            rope_tm(bk(3)[:, :].rearrange("p (h d) -> p h d", h=8), qtm[:].rearrange("p (h d) -> p h d", h=8),
                                    8, 32, cos32[:, t, :], sin32[:, t, :], rtmp, [rbank[3], rtab], [rqtm])
                            S.op("act", "activation", dict(out=mixA[:, i, :], in_=bk(4)[:, :], func=AF.Silu), reads=[rbank[4]], writes=[rmixA[i]])
                            S.op("act", "activation", dict(out=GS[:, i, :], in_=bk(1)[:, 0:24], func=AF.Sigmoid), reads=[rbank[1]], writes=[rGS])
                        if blk + 1 < 8:
                            make_hT_b((blk + 1) * 4 + r, xn, hT, rxn, rhT, n_dve=0)
                        for idx in range(2):
                            S.op("pe", "transpose", dict(out=bk16(7)[:, idx * 128:(idx + 1) * 128], in_=ktm[:, idx * 128:(idx + 1) * 128], identity=identb[:]),
                                 reads=[rktm, rconst], writes=[rbank[7]])
                        for idx, (KT, rK) in enumerate(((KE, rKE), (KW, rKW))):
                            for g in range(2):
                                S.op("dve", "tensor_copy", dict(out=KT[0:64, g, t * 128:(t + 1) * 128],
                                                                in_=bk16(7)[g * 64:(g + 1) * 64, idx * 128:(idx + 1) * 128]),
                                     reads=[rbank[7]], writes=[rK])
                        if i is None:
                            continue
                        for pr in range(4):
                            S.op("pe", "transpose", dict(out=bk16(3)[:, pr * 128:(pr + 1) * 128], in_=qtm[:, pr * 128:(pr + 1) * 128],
                                                         identity=identb[:]), reads=[rqtm, rconst], writes=[rbank[3]])
                        qsrc = bk16(3)[:, 0:512].rearrange("p (a q) -> p a q", a=4)
                        for hf in range(2):
                            S.op("dve", "tensor_copy", dict(out=QT[0:64, hf:8:2, i * 128:(i + 1) * 128], in_=qsrc[hf * 64:(hf + 1) * 64, :, :]),
                                 reads=[rbank[3]], writes=[rQT[i]])
                dbg("QT", QT[0:64, :, :], [64, 8, NOWN * 128], BF16, reads=rQT)
                dbg("KE", KE[:, :, :], [128, 2, S_LEN], BF16, reads=[rKE])
                dbg("KW", KW[:, :, :], [64, 2, S_LEN], BF16, reads=[rKW])
                dbg("VS", VS[:, :, :, :], [128, NT, 2, 65], BF16, reads=[rVS])
                dbg("kcmpT", kcmpT, [128, S_LEN], BF16, reads=[rmixB])
                dbg("GS", GS[:, :, :], [128, NOWN, 24], reads=[rGS])
                S.flush()
                checkpoint()

            with contextlib.ExitStack() as st:
                W1 = [W1k, sb(st, "W1v", [128, 32, 128], BF16)]
                cst = sb(st, "cst", [128, 2])
                hid = sb(st, "hid", [128, 256], BF16); ctmp = sb(st, "ctmp", [128, 64])
                kcT = sb(st, "kcT", [64, 2, 256], BF16)
                RC = sb(st, "RC", [128, 2, 2, 128], BF16)
                m1 = sb(st, "m1", [128, 2, 64]); vcv = sb(st, "vcv", [128, 2]); cposi = sb(st, "cposi", [128, 2], I32)
                cposf = sb(st, "cposf", [128, 2]); kctm = sb(st, "kctm", [128, 64], BF16)
                rcst = Res("cst"); rhid = Res("hid"); rkcT = Res("kcT"); rRC = Res("RC"); rk2 = Res("kctm")
                vv_ = cvw1.rearrange("(l d) j -> d l j", d=64)
                S.dma("pool", W1[1][0:64, :, :], vv_, writes=[rW]); S.dma("pool", W1[1][64:128, :, :], vv_, writes=[rW])
                S.dma("sp", m1[:, 0, :], m1_d[0:128, :], writes=[rW]); S.dma("sp", m1[:, 1, :], m1_d[128:256, :], writes=[rW])
                v16 = valid_d.rearrange("(n s) -> n s", s=16); p16 = pos_i.rearrange("(n s) -> n s", s=16)
                S.dma("sp", vcv[:, :], validc_pt[:, :], writes=[rW])
                S.dma("sp", cposi[:, :], cpos_pt[:, :], writes=[rW])
                S.op("dve", "tensor_copy", dict(out=cposf[:], in_=cposi[:]), reads=[rW], writes=[rW])
                ccos = sb(st, "ccos", [128, 2, 32]); csin = sb(st, "csin", [128, 2, 32])
                cinv = sb(st, "cinv", [128, 32]); cang = sb(st, "cang", [128, 2, 32]); cki = sb(st, "cki", [128, 2, 32], I32)
                ckf = sb(st, "ckf", [128, 2, 32])
                S.dma("sp", cinv[:], inv32_d.partition_broadcast(128), writes=[rW])
                S.op("dve", "tensor_tensor", dict(out=cang[:], in0=cposf[:].unsqueeze(2).to_broadcast([128, 2, 32]),
                                                      in1=cinv[:].unsqueeze(1).to_broadcast([128, 2, 32]), op=ALU.mult), reads=[rW], writes=[rW])
                for tab, off in ((csin, 0.0), (ccos, PI / 2)):
                    S.op("dve", "tensor_scalar", dict(out=cki[:], in0=cang[:], scalar1=off, scalar2=1.0 / (2 * PI), op0=ALU.add, op1=ALU.mult),
                         reads=[rW], writes=[rW])
                    S.op("dve", "tensor_copy", dict(out=ckf[:], in_=cki[:]), reads=[rW], writes=[rW])
                    S.op("dve", "scalar_tensor_tensor", dict(out=ckf[:], in0=ckf[:], scalar=-2 * PI, in1=cang[:], op0=ALU.mult, op1=ALU.add),
                         reads=[rW], writes=[rW])
                    S.op("dve", "tensor_scalar", dict(out=ckf[:], in0=ckf[:], scalar1=off, scalar2=PI, op0=ALU.add, op1=ALU.min),
                         reads=[rW], writes=[rW])
                    S.op("dve", "tensor_scalar", dict(out=ckf[:], in0=ckf[:], scalar1=-PI, scalar2=None, op0=ALU.max), reads=[rW], writes=[rW])
                    S.op("act", "activation", dict(out=tab[:], in_=ckf[:], func=AF.Sin), reads=[rW], writes=[rW])
                for j in range(2):
                    for l in range(32):
                        S.op("pe", "matmul", dict(out=bk(0)[:, j:j + 1], lhsT=W1[j][0:64, l, :], rhs=posT[0:64, l:l + 1],
                                                             start=(l == 0), stop=(l == 31)), reads=[rW], writes=[rbank[0]])
                S.op("dve", "tensor_copy", dict(out=cst[:], in_=bk(0)[:, 0:2]), reads=[rbank[0]], writes=[rcst])
                S.op("dve", "memset", dict(ap=RC[:], constant=0.0), writes=[rRC])
                S.op("dve", "memset", dict(ap=kcT[:], constant=0.0), writes=[rkcT])
                cmpD = sb(st, "cmpD", [128, 2, 16, 256], BF16); rcmpD = Res("cmpD")
                for j, srcT in enumerate((kcmpT, vcmpT)):
                    S.op("dve", "tensor_copy", dict(out=cmpD[:, j, :, :], in_=srcT.rearrange("p (n s) -> p s n", s=16)), reads=[rmixB], writes=[rcmpD])
                HB = {(0, 0): 1, (0, 1): 3, (1, 0): 4, (1, 1): 5}
                hid2 = [hid, sb(st, "hid_b", [128, 256], BF16)]; rhid2 = [rhid, Res("hid_b")]
                for g in range(2):
                    rows = slice(g * 64, (g + 1) * 64)
                    for j in range(2):
                        hbk = HB[(g, j)]
                        for l in range(32):
                            S.op("pe", "matmul", dict(out=bk(hbk)[:, 0:255], lhsT=W1[j][rows, l, :],
                                                      rhs=cmpD[rows, j, l % 16, l // 16:l // 16 + 255],
                                                      start=(l == 0), stop=(l == 31)),
                                 reads=[rW, rcmpD], writes=[rbank[hbk]])
                for g in range(2):
                    for j in range(2):
                        hbk = HB[(g, j)]
                        hx = (2 * g + j) % 2
                        S.op("act", "activation", dict(out=hid2[hx][:, 0:255], in_=bk(hbk)[:, 0:255], func=AF.Silu, bias=cst[:, j:j + 1]),
                             reads=[rbank[hbk], rcst], writes=[rhid2[hx]])
                        for ch in range(2):
                            nn = 128 if ch == 0 else 127
                            ob = 2 if ch == 0 else 6
                            S.op("pe", "matmul", dict(out=bk(ob)[0:nn, 0:64], lhsT=hid2[hx][:, ch * 128:ch * 128 + nn], rhs=W2[j][:, :],
                                                      start=True, stop=True), reads=[rhid2[hx], rW], writes=[rbank[ob]])
                            if j == 0:
                                rope_tm(bk(ob)[0:nn, 0:64].rearrange("p (h d) -> p h d", h=1), kctm[0:nn, :].rearrange("p (h d) -> p h d", h=1),
                                        1, 32, ccos[0:nn, ch, :], csin[0:nn, ch, :], ctmp[0:nn, :], [rbank[ob], rW], [rk2])
                                S.op("pe", "transpose", dict(out=bk16(7)[0:64, 0:nn], in_=kctm[0:nn, :], identity=identb[0:nn, 0:nn]),
                                     reads=[rk2, rconst], writes=[rbank[7]])
                                S.op("dve", "tensor_copy", dict(out=kcT[:, g, ch * 128:ch * 128 + nn], in_=bk16(7)[0:64, 0:nn]),
                                     reads=[rbank[7]], writes=[rkcT])
                            else:
                                S.op("dve", "tensor_scalar", dict(out=RC[0:nn, ch, g, 0:64], in0=bk(ob)[0:nn, 0:64],
                                                                  scalar1=vcv[0:nn, ch:ch + 1], scalar2=None, op0=ALU.mult),
                                     reads=[rbank[ob], rW], writes=[rRC])
                for g in range(2):
                    for ch in range(2):
                        S.op("dve", "tensor_copy", dict(out=RC[:, ch, g, 64:65], in_=vcv[:, ch:ch + 1]), reads=[rW], writes=[rRC])
                        S.op("dve", "tensor_scalar", dict(out=RC[:, ch, g, 65:128], in0=m1[:, ch, 0:63], scalar1=vcv[:, ch:ch + 1],
                                                                       scalar2=None, op0=ALU.mult), reads=[rW], writes=[rRC])
                dbg("kcT", kcT[:, :, :], [64, 2, 256], BF16, reads=[rkcT])
                dbg("RC", RC[:, :, :, :], [128, 2, 2, 128], BF16, reads=[rRC])
                S.flush()
                checkpoint()

                cmask = sb(st, "cmask", [128, 2, NOWN * 128], BF16)
                iforce = sb(st, "iforce", [128, NOWN, 64])
                rcm = Res("cmask", multi=True)
                for ch in range(2):
                    S.dma("pool", cmask[:, ch, :], cmpmask_d[ch * 128:(ch + 1) * 128, :], writes=[rcm])
                S.dma("sp", iforce[:], impforce_d.rearrange("(i p) j -> p i j", p=128), writes=[rcm])
                Pt = [sb(st, f"Pt{i}", [128, 512], BF16) for i in range(6)]; rPt = [Res(f"Pt{i}") for i in range(6)]
                accS = [sb(st, f"accS{i}", [65, 512]) for i in range(2)]; raccS = [Res("accS0"), Res("accS1")]
                NPS = 4
                cmpS = [sb(st, f"cmpS{i}", [128, 4, 128]) for i in range(NPS)]; rcmpS = [Res(f"cmpS{i}") for i in range(NPS)]
                sm = [sb(st, f"sm{i}", [128, 16]) for i in range(NPS)]; rsm = [Res(f"sm{i}") for i in range(NPS)]
                imp = [sb(st, f"imp{i}", [128, 64]) for i in range(NPS)]; imp2 = [sb(st, f"imp2{i}", [128, 64]) for i in range(NPS)]
                imp3 = [sb(st, f"imp3{i}", [128, 64]) for i in range(NPS)]
                m16 = [sb(st, f"m16{i}", [128, 16]) for i in range(NPS)]
                selb = [sb(st, f"selb{i}", [128, 64], BF16) for i in range(NPS)]; rimp = [Res(f"imp{i}") for i in range(NPS)]
                ocmp = [sb(st, f"ocmp{i}", [128, 4, 64]) for i in range(NPS)]; rocmp = [Res(f"ocmp{i}") for i in range(NPS)]
                oacc = [sb(st, f"oacc{i}", [128, 4, 64]) for i in range(NPS)]; roacc = [Res(f"oacc{i}") for i in range(NPS)]
                wts = [sb(st, f"wts{i}", [128, 3, 4]) for i in range(NPS)]
                Pp = [sb(st, f"Pp{i}", [128, 512], BF16) for i in range(2)]; rPp = [Res("Pp0"), Res("Pp1")]
                for x_ in range(NPS):
                    S.op("dve", "memset", dict(ap=imp[x_][:], constant=0.0), writes=[rimp[x_]])

                def attn_units(units, sx, accb, group_starts=(0,), before_pv=None):
                    n = len(units)
                    before_pv = before_pv or {}

                    def emit_score(u):
                        b = 2 * sx + (u % 2)
                        S.op("pe", "matmul", dict(out=bk(b)[:, :], lhsT=units[u][0], rhs=units[u][1], start=True, stop=True),
                             reads=units[u][2], writes=[rbank[b]])
                    emit_score(0)
                    if n > 1:
                        emit_score(1)
                    for u in range(n):
                        _, _, _, maskt, scale, vl, vrd = units[u]
                        b = 2 * sx + (u % 2)
                        p = 3 * sx + (u % 3)
                        S.op("act", "activation", dict(out=Pt[p][:, :], in_=bk(b)[:, :], func=AF.Exp, scale=scale),
                             reads=[rbank[b]], writes=[rPt[p]])
                        if maskt is not None:
                            pv = Pt[p][:, :].rearrange("p (h q) -> p h q", h=4)
                            S.op("dve", "tensor_tensor", dict(out=pv, in0=pv, in1=maskt[:].unsqueeze(1).to_broadcast([128, 4, 128]), op=ALU.mult),
                                 reads=[rPt[p], rconst], writes=[rPt[p]])
                        if u + 2 < n:
                            emit_score(u + 2)
                        if u in before_pv:
                            before_pv[u]()
                        S.op("pe", "matmul", dict(out=bk(accb)[0:65, :], lhsT=vl, rhs=Pt[p][:, :],
                                                  start=(u in group_starts), stop=(u + 1 in group_starts or u == n - 1)),
                             reads=[rPt[p]] + vrd, writes=[rbank[accb]])
                        yield

                def finalize_T(accb, si):
                    S.op("dve", "tensor_copy", dict(out=accS[si][:, :], in_=bk(accb)[0:65, :]), reads=[rbank[accb]], writes=[raccS[si]])
                    for h in range(4):
                        S.op("pe", "transpose", dict(out=bk(6)[:, h * 65:(h + 1) * 65], in_=accS[si][:, h * 128:(h + 1) * 128],
                                                     identity=ident[0:65, 0:65]), reads=[raccS[si], rconst], writes=[rbank[6]])

                b7lock = {"owner": None}

                def nsa_stream(k, i, g):
                    sx = k % 2
                    ps = k % NPS
                    qt = 8 * (i // 4) + 4 + (i % 4)
                    qs = slice(i * 128, (i + 1) * 128)
                    hs = slice(4 * g, 4 * g + 4)
                    accb = 4 + sx
                    SM, IMP, IMP2, IMP3, M16, SELB = sm[ps], imp[ps], imp2[ps], imp3[ps], m16[ps], selb[ps]
                    OC, OA, WT, CS = ocmp[ps], oacc[ps], wts[ps], cmpS[ps]
                    rSM, rIMP, rOC, rOA, rCS = rsm[ps], rimp[ps], rocmp[ps], roacc[ps], rcmpS[ps]
                    nch = 2 if 8 * qt + 6 >= 128 else 1
                    while b7lock["owner"] not in (None, k):
                        yield
                    b7lock["owner"] = k
                    for ch in range(nch):
                        S.op("pe", "matmul", dict(out=bk(7)[:, :], lhsT=kcT[:, g, ch * 128:(ch + 1) * 128], rhs=QT[0:64, hs, qs],
                                                  start=True, stop=True), reads=[rkcT, rQT[i]], writes=[rbank[7]])
                        yield from idle(3)
                        S.op("act", "activation", dict(out=Pp[ch][:, :], in_=bk(7)[:, :], func=AF.Exp, scale=0.125),
                             reads=[rbank[7]], writes=[rPp[ch]])
                        yield from idle(3)
                        pv = Pp[ch][:, :].rearrange("p (h q) -> p h q", h=4)
                        S.op("dve", "tensor_tensor", dict(out=pv, in0=pv, in1=cmask[:, ch, qs].unsqueeze(1).to_broadcast([128, 4, 128]), op=ALU.mult),
                             reads=[rPp[ch], rcm], writes=[rPp[ch]])
                        yield
                    yield from idle(2)
                    for h in range(4):
                        for ch in range(nch):
                            S.op("pe", "matmul", dict(out=bk(7)[:, h * 128:(h + 1) * 128], lhsT=Pp[ch][:, h * 128:(h + 1) * 128], rhs=RC[:, ch, g, :],
                                                      start=(ch == 0), stop=(ch == nch - 1)), reads=[rPp[ch], rRC], writes=[rbank[7]])
                    yield from idle(3)
                    S.op("dve", "tensor_copy", dict(out=CS[:, :, :], in_=bk(7)[:, :].rearrange("p (h c) -> p h c", h=4)), reads=[rbank[7]], writes=[rCS])
                    b7lock["owner"] = None
                    yield
                    S.op("dve", "tensor_scalar", dict(out=SM[:, 0:4], in0=CS[:, :, 64], scalar1=1e-30, scalar2=None, op0=ALU.max), reads=[rCS], writes=[rSM])
                    S.op("dve", "reciprocal", dict(out=SM[:, 4:8], in_=SM[:, 0:4]), reads=[rSM], writes=[rSM])
                    glv0 = GS[:, i, 12 * g:12 * g + 12].rearrange("p (h b) -> p b h", b=3)[:, 0, :]
                    S.op("dve", "tensor_tensor", dict(out=WT[:, 0, :], in0=glv0, in1=SM[:, 4:8], op=ALU.mult), reads=[rGS, rSM], writes=[rSM])
                    S.op("dve", "tensor_tensor", dict(out=OA[:], in0=CS[:, :, 0:64], in1=WT[:, 0, :].unsqueeze(2).to_broadcast([128, 4, 64]), op=ALU.mult),
                         reads=[rCS, rSM], writes=[rOA])
                    yield
                    for h in range(4):
                        if h == 0:
                            S.op("dve", "tensor_scalar", dict(out=IMP[:, 0:63], in0=CS[:, h, 65:128], scalar1=SM[:, 4:5], scalar2=None, op0=ALU.mult),
                                 reads=[rCS, rSM], writes=[rIMP])
                        else:
                            S.op("dve", "scalar_tensor_tensor", dict(out=IMP[:, 0:63], in0=CS[:, h, 65:128], scalar=SM[:, 4 + h:5 + h], in1=IMP[:, 0:63],
                                                                     op0=ALU.mult, op1=ALU.add), reads=[rCS, rSM, rIMP], writes=[rIMP])
                    yield
                    S.op("dve", "tensor_tensor", dict(out=IMP2[:], in0=IMP[:], in1=iforce[:, i, :], op=ALU.max), reads=[rIMP, rcm], writes=[rIMP])
                    S.op("dve", "max", dict(out=M16[:, 0:8], in_=IMP2[:]), reads=[rIMP], writes=[rIMP])
                    S.op("dve", "match_replace", dict(out=IMP3[:], in_to_replace=M16[:, 0:8], in_values=IMP2[:], imm_value=-1e30),
                         reads=[rIMP], writes=[rIMP])
                    yield
                    S.op("dve", "max", dict(out=M16[:, 8:16], in_=IMP3[:]), reads=[rIMP], writes=[rIMP])
                    S.op("dve", "tensor_scalar", dict(out=SELB[:], in0=IMP2[:], scalar1=M16[:, 15:16], scalar2=-30000.0, op0=ALU.is_lt, op1=ALU.mult),
                         reads=[rIMP], writes=[rIMP])
                    yield from idle(6)
                    while b7lock["owner"] not in (None, k):
                        yield
                    S.op("pe", "transpose", dict(out=bk16(7)[0:64, 0:128], in_=SELB[:, :], identity=identb[:]), reads=[rIMP, rconst], writes=[rbank[7]])
                    S.op("dve", "tensor_copy", dict(out=QT[64:128, hs, qs], in_=bk16(7)[0:64, 0:128].unsqueeze(1).to_broadcast([64, 4, 128])),
                         reads=[rbank[7]], writes=[rQS[i][g]])
                    yield "GATE"
                    units = []
                    for kt in range(qt + 1):
                        ks = slice(kt * 128, (kt + 1) * 128)
                        units.append((KE[:, g, ks], QT[:, hs, qs], [rKE, rQT[i], rQS[i][g]],
                                      trile if kt == qt else None, 0.125, VS[:, kt, g, :], [rVS]))
                    n_slc = len(units)
                    glv = GS[:, i, 12 * g:12 * g + 12].rearrange("p (h b) -> p b h", b=3)

                    def slc_finalize_rest():
                        for h_ in range(4):
                            S.op("pe", "transpose", dict(out=bk(6)[:, h_ * 65:(h_ + 1) * 65], in_=accS[sx][:, h_ * 128:(h_ + 1) * 128],
                                                         identity=ident[0:65, 0:65]), reads=[raccS[sx], rconst], writes=[rbank[6]])
                        S.op("dve", "reciprocal", dict(out=SM[:, 12:16], in_=bk(6)[:, 64:260:65]), reads=[rbank[6]], writes=[rSM])
                        S.op("dve", "tensor_tensor", dict(out=WT[:, 1, :], in0=glv[:, 1, :], in1=SM[:, 12:16], op=ALU.mult), reads=[rGS, rSM], writes=[rSM])
                        S.op("dve", "tensor_tensor", dict(out=OC[:], in0=bk(6)[:, 0:260].rearrange("p (h d) -> p h d", h=4)[:, :, 0:64],
                                                          in1=WT[:, 1, :].unsqueeze(2).to_broadcast([128, 4, 64]), op=ALU.mult),
                             reads=[rbank[6], rSM, rOA], writes=[rOC])
                        S.op("dve", "tensor_tensor", dict(out=OA[:], in0=OA[:], in1=OC[:], op=ALU.add), reads=[rOC, rOA], writes=[rOA])

                    for kt in range(qt - 4, qt + 1):
                        ks = slice(kt * 128, (kt + 1) * 128)
                        mk = trile if kt == qt else (trigt if kt == qt - 4 else None)
                        units.append((KW[:, g, ks], QT[0:64, hs, qs], [rKW, rQT[i]], mk, 0.125, VW[:, kt, g, :], [rVW]))

                    def copy_out():
                        S.op("dve", "tensor_copy", dict(out=accS[sx][:, :], in_=bk(accb)[0:65, :]), reads=[rbank[accb]], writes=[raccS[sx]])

                    for n_, _ in enumerate(attn_units(units, sx, accb, group_starts=(0, n_slc), before_pv={n_slc: copy_out})):
                        yield
                        if n_ == n_slc:
                            slc_finalize_rest()
                            yield
                    S.op("dve", "tensor_copy", dict(out=accS[sx][:, :], in_=bk(accb)[0:65, :]), reads=[rbank[accb]], writes=[raccS[sx]])
                    yield "RELEASE"
                    yield from idle(3)
                    for h_ in range(4):
                        S.op("pe", "transpose", dict(out=bk(6)[:, h_ * 65:(h_ + 1) * 65], in_=accS[sx][:, h_ * 128:(h_ + 1) * 128],
                                                     identity=ident[0:65, 0:65]), reads=[raccS[sx], rconst], writes=[rbank[6]])
                    S.op("dve", "reciprocal", dict(out=SM[:, 12:16], in_=bk(6)[:, 64:260:65]), reads=[rbank[6]], writes=[rSM])
                    S.op("dve", "tensor_tensor", dict(out=WT[:, 2, :], in0=glv[:, 2, :], in1=SM[:, 12:16], op=ALU.mult), reads=[rGS, rSM], writes=[rSM])
                    S.op("dve", "tensor_tensor", dict(out=OC[:], in0=bk(6)[:, 0:260].rearrange("p (h d) -> p h d", h=4)[:, :, 0:64],
                                                      in1=WT[:, 2, :].unsqueeze(2).to_broadcast([128, 4, 64]), op=ALU.mult),
                         reads=[rbank[6], rSM, rOA], writes=[rOC])
                    S.op("dve", "tensor_tensor", dict(out=OA[:], in0=OA[:], in1=OC[:], op=ALU.add), reads=[rOC, rOA], writes=[rOA])
                    mv = mixA[:, i, 256 * g:256 * g + 256].rearrange("p (h d) -> p h d", h=4)
                    S.op("dve", "tensor_tensor", dict(out=mv, in0=mv, in1=OA[:], op=ALU.mult), reads=[rOA, rmixA[i]], writes=[rmixA[i]])
                    yield

                order = [x for j in range(8) for g in range(2) for x in ((j, g), (j + 8, g))]
                run_staggered([nsa_stream(k, i, g) for k, (i, g) in enumerate(order)], 3, max_active=4, gate_key=lambda k: k % 2)
                dbg("mixA", mixA[:, :, :], [128, NOWN, 512], BF16, reads=rmixA)
                S.flush()
                checkpoint()

        wo = sb(gst, "wo", [128, 8, D], BF16); rwo = Res("wo", multi=True)
        wo_v = w_out.rearrange("(c p) n -> p c n", p=128)
        with contextlib.ExitStack() as st:
            cos16, sin16, rtab16 = rope_tables(st, 16, "r16")
            wm = sb(st, "wm", [128, 8, MLA_COLS], BF16); rwm = Res("wm", multi=True)
            w_v = w_in.rearrange("(c p) n -> p c n", p=128)
            for c in range(8):
                S.dma("pool", wm[:, c, :], w_v[:, c, NSA_COLS:IN_W], writes=[rwm])
            for c in range(8):
                S.dma("pool", wo[:, c, :], wo_v[:, c, :], writes=[rwo])
            cqnT = sb(st, "cqnT", [128, 2, NOWN * 128], BF16); ckvnT = sb(st, "ckvnT", [128, S_LEN], BF16)
            KH = [sb(st, f"KH{i}", [96, S_LEN], BF16) for i in range(2)]
            VH = [sb(st, f"VH{i}", [128, NT, 65], BF16) for i in range(2)]
            rcqn = Res("cqnT"); rckv = Res("ckvnT"); rKHrot = [Res("KHrot0"), Res("KHrot1")]; rKH = [Res("KH0"), Res("KH1")]
            rVH = [Res("VH0"), Res("VH1")]
            wq = sb(st, "wq", [128, 2, 768], BF16); wqp = sb(st, "wqp", [128, 2, 8, 96], BF16); wkv = sb(st, "wkv", [128, 1024], BF16)
            wqf = sb(st, "wqf", [128, 2, 768]); wkvf = sb(st, "wkvf", [128, 1024]); gq = sb(st, "gq", [128, 2]); gkv = sb(st, "gkv", [128, 1])
            rwq = Res("wq", multi=True)
            S.dma("sp", wqf[:], w_q_up.rearrange("(c p) n -> p c n", p=128), writes=[rwq])
            S.dma("sp", wkvf[:], w_kv_up[:, :], writes=[rwq])
            S.dma("sp", gq[:], gq_col[:, :], writes=[rwq])
            S.dma("sp", gkv[:], gkv_col[:, :], writes=[rwq])
            for c in range(2):
                S.op("dve", "tensor_scalar", dict(out=wq[:, c, :], in0=wqf[:, c, :], scalar1=gq[:, c:c + 1], scalar2=None, op0=ALU.mult),
                     reads=[rwq], writes=[rwq])
                wv4 = wq[:, c, :].rearrange("p (h d) -> p h d", h=8)
                S.op("dve", "tensor_copy", dict(out=wqp[:, c, :, 0:64], in_=wv4[:, :, 0:64]), reads=[rwq], writes=[rwq])
                S.op("dve", "tensor_copy", dict(out=wqp[:, c, :, 64:80], in_=wv4[:, :, 80:96]), reads=[rwq], writes=[rwq])
                S.op("dve", "tensor_copy", dict(out=wqp[:, c, :, 80:96], in_=wv4[:, :, 64:80]), reads=[rwq], writes=[rwq])
            S.op("dve", "tensor_scalar", dict(out=wkv[:], in0=wkvf[:], scalar1=gkv[:, 0:1], scalar2=None, op0=ALU.mult), reads=[rwq], writes=[rwq])
            for hb in range(2):
                S.op("dve", "tensor_copy", dict(out=VH[hb][:, :, 64], in_=valid[:]), reads=[rconst], writes=[rVH[hb]])

            cosF = sb(st, "cosF", [128, NOWN * 128]); sinF = sb(st, "sinF", [128, NOWN * 128])
            rF = Res("ropeF")
            with contextlib.ExitStack() as t2:
                pFi = sb(t2, "pFi", [128, NOWN * 128], I32); angF = sb(t2, "angF", [128, NOWN * 128])
                kFi = sb(t2, "kFi", [128, NOWN * 128], I32); invF = sb(t2, "invF", [128, 1]); sgnF = sb(t2, "sgnF", [128, 1])
                rq = Res("pF", multi=True)
                for s_ in range(4):
                    S.dma("sp", pFi[64:96, s_ * 512:(s_ + 1) * 512],
                          pos_i[(2 * s_ + 1) * 512:(2 * s_ + 2) * 512].partition_broadcast(32), writes=[rq])
                S.dma("sp", invF[:], invF_d[:, :], writes=[rq]); S.dma("sp", sgnF[:], sgnF_d[:, :], writes=[rq])
                R = slice(64, 96)
                S.op("dve", "tensor_copy", dict(out=angF[R, :], in_=pFi[R, :]), reads=[rq], writes=[rq])
                S.op("dve", "tensor_scalar", dict(out=angF[R, :], in0=angF[R, :], scalar1=invF[R, 0:1], scalar2=None, op0=ALU.mult), reads=[rq], writes=[rq])
                for tab, off in ((sinF, 0.0), (cosF, PI / 2)):
                    S.op("dve", "tensor_scalar", dict(out=kFi[R, :], in0=angF[R, :], scalar1=off, scalar2=1.0 / (2 * PI), op0=ALU.add, op1=ALU.mult),
                         reads=[rq], writes=[rq])
                    S.op("dve", "tensor_copy", dict(out=tab[R, :], in_=kFi[R, :]), reads=[rq], writes=[rq, rF])
                    S.op("dve", "scalar_tensor_tensor", dict(out=tab[R, :], in0=tab[R, :], scalar=-2 * PI, in1=angF[R, :], op0=ALU.mult, op1=ALU.add),
                         reads=[rq], writes=[rq, rF])
                    S.op("dve", "tensor_scalar", dict(out=tab[R, :], in0=tab[R, :], scalar1=off, scalar2=PI, op0=ALU.add, op1=ALU.min),
                         reads=[rq], writes=[rq, rF])
                    S.op("dve", "tensor_scalar", dict(out=tab[R, :], in0=tab[R, :], scalar1=-PI, scalar2=None, op0=ALU.max), reads=[rq], writes=[rq, rF])
                    S.op("act", "activation", dict(out=tab[R, :], in_=tab[R, :], func=AF.Sin), reads=[rq], writes=[rq, rF])
                S.op("dve", "tensor_scalar", dict(out=sinF[R, :], in0=sinF[R, :], scalar1=sgnF[R, 0:1], scalar2=None, op0=ALU.mult), reads=[rq], writes=[rq, rF])
                S.flush()
                checkpoint()

            with contextlib.ExitStack() as t1:
                xt = [sb(t1, f"xt{i}", [128, D]) for i in range(2)]; rxt = [Res("xt0"), Res("xt1")]
                xn = [sb(t1, f"xn{i}", [128, D], BF16) for i in range(2)]; rxn = [Res("xn0"), Res("xn1")]
                hT = [sb(t1, f"hT{i}", [128, 8, 512], BF16) for i in range(2)]; rhT = [Res("hT0"), Res("hT1")]
                junk = sb(t1, "junk", [128, 256], BF16)
                rtmp = sb(t1, "ropetmp", [128, 64]); latn = [sb(t1, f"latn{i}", [128, 416], BF16) for i in range(2)]; rlat = [Res("latn0"), Res("latn1")]
                st3 = [sb(t1, f"st3{i}", [128, 2, 2]) for i in range(2)]; rst3 = [Res("st30"), Res("st31")]
                LBS = ((0, 1), (2, 4))
                for r in range(4):
                    make_hT(r, xt, xn, hT, rxt, rxn, rhT)

                def stage1(hh):
                    sq = st3[hh % 2]; rsq = rst3[hh % 2]
                    for j in range(2):
                        t = 2 * hh + j
                        blk, r = divmod(t, 4)
                        hb = blk % 2
                        i = own_index(t)
                        lbk = LBS[hh % 2][j]
                        if blk + 1 < 8:
                            make_hT_a(t + 4, xt, xn, rxt, rxn)
                        for c in range(8):
                            S.op("pe", "matmul", dict(out=bk(lbk)[:, 0:416], lhsT=hT[hb][:, c, r * 128:(r + 1) * 128], rhs=wm[:, c, 0:416],
                                                      start=(c == 0), stop=(c == 7)), reads=[rwm, rhT[hb]], writes=[rbank[lbk]])
                        for k_, (a_, n_) in enumerate(((0, 256), (256, 128))):
                            S.op("act", "activation", dict(out=junk[:, 0:n_], in_=bk(lbk)[:, a_:a_ + n_], func=AF.Square, accum_out=sq[:, k_, j:j + 1]),
                                 reads=[rbank[lbk]], writes=[rsq])
                        if i is not None:
                            for c in range(8):
                                S.op("pe", "matmul", dict(out=bk(3)[:, :], lhsT=hT[hb][:, c, r * 128:(r + 1) * 128], rhs=wm[:, c, 416:928],
                                                          start=(c == 0), stop=(c == 7)), reads=[rwm, rhT[hb]], writes=[rbank[3]])
                            S.op("act", "activation", dict(out=mixB[:, i, :], in_=bk(3)[:, :], func=AF.Silu), reads=[rbank[3]], writes=[rmixB, rmixBt[i]])
                        if blk + 1 < 8:
                            make_hT_b(t + 4, xn, hT, rxn, rhT, n_dve=8)

                def stats(hh):
                    sq = st3[hh % 2]; rsq = rst3[hh % 2]
                    for k_, n_ in enumerate((256, 128)):
                        S.op("dve", "tensor_scalar", dict(out=sq[:, k_, :], in0=sq[:, k_, :], scalar1=1.0 / n_, scalar2=1e-6, op0=ALU.mult, op1=ALU.add),
                             reads=[rsq], writes=[rsq])
                    S.op("act", "activation", dict(out=sq[:, :, :], in_=sq[:, :, :], func=AF.Sqrt), reads=[rsq], writes=[rsq])
                    S.op("dve", "reciprocal", dict(out=sq[:, :, :], in_=sq[:, :, :]), reads=[rsq], writes=[rsq])

                def stage2(hh):
                    sq = st3[hh % 2]; rsq = rst3[hh % 2]
                    for j in range(2):
                        t = 2 * hh + j
                        i = own_index(t)
                        lb = j
                        lbk = LBS[hh % 2][j]
                        ps = bk(lbk)
                        if i is not None:
                            S.op("dve", "tensor_scalar", dict(out=latn[lb][:, 0:256], in0=ps[:, 0:256], scalar1=sq[:, 0, j:j + 1], scalar2=None, op0=ALU.mult),
                                 reads=[rbank[lbk], rsq], writes=[rlat[lb]])
                        S.op("dve", "tensor_scalar", dict(out=latn[lb][:, 256:384], in0=ps[:, 256:384], scalar1=sq[:, 1, j:j + 1], scalar2=None, op0=ALU.mult),
                             reads=[rbank[lbk], rsq], writes=[rlat[lb]])
                        rope_tm(ps[:, 384:416].rearrange("p (h d) -> p h d", h=1), latn[lb][:, 384:416].rearrange("p (h d) -> p h d", h=1),
                                1, 16, cos16[:, t, :], sin16[:, t, :], rtmp, [rbank[lbk], rtab16], [rlat[lb]])
                        S.op("pe", "transpose", dict(out=bk16(7)[:, 0:128], in_=latn[lb][:, 256:384], identity=identb[:]), reads=[rlat[lb], rconst], writes=[rbank[7]])
                        S.op("pe", "transpose", dict(out=bk16(7)[0:32, 128:256], in_=latn[lb][:, 384:416], identity=identb[:]), reads=[rlat[lb], rconst], writes=[rbank[7]])
                        if i is not None:
                            for c in range(2):
                                S.op("pe", "transpose", dict(out=bk16(7)[:, 256 + c * 128:384 + c * 128], in_=latn[lb][:, c * 128:(c + 1) * 128], identity=identb[:]),
                                     reads=[rlat[lb], rconst], writes=[rbank[7]])
                        S.op("dve", "tensor_copy", dict(out=ckvnT[:, t * 128:(t + 1) * 128], in_=bk16(7)[:, 0:128]), reads=[rbank[7]], writes=[rckv])
                        for hb2 in range(2):
                            S.op("dve", "tensor_copy", dict(out=KH[hb2][64:96, t * 128:(t + 1) * 128], in_=bk16(7)[0:32, 128:256]),
                                 reads=[rbank[7]], writes=[rKHrot[hb2]])
                        if i is not None:
                            S.op("dve", "tensor_copy", dict(out=cqnT[:, :, i * 128:(i + 1) * 128], in_=bk16(7)[:, 256:512].rearrange("p (c q) -> p c q", c=2)),
                                 reads=[rbank[7]], writes=[rcqn])

                for hh in range(16):
                    stage1(hh)
                    if hh >= 1:
                        stage2(hh - 1)
                    stats(hh)
                stage2(15)
                dbg("ckvnT", ckvnT[:, :], [128, S_LEN], BF16, reads=[rckv])
                dbg("cqnT", cqnT[:, :, :], [128, 2, NOWN * 128], BF16, reads=[rcqn])
                dbg("krot", KH[0][64:96, :], [32, S_LEN], BF16, reads=[rKHrot[0]])
                S.flush()
                checkpoint()

            with contextlib.ExitStack() as t1:
                Pt = [sb(t1, f"Pt{i}", [128, 512], BF16) for i in range(6)]; rPt = [Res(f"Pt{i}") for i in range(6)]
                accS = [sb(t1, f"accS{i}", [65, 512]) for i in range(2)]; raccS = [Res("accS0"), Res("accS1")]
                qh2 = [[sb(t1, f"qh{i}_{j}", [96, 512], BF16) for j in range(2)] for i in range(2)]; rqh2 = [[Res(f"qh{i}_{j}") for j in range(2)] for i in range(2)]
                qtmp = [sb(t1, f"qtmp{i}", [128, 512]) for i in range(2)]; rqtmp = [Res("qtmp0"), Res("qtmp1")]
                qtmp2 = [sb(t1, f"qtmpb{i}", [128, 512]) for i in range(2)]; rqtmp2 = [Res("qtmpb0"), Res("qtmpb1")]
                sm = [sb(t1, f"sm{i}", [128, 16]) for i in range(2)]; rsm = [Res("sm0"), Res("sm1")]
                otmp = [sb(t1, f"otmp{i}", [128, 4, 64]) for i in range(2)]; rot = [Res("otmp0"), Res("otmp1")]
                SC = float(96 ** -0.5)
                R = slice(64, 96)

                def mla_stream(h):
                    hb = h % 2
                    accb = 4 + hb
                    for blk in range(8):
                        gb = 7 - (blk % 2)
                        S.op("pe", "matmul", dict(out=bk(gb)[0:64, :], lhsT=wkv[:, h * 128:h * 128 + 64], rhs=ckvnT[:, blk * 512:(blk + 1) * 512],
                                                  start=True, stop=True), reads=[rwq, rckv], writes=[rbank[gb]])
                        S.op("dve", "tensor_copy", dict(out=KH[hb][0:64, blk * 512:(blk + 1) * 512], in_=bk(gb)[0:64, :]),
                             reads=[rbank[gb]], writes=[rKH[hb]])
                        yield
                    for t8 in range(4):
                        gb = 7 - (t8 % 2)
                        for tt in range(8):
                            t = t8 * 8 + tt
                            S.op("pe", "matmul", dict(out=bk(gb)[:, tt * 64:(tt + 1) * 64], lhsT=ckvnT[:, t * 128:(t + 1) * 128],
                                                      rhs=wkv[:, h * 128 + 64:h * 128 + 128], start=True, stop=True),
                                 reads=[rwq, rckv], writes=[rbank[gb]])
                        if t8 == 0:
                            for tt in range(4):
                                S.op("dve", "tensor_scalar", dict(out=VH[hb][:, tt, 0:64], in0=bk(gb)[:, tt * 64:(tt + 1) * 64], scalar1=valid[:, tt:tt + 1],
                                                                  scalar2=None, op0=ALU.mult), reads=[rbank[gb], rconst], writes=[rVH[hb]])
                            S.op("dve", "tensor_copy", dict(out=VH[hb][:, 4:8, 0:64], in_=bk(gb)[:, 256:512].rearrange("p (t d) -> p t d", t=4)),
                                 reads=[rbank[gb]], writes=[rVH[hb]])
                        else:
                            S.op("dve", "tensor_copy", dict(out=VH[hb][:, t8 * 8:(t8 + 1) * 8, 0:64], in_=bk(gb)[:, :].rearrange("p (t d) -> p t d", t=8)),
                                 reads=[rbank[gb]], writes=[rVH[hb]])
                        yield
                    def qgen(s_):
                        cs = slice(s_ * 512, (s_ + 1) * 512)
                        qd = qh2[hb][s_ % 2]; rqd = rqh2[hb][s_ % 2]
                        for c in range(2):
                            S.op("pe", "matmul", dict(out=bk(7)[0:96, :], lhsT=wq[:, c, h * 96:(h + 1) * 96], rhs=cqnT[:, c, cs],
                                                      start=(c == 0), stop=(c == 1)), reads=[rwq, rcqn], writes=[rbank[7]])
                        for c in range(2):
                            S.op("pe", "matmul", dict(out=bk(6)[0:96, :], lhsT=wqp[:, c, h, :], rhs=cqnT[:, c, cs],
                                                      start=(c == 0), stop=(c == 1)), reads=[rwq, rcqn], writes=[rbank[6]])
                        S.op("dve", "tensor_copy", dict(out=qd[0:64, :], in_=bk(7)[0:64, :]), reads=[rbank[7]], writes=[rqd])
                        S.op("dve", "tensor_tensor", dict(out=qtmp[hb][R, :], in0=bk(7)[R, :], in1=cosF[R, cs], op=ALU.mult), reads=[rbank[7], rF], writes=[rqtmp[hb]])
                        S.op("dve", "tensor_tensor", dict(out=qtmp2[hb][R, :], in0=bk(6)[R, :], in1=sinF[R, cs], op=ALU.mult), reads=[rbank[6], rF], writes=[rqtmp2[hb]])
                        S.op("dve", "tensor_tensor", dict(out=qd[R, :], in0=qtmp[hb][R, :], in1=qtmp2[hb][R, :], op=ALU.add),
                             reads=[rqtmp[hb], rqtmp2[hb]], writes=[rqd])

                    def fin_rest(s_):
                        for r in range(4):
                            S.op("pe", "transpose", dict(out=bk(6)[:, r * 65:(r + 1) * 65], in_=accS[hb][:, r * 128:(r + 1) * 128], identity=ident[0:65, 0:65]),
                                 reads=[raccS[hb], rconst], writes=[rbank[6]])
                        S.op("dve", "reciprocal", dict(out=sm[hb][:, 4:8], in_=bk(6)[:, 64:260:65]), reads=[rbank[6]], writes=[rsm[hb]])
                        S.op("dve", "tensor_tensor", dict(out=otmp[hb][:], in0=bk(6)[:, 0:260].rearrange("p (r d) -> p r d", r=4)[:, :, 0:64],
                                                          in1=sm[hb][:, 4:8].unsqueeze(2).to_broadcast([128, 4, 64]), op=ALU.mult),
                             reads=[rbank[6], rsm[hb]], writes=[rot[hb]])
                        mvb = mixB[:, s_ * 4:(s_ + 1) * 4, h * 64:(h + 1) * 64]
                        rmv = [rmixBt[s_ * 4 + r] for r in range(4)]
                        S.op("dve", "tensor_tensor", dict(out=mvb, in0=mvb, in1=otmp[hb][:, :, :], op=ALU.mult), reads=[rot[hb]] + rmv, writes=rmv)

                    qgen(0)
                    yield from idle(4)
                    flat = [(s_, u) for s_ in range(4) for u in range(8 * s_ + 8)]
                    nflat = len(flat)

                    def emit_score(gi):
                        s_, u = flat[gi]
                        nk = 8 * s_ + 8
                        b = 2 * hb + (gi % 2)
                        c0 = max(0, u - (nk - 4)) * 128
                        qd = qh2[hb][s_ % 2]; rqd = rqh2[hb][s_ % 2]
                        S.op("pe", "matmul", dict(out=bk(b)[:, c0:512], lhsT=KH[hb][:, u * 128:(u + 1) * 128], rhs=qd[:, c0:512],
                                                  start=True, stop=True), reads=[rKH[hb], rKHrot[hb], rqd], writes=[rbank[b]])
                    emit_score(0)
                    emit_score(1)
                    for gi in range(nflat):
                        s_, u = flat[gi]
                        nk = 8 * s_ + 8
                        b = 2 * hb + (gi % 2)
                        p = 3 * hb + (gi % 3)
                        c0 = max(0, u - (nk - 4)) * 128
                        S.op("act", "activation", dict(out=Pt[p][:, c0:512], in_=bk(b)[:, c0:512], func=AF.Exp, scale=SC),
                             reads=[rbank[b]], writes=[rPt[p]])
                        if u >= nk - 4:
                            S.op("dve", "tensor_tensor", dict(out=Pt[p][:, c0:c0 + 128], in0=Pt[p][:, c0:c0 + 128], in1=trile[:], op=ALU.mult),
                                 reads=[rPt[p], rconst], writes=[rPt[p]])
                        if gi + 2 < nflat:
                            emit_score(gi + 2)
                        if u == 0 and s_ > 0:
                            S.op("dve", "tensor_copy", dict(out=accS[hb][:, :], in_=bk(accb)[0:65, :]), reads=[rbank[accb]], writes=[raccS[hb]])
                        S.op("pe", "matmul", dict(out=bk(accb)[0:65, c0:512], lhsT=VH[hb][:, u, :], rhs=Pt[p][:, c0:512],
                                                  start=(u == 0), stop=(u == nk - 1)), reads=[rPt[p], rVH[hb]], writes=[rbank[accb]])
                        yield
                        if u == 1 and s_ > 0:
                            fin_rest(s_ - 1)
                            yield
                        if u == 3 and s_ + 1 < 4:
                            qgen(s_ + 1)
                            yield
                    S.op("dve", "tensor_copy", dict(out=accS[hb][:, :], in_=bk(accb)[0:65, :]), reads=[rbank[accb]], writes=[raccS[hb]])
                    yield from idle(3)
                    fin_rest(3)
                    yield

                run_staggered([mla_stream(h) for h in range(8)], 44, max_active=2, gate_key=lambda k: k)
                dbg("mixB", mixB[:, :, :], [128, NOWN, 512], BF16, reads=rmixBt)
                S.flush()
                checkpoint()

        with contextlib.ExitStack() as st:
            xt = [sb(st, f"xo{i}", [128, D]) for i in range(2)]; rxt = [Res("xo0"), Res("xo1")]
            mT = [sb(st, f"mT{i}", [128, 8, 128], BF16) for i in range(2)]; rmT = [Res("mT0"), Res("mT1")]
            yo = [sb(st, f"yo{i}", [128, D]) for i in range(2)]; ryo = [Res("yo0"), Res("yo1")]
            junk = sb(st, "junk5", [128, D], BF16); st5 = [sb(st, f"st5{i}", [128, 4]) for i in range(2)]; rst5 = [Res("st50"), Res("st51")]

            def p5_front(i):
                b = i % 2
                t = 8 * (i // 4) + 4 + (i % 4)
                tb = 7 - b
                S.dma("sp", xt[b][:], xs[t * 128:(t + 1) * 128, :], writes=[rxt[b]])
                for c in range(8):
                    src = mixA[:, i, c * 128:(c + 1) * 128] if c < 4 else mixB[:, i, (c - 4) * 128:(c - 3) * 128]
                    S.op("pe", "transpose", dict(out=bk16(tb)[:, c * 128:(c + 1) * 128], in_=src, identity=identb[:]),
                         reads=[rmixA[i], rmixBt[i], rconst], writes=[rbank[tb]])
                S.op("act", "activation", dict(out=mT[b][:, :, :], in_=bk16(tb)[:, :].rearrange("p (c q) -> p c q", c=8), func=AF.Copy),
                     reads=[rbank[tb]], writes=[rmT[b]])

            p5_front(0)
            for i in range(NOWN):
                b = i % 2
                if i + 1 < NOWN:
                    p5_front(i + 1)
                for hh in range(2):
                    ob = 2 * b + hh
                    for c in range(8):
                        S.op("pe", "matmul", dict(out=bk(ob)[:, :], lhsT=mT[b][:, c, :], rhs=wo[:, c, hh * 512:(hh + 1) * 512],
                                                  start=(c == 0), stop=(c == 7)), reads=[rmT[b], rwo], writes=[rbank[ob]])
                    hsl = slice(hh * 512, (hh + 1) * 512)
                    S.op("dve", "tensor_tensor", dict(out=yo[b][:, hsl], in0=bk(ob)[:, :], in1=gate_bc[:, hsl], op=ALU.mult),
                         reads=[rbank[ob], rmod], writes=[ryo[b]])
                S.op("dve", "tensor_tensor", dict(out=yo[b][:, :], in0=yo[b][:, :], in1=xt[b][:, :], op=ALU.add), reads=[ryo[b], rxt[b]], writes=[ryo[b]])
                S.op("act", "activation", dict(out=junk[:], in_=yo[b][:], func=AF.Square, accum_out=st5[b][:, 0:1]), reads=[ryo[b]], writes=[rst5[b]])
                S.op("dve", "tensor_scalar", dict(out=st5[b][:, 1:2], in0=st5[b][:, 0:1], scalar1=1.0 / D, scalar2=1e-6, op0=ALU.mult, op1=ALU.add), reads=[rst5[b]], writes=[rst5[b]])
                S.op("act", "activation", dict(out=st5[b][:, 2:3], in_=st5[b][:, 1:2], func=AF.Sqrt), reads=[rst5[b]], writes=[rst5[b]])
                S.op("dve", "reciprocal", dict(out=st5[b][:, 2:3], in_=st5[b][:, 2:3]), reads=[rst5[b]], writes=[rst5[b]])
                S.op("dve", "scalar_tensor_tensor", dict(out=yo[b][:, :], in0=yo[b][:, :], scalar=st5[b][:, 2:3], in1=fing_bc[:, :], op0=ALU.mult, op1=ALU.mult),
                     reads=[ryo[b], rst5[b], rconst], writes=[ryo[b]])
                S.dma("sp", out_d[i * 128:(i + 1) * 128, :], yo[b][:, :], reads=[ryo[b]])
            S.wait_all_dma("sp")
            S.flush()
            checkpoint()
    return nc, dbg_out


def _constants(par):
    delta = 1 - par
    c = {}
    c["inv32"] = (10000.0 ** (-np.arange(32, dtype=np.float32) / 32)).astype(np.float32)
    c["inv16"] = (10000.0 ** (-np.arange(16, dtype=np.float32) / 16)).astype(np.float32)
    invF = np.zeros((128, 1), np.float32); sgnF = np.zeros((128, 1), np.float32)
    for r in range(64, 96):
        invF[r, 0] = c["inv16"][(r - 64) % 16]
        sgnF[r, 0] = -1.0 if r < 80 else 1.0
    c["invF"] = invF; c["sgnF"] = sgnF
    k = np.arange(S_LEN)
    c["E_aug"] = (k[None, :] // 64 == np.arange(64)[:, None]).astype(np.float32)
    c["ident"] = np.eye(128, dtype=np.float32)
    kk = np.arange(128)[:, None]; qq = np.arange(128)[None, :]
    c["tri_le"] = (kk <= qq).astype(np.float32)
    c["tri_gt"] = (kk > qq).astype(np.float32)
    own_t = np.concatenate([np.arange((2 * s + 1) * 512, (2 * s + 2) * 512) for s in range(4)])
    n = np.arange(256)[:, None]
    c["cmpmask"] = ((16 * n + 31 <= own_t[None, :]) & (n < 255)).astype(np.float32)
    j = np.arange(64)[None, :]
    cur = (own_t // 64)[:, None]
    jr = j - 8 * delta; curr = cur - 8 * delta
    forced = (jr == 0) | (jr == curr) | (jr == curr - 1)
    keep = np.ones((NOWN * 128, 64), np.float32); force = np.zeros((NOWN * 128, 64), np.float32)
    fut = jr > curr
    dummy = jr < 0
    force[forced & ~dummy & ~fut] = 1.0e4
    c["impkeep"] = keep; c["impforce"] = force
    start = np.arange(256)[:, None] * 16; bstart = np.arange(64)[None, :] * 64
    ov = np.minimum(start + 32, bstart + 64) - np.maximum(start, bstart)
    m1 = (np.clip(ov, 0, None) / 32).astype(np.float32); m1[255] = 0.0
    c["m1"] = m1
    return c


_CACHE = {}


def kernel(x, c, positions, ada_w, ada_b, norm_g, w_in, cmp_pos, cmp_k_w1, cmp_k_w2, cmp_v_w1, cmp_v_w2,
           q_norm_g, w_q_up, kv_norm_g, w_kv_up, w_out, final_norm_g, _dbg=(), _stop=99):
    x = np.asarray(x, np.float32); c = np.asarray(c, np.float32); positions = np.asarray(positions, np.int32)
    key = (tuple(_dbg), _stop)
    if key not in _CACHE:
        _CACHE[key] = build(_dbg, _stop)
    nc, dbg_out = _CACHE[key]
    shared = {
        "ada_w": np.asarray(ada_w, np.float32)[0], "ada_b": np.asarray(ada_b, np.float32)[0], "norm_g": np.asarray(norm_g, np.float32)[0],
        "w_in": np.asarray(w_in, np.float32)[0], "cmp_pos": np.asarray(cmp_pos, np.float32)[0],
        "cmp_k_w1": np.asarray(cmp_k_w1, np.float32)[0], "cmp_k_w2": np.asarray(cmp_k_w2, np.float32)[0],
        "cmp_v_w1": np.asarray(cmp_v_w1, np.float32)[0], "cmp_v_w2": np.asarray(cmp_v_w2, np.float32)[0],
        "q_norm_g": np.asarray(q_norm_g, np.float32)[0], "w_q_up": np.asarray(w_q_up, np.float32)[0],
        "kv_norm_g": np.asarray(kv_norm_g, np.float32)[0], "w_kv_up": np.asarray(w_kv_up, np.float32)[0],
        "w_out": np.asarray(w_out, np.float32)[0], "final_norm_g": np.asarray(final_norm_g, np.float32),
    }
    shared["g_col"] = np.ascontiguousarray(shared["norm_g"].reshape(8, 128).T)
    shared["gq_col"] = np.ascontiguousarray(shared["q_norm_g"].reshape(2, 128).T)
    shared["gkv_col"] = np.ascontiguousarray(shared["kv_norm_g"].reshape(1, 128).T)
    shared["cmp_posT"] = np.ascontiguousarray(shared["cmp_pos"].T)
    consts = [_constants(0), _constants(1)]
    in_maps = []
    for core in range(8):
        b, par = divmod(core, 2)
        if par == 1:
            xs = x[b]; ps = positions[b]; valid = np.ones(S_LEN, np.float32)
        else:
            xs = np.concatenate([np.zeros((512, D), np.float32), x[b, :S_LEN - 512]], axis=0)
            ps = np.concatenate([np.zeros(512, np.int32), positions[b, :S_LEN - 512]])
            valid = np.concatenate([np.zeros(512, np.float32), np.ones(S_LEN - 512, np.float32)])
        m = {"xs": np.ascontiguousarray(xs), "pos_i": np.ascontiguousarray(ps), "valid": valid, "c_b": np.ascontiguousarray(c[b])}
        m["valid_pt"] = np.ascontiguousarray(valid.reshape(NT, 128).T)
        m["pos_pt"] = np.ascontiguousarray(ps.reshape(NT, 128).T)
        m["c_col"] = np.ascontiguousarray(c[b].reshape(8, 128).T)
        nidx = np.minimum(np.arange(256), 254)
        vc_ = valid[16 * nidx].copy(); vc_[255] = 0.0
        cp_ = ps[16 * nidx + 31].copy(); cp_[255] = 0
        m["validc_pt"] = np.ascontiguousarray(vc_.reshape(2, 128).T.astype(np.float32))
        m["cpos_pt"] = np.ascontiguousarray(cp_.reshape(2, 128).T.astype(np.int32))
        m.update(shared)
        m.update(consts[par])
        in_maps.append(m)
    res = run_bass_kernel_spmd(nc, in_maps, core_ids=list(range(8)))
    out = np.zeros((4, S_LEN, D), np.float32)
    for core in range(8):
        b, par = divmod(core, 2)
        o = np.asarray(res.results[core]["out"], np.float32)
        for s in range(4):
            qb = 2 * s + par
            out[b, qb * 512:(qb + 1) * 512, :] = o[s * 512:(s + 1) * 512, :]
    if _dbg:
        kernel.last_dbg = [{k: np.asarray(res.results[core]["dbg_" + k]) for k in dbg_out} for core in range(8)]
    return out
```

```python
import contextlib
import numpy as np
import ml_dtypes
import concourse.bass as bass
import concourse.mybir as mybir
from concourse.bass_utils import run_bass_kernel_spmd

F32 = mybir.dt.float32
BF16 = mybir.dt.bfloat16
I32 = mybir.dt.int32
ALU = mybir.AluOpType
AF = mybir.ActivationFunctionType

S_LEN = 4096
D = 1024
NT = 32
NOWN = 16
IN_W = 2744
PI = float(np.pi)

ENGS = ("pe", "act", "dve", "pool", "sp")
SAME_ENG_SYNC = {"pe": False, "act": True, "dve": True, "pool": True, "sp": False}
N_DMA_SLOTS = 10


class Res:
    __slots__ = ("name", "w", "r", "excl", "multi", "ws")

    def __init__(self, name, excl=False, multi=False):
        self.name = name
        self.w = None
        self.r = {}
        self.excl = excl
        self.multi = multi
        self.ws = {}


class Sched:
    def __init__(self, nc, stack):
        self.nc = nc
        self.sems = {}
        for e in ENGS:
            self.sems[e] = stack.enter_context(nc.semaphore("s_" + e))
        self.dq = ("sp", "pool")
        for q in self.dq:
            for i in range(N_DMA_SLOTS):
                self.sems[("d", q, i)] = stack.enter_context(nc.semaphore(f"d_{q}{i}"))
        self.cnt = {k: 0 for k in self.sems}
        self.ops = {e: [] for e in ENGS}
        self.seen = {e: {} for e in ENGS}
        self.dslot = {q: 0 for q in self.dq}
        self.dead = False

    def _wait(self, eng, key, val):
        if self.dead:
            return
        if self.seen[eng].get(key, 0) >= val:
            return
        self.seen[eng][key] = val
        sem = self.sems[key]
        self.ops[eng].append(lambda e, sem=sem, val=val: e.wait_ge(sem, val))

    def _deps(self, eng, reads, writes, extra=()):
        deps = {}

        def add(tok):
            if tok is None:
                return
            k, v = tok
            if deps.get(k, 0) < v:
                deps[k] = v
        for r in reads:
            add(r.w)
            if r.multi:
                for k, v in r.ws.items():
                    add((k, v))
            if r.excl:
                for k, v in r.r.items():
                    if k != eng:
                        add((k, v))
        for w in writes:
            if w.multi:
                continue
            add(w.w)
            for k, v in w.r.items():
                add((k, v))
        for t in extra:
            add(t)
        for k, v in deps.items():
            if k == eng and not SAME_ENG_SYNC[eng]:
                continue
            self._wait(eng, k, v)

    def _mark(self, tok, reads, writes):
        k, v = tok
        for r in reads:
            if r.r.get(k, 0) < v:
                r.r[k] = v
        for w in writes:
            if w.multi:
                if w.ws.get(k, 0) < v:
                    w.ws[k] = v
                continue
            w.w = tok
            w.r = {}

    def op(self, eng, meth, kw, reads=(), writes=()):
        if self.dead:
            return None
        self._deps(eng, reads, writes)
        sem = self.sems[eng]
        self.cnt[eng] += 1
        tok = (eng, self.cnt[eng])
        self.ops[eng].append(lambda e, meth=meth, kw=kw, sem=sem: getattr(e, meth)(**kw).then_inc(sem, 1))
        self._mark(tok, reads, writes)
        return tok

    def dma(self, q, out, in_, reads=(), writes=(), **kw):
        if self.dead:
            return None
        slot = self.dslot[q]
        self.dslot[q] = (slot + 1) % N_DMA_SLOTS
        key = ("d", q, slot)
        prev = (key, self.cnt[key]) if self.cnt[key] > 0 else None
        self._deps(q, reads, writes, extra=(prev,))
        self.cnt[key] += 16
        tok = (key, self.cnt[key])
        sem = self.sems[key]
        self.ops[q].append(lambda e, out=out, in_=in_, sem=sem, kw=kw:
                           e.dma_start(out=out, in_=in_, **kw).then_inc(sem, 16))
        self._mark(tok, reads, writes)
        return tok

    def wait_all_dma(self, eng):
        for q in self.dq:
            for i in range(N_DMA_SLOTS):
                k = ("d", q, i)
                if self.cnt[k]:
                    self._wait(eng, k, self.cnt[k])

    def flush(self):
        if not any(self.ops[e] for e in ENGS):
            return
        nc = self.nc
        ops = self.ops
        with nc.Block() as block:
            @block.tensor
            def _(e):
                for f in ops["pe"]:
                    f(e)

            @block.scalar
            def _(e):
                for f in ops["act"]:
                    f(e)

            @block.vector
            def _(e):
                for f in ops["dve"]:
                    f(e)

            @block.gpsimd
            def _(e):
                for f in ops["pool"]:
                    f(e)

            @block.sync
            def _(e):
                for f in ops["sp"]:
                    f(e)
        self.ops = {e: [] for e in ENGS}


O_Q, O_KC, O_VC, O_KS, O_VS, O_KW, O_VW, O_GL, O_ZN, O_CQ, O_CKV, O_KR, O_ZM = (
    0, 512, 640, 768, 896, 1024, 1152, 1280, 1304, 1816, 2072, 2200, 2232)
NSA_COLS = 1816
MLA_COLS = IN_W - NSA_COLS


class _Stop(Exception):
    pass


def build(dbg_names=(), stop=99):
    nc = bass.Bass("TRN2", target_bir_lowering=False)
    dram = {}

    def din(name, shape, dt=F32):
        dram[name] = nc.dram_tensor(name, list(shape), dt, kind="ExternalInput").ap()
        return dram[name]

    xs = din("xs", [S_LEN, D])
    pos_i = din("pos_i", [S_LEN], I32)
    valid_d = din("valid", [S_LEN])
    c_b = din("c_b", [D])
    ada_w = din("ada_w", [D, 3 * D]); ada_b = din("ada_b", [3 * D]); norm_g = din("norm_g", [D])
    w_in = din("w_in", [D, IN_W]); cmp_pos = din("cmp_pos", [32, 64])
    ckw1 = din("cmp_k_w1", [2048, 128]); ckw2 = din("cmp_k_w2", [128, 64])
    cvw1 = din("cmp_v_w1", [2048, 128]); cvw2 = din("cmp_v_w2", [128, 64])
    q_norm_g = din("q_norm_g", [256]); w_q_up = din("w_q_up", [256, 768])
    kv_norm_g = din("kv_norm_g", [128]); w_kv_up = din("w_kv_up", [128, 1024])
    w_out = din("w_out", [D, D]); fin_g = din("final_norm_g", [D])
    inv32_d = din("inv32", [32]); inv16_d = din("inv16", [16])
    invF_d = din("invF", [128, 1]); sgnF_d = din("sgnF", [128, 1])
    E_d = din("E_aug", [64, S_LEN]); ident_d = din("ident", [128, 128])
    trile_d = din("tri_le", [128, 128]); trigt_d = din("tri_gt", [128, 128])
    cmpmask_d = din("cmpmask", [256, NOWN * 128])
    impkeep_d = din("impkeep", [NOWN * 128, 64]); impforce_d = din("impforce", [NOWN * 128, 64])
    m1_d = din("m1", [256, 64])
    valid_pt = din("valid_pt", [128, NT]); pos_pt = din("pos_pt", [128, NT], I32)
    c_col = din("c_col", [128, 8]); g_col = din("g_col", [128, 8]); gq_col = din("gq_col", [128, 2]); gkv_col = din("gkv_col", [128, 1])
    posT_d = din("cmp_posT", [64, 32]); validc_pt = din("validc_pt", [128, 2]); cpos_pt = din("cpos_pt", [128, 2], I32)
    out_d = nc.dram_tensor("out", [NOWN * 128, D], F32, kind="ExternalOutput").ap()
    dbg_out = {}

    with contextlib.ExitStack() as gst:
        S = Sched(nc, gst)
        ckpt_n = [0]

        def checkpoint():
            ckpt_n[0] += 1
            if ckpt_n[0] == stop:
                S.wait_all_dma("sp")
                S.flush()
                S.dead = True

        uniq = [0]

        def sb(st, name, shape, dt=F32):
            uniq[0] += 1
            return st.enter_context(nc.sbuf_tensor(f"t{uniq[0]}_{name}", list(shape), dt))

        def dbg(name, ap, shape, dt=F32, reads=()):
            if name in dbg_names:
                d = nc.dram_tensor("dbg_" + name, list(shape), dt, kind="ExternalOutput").ap()
                dbg_out[name] = d
                S.dma("sp", d, ap, reads=reads)

        banks = [gst.enter_context(nc.psum_tensor(f"bank{i}", [128, 512], F32)) for i in range(8)]
        rbank = [Res(f"bank{i}", excl=True) for i in range(8)]

        def bk(i):
            return banks[i]

        def bk16(i):
            return banks[i][:].bitcast(BF16)

        ident = sb(gst, "ident", [128, 128]); identb = sb(gst, "identb", [128, 128], BF16)
        trile = sb(gst, "trile", [128, 128], BF16); trigt = sb(gst, "trigt", [128, 128], BF16)
        ones_f = sb(gst, "ones_f", [128, 128])
        scl1 = sb(gst, "scl1", [128, 8]); shf = sb(gst, "shf", [128, 8])
        gate_bc = sb(gst, "gate_bc", [128, D]); fing_bc = sb(gst, "fing_bc", [128, D])
        valid = sb(gst, "validc", [128, NT]); posf = sb(gst, "posf", [128, NT])
        rstd_all = sb(gst, "rstd_all", [128, NT]); rrstd = Res("rstd")
        mixA = sb(gst, "mixA", [128, NOWN, 512], BF16); mixB = sb(gst, "mixB", [128, NOWN, 512], BF16)
        rconst = Res("const", multi=True); rmod = Res("mod"); rmixA = [Res(f"mixA{i}") for i in range(NOWN)]
        rmixB = Res("mixB"); rmixBt = [Res(f"mixBt{i}") for i in range(NOWN)]

        S.dma("sp", ident[:], ident_d[:, :], writes=[rconst])
        S.dma("pool", identb[:], ident_d[:, :], writes=[rconst])
        S.dma("pool", trile[:], trile_d[:, :], writes=[rconst])
        S.dma("pool", trigt[:], trigt_d[:, :], writes=[rconst])
        S.dma("sp", valid[:], valid_pt[:, :], writes=[rconst])
        S.dma("sp", fing_bc[:], fin_g.partition_broadcast(128), writes=[rconst])
        S.op("dve", "memset", dict(ap=ones_f[:], constant=1.0), writes=[rconst])

        with contextlib.ExitStack() as st:
            ccol = sb(st, "ccol", [128, 8]); scol = sb(st, "scol", [128, 8], BF16)
            gcol = sb(st, "gcol", [128, 8]); posi = sb(st, "posi", [128, NT], I32)
            adab = sb(st, "adab", [1, 3 * D]); modrow = sb(st, "modrow", [1, 3 * D])
            awb = [sb(st, f"awb{i}", [128, 8, 512], BF16) for i in range(2)]
            rawb = [Res("awb0"), Res("awb1")]; rc = Res("ccol", multi=True); rrow = Res("modrow")
            S.dma("sp", ccol[:], c_col[:, :], writes=[rc])
            S.dma("sp", gcol[:], g_col[:, :], writes=[rc])
            S.dma("sp", posi[:], pos_pt[:, :], writes=[rc])
            S.dma("sp", adab[:], ada_b.rearrange("(o n) -> o n", o=1), writes=[rc])
            S.op("act", "activation", dict(out=scol[:], in_=ccol[:], func=AF.Silu), reads=[rc], writes=[rc])
            S.op("dve", "tensor_copy", dict(out=posf[:], in_=posi[:]), reads=[rc], writes=[rconst])
            aw_v = ada_w.rearrange("(c p) n -> p c n", p=128)
            for n in range(6):
                b = n % 2
                S.dma("pool", awb[b][:], aw_v[:, :, n * 512:(n + 1) * 512], writes=[rawb[b]])
                for c in range(8):
                    S.op("pe", "matmul", dict(out=bk(n % 2)[0:1, :], lhsT=scol[:, c:c + 1], rhs=awb[b][:, c, :],
                                                              start=(c == 0), stop=(c == 7)),
                         reads=[rc, rawb[b]], writes=[rbank[n % 2]])
                S.op("dve", "tensor_tensor", dict(out=modrow[:, n * 512:(n + 1) * 512], in0=bk(n % 2)[0:1, :],
                                                        in1=adab[:, n * 512:(n + 1) * 512], op=ALU.add),
                     reads=[rbank[n % 2], rc], writes=[rrow])
            for j in range(16):
                S.op("pe", "matmul", dict(out=bk(2)[:, j:j + 1], lhsT=modrow[:, j * 128:(j + 1) * 128], rhs=ones_f[0:1, 0:1],
                                                start=True, stop=True), reads=[rrow, rconst], writes=[rbank[2]])
            S.op("dve", "tensor_copy", dict(out=shf[:], in_=bk(2)[:, 0:8]), reads=[rbank[2]], writes=[rmod])
            S.op("dve", "scalar_tensor_tensor", dict(out=scl1[:], in0=bk(2)[:, 8:16], scalar=1.0, in1=gcol[:],
                                                         op0=ALU.add, op1=ALU.mult), reads=[rbank[2], rc], writes=[rmod])
            for hh in range(2):
                S.op("pe", "matmul", dict(out=bk(3)[:, :], lhsT=ones_f[0:1, :], rhs=modrow[:, 2048 + hh * 512:2048 + (hh + 1) * 512],
                                                  start=True, stop=True), reads=[rrow, rconst], writes=[rbank[3]])
                S.op("dve", "tensor_copy", dict(out=gate_bc[:, hh * 512:(hh + 1) * 512], in_=bk(3)[:, :]),
                     reads=[rbank[3]], writes=[rmod])
            xpre = [sb(st, f"xpre{i}", [128, D]) for i in range(3)]; rxpre = [Res(f"xpre{i}") for i in range(3)]
            jpre = sb(st, "jpre", [128, D], BF16)
            for t in range(NT):
                b3 = t % 3
                S.dma("sp", xpre[b3][:], xs[t * 128:(t + 1) * 128, :], writes=[rxpre[b3]])
                S.op("act", "activation", dict(out=jpre[:], in_=xpre[b3][:], func=AF.Square, accum_out=rstd_all[:, t:t + 1]),
                     reads=[rxpre[b3]], writes=[rrstd])
            S.op("dve", "tensor_scalar", dict(out=rstd_all[:], in0=rstd_all[:], scalar1=1.0 / D, scalar2=1e-6, op0=ALU.mult, op1=ALU.add),
                 reads=[rrstd], writes=[rrstd])
            S.op("act", "activation", dict(out=rstd_all[:], in_=rstd_all[:], func=AF.Sqrt), reads=[rrstd], writes=[rrstd])
            S.op("dve", "reciprocal", dict(out=rstd_all[:], in_=rstd_all[:]), reads=[rrstd], writes=[rrstd])
            dbg("scl1", scl1[:], [128, 8], reads=[rmod]); dbg("shf", shf[:], [128, 8], reads=[rmod])
            dbg("gate", gate_bc[0:1, :], [1, D], reads=[rmod])
            S.flush()
            checkpoint()

        def rope_tables(st, half, name):
            cosT = sb(st, name + "cos", [128, NT, half]); sinT = sb(st, name + "sin", [128, NT, half])
            with contextlib.ExitStack() as t2:
                invb = sb(t2, name + "inv", [128, half]); ang = sb(t2, name + "ang", [128, NT, half])
                ki = sb(t2, name + "ki", [128, NT, half], I32); kf = sb(t2, name + "kf", [128, NT, half])
                rr = Res(name + "tmp"); rt = Res(name + "tab")
                S.dma("sp", invb[:], (inv32_d if half == 32 else inv16_d).partition_broadcast(128), writes=[rr])
                S.op("dve", "tensor_tensor", dict(out=ang[:], in0=posf[:].unsqueeze(2).to_broadcast([128, NT, half]),
                                                      in1=invb[:].unsqueeze(1).to_broadcast([128, NT, half]), op=ALU.mult),
                     reads=[rr, rconst], writes=[rr])
                for tab, off in ((sinT, 0.0), (cosT, PI / 2)):
                    S.op("dve", "tensor_scalar", dict(out=ki[:], in0=ang[:], scalar1=off, scalar2=1.0 / (2 * PI),
                                                                 op0=ALU.add, op1=ALU.mult), reads=[rr], writes=[rr])
                    S.op("dve", "tensor_copy", dict(out=kf[:], in_=ki[:]), reads=[rr], writes=[rr])
                    S.op("dve", "scalar_tensor_tensor", dict(out=kf[:], in0=kf[:], scalar=-2 * PI, in1=ang[:],
                                                                 op0=ALU.mult, op1=ALU.add), reads=[rr], writes=[rr])
                    S.op("dve", "tensor_scalar", dict(out=kf[:], in0=kf[:], scalar1=off, scalar2=PI,
                                                                 op0=ALU.add, op1=ALU.min), reads=[rr], writes=[rr])
                    S.op("dve", "tensor_scalar", dict(out=kf[:], in0=kf[:], scalar1=-PI, scalar2=None, op0=ALU.max),
                         reads=[rr], writes=[rr])
                    S.op("act", "activation", dict(out=tab[:], in_=kf[:], func=AF.Sin), reads=[rr], writes=[rr, rt])
                S.flush()
                checkpoint()
            return cosT, sinT, rt

        def rope_tm(src, dst, nh, half, cosv, sinv, tmp, reads, writes):
            n = nh * half
            npart = src.shape[0]
            cb = cosv.unsqueeze(1).to_broadcast([npart, nh, half]); sbv = sinv.unsqueeze(1).to_broadcast([npart, nh, half])
            x1 = src[:, :, 0:half]; x2 = src[:, :, half:2 * half]
            t1 = tmp[:, 0:n].rearrange("p (h d) -> p h d", h=nh); t2 = tmp[:, n:2 * n].rearrange("p (h d) -> p h d", h=nh)
            S.op("dve", "tensor_tensor", dict(out=t1, in0=x1, in1=cb, op=ALU.mult), reads=reads, writes=[rtmp_rope])
            S.op("dve", "tensor_tensor", dict(out=t2, in0=x2, in1=sbv, op=ALU.mult), reads=reads, writes=[rtmp_rope])
            S.op("dve", "tensor_tensor", dict(out=dst[:, :, 0:half], in0=t1, in1=t2, op=ALU.subtract),
                 reads=[rtmp_rope], writes=writes)
            S.op("dve", "tensor_tensor", dict(out=t1, in0=x2, in1=cb, op=ALU.mult), reads=reads, writes=[rtmp_rope])
            S.op("dve", "tensor_tensor", dict(out=t2, in0=x1, in1=sbv, op=ALU.mult), reads=reads, writes=[rtmp_rope])
            S.op("dve", "tensor_tensor", dict(out=dst[:, :, half:2 * half], in0=t1, in1=t2, op=ALU.add),
                 reads=[rtmp_rope], writes=writes)

        rtmp_rope = Res("ropetmp")

        def make_hT_a(t, xt, xn, rxt, rxn):
            b = t % 2
            S.dma("sp", xt[b][:], xs[t * 128:(t + 1) * 128, :], writes=[rxt[b]])
            S.op("act", "activation", dict(out=xn[b][:], in_=xt[b][:], func=AF.Copy, scale=rstd_all[:, t:t + 1]),
                 reads=[rxt[b], rrstd], writes=[rxn[b]])

        def make_hT_b(t, xn, hT, rxn, rhT, n_dve=4):
            b = t % 2
            hb = (t // 4) % 2
            for c in range(8):
                tb = 5 + c // 4
                S.op("pe", "transpose", dict(out=bk16(tb)[:, (c % 4) * 128:(c % 4 + 1) * 128], in_=xn[b][:, c * 128:(c + 1) * 128],
                                             identity=identb[:]), reads=[rxn[b], rconst], writes=[rbank[tb]])
            col = (t % 4) * 128
            for c in range(8):
                tb = 5 + c // 4
                src = bk16(tb)[:, (c % 4) * 128:(c % 4 + 1) * 128]
                if c < n_dve:
                    S.op("dve", "tensor_scalar", dict(out=hT[hb][:, c, col:col + 128], in0=src,
                                                      scalar1=scl1[:, c:c + 1], scalar2=shf[:, c:c + 1], op0=ALU.mult, op1=ALU.add),
                         reads=[rbank[tb], rmod], writes=[rhT[hb]])
                else:
                    S.op("act", "activation", dict(out=hT[hb][:, c, col:col + 128], in_=src,
                                                   func=AF.Identity, bias=shf[:, c:c + 1], scale=scl1[:, c:c + 1]),
                         reads=[rbank[tb], rmod], writes=[rhT[hb]])

        def make_hT(t, xt, xn, hT, rxt, rxn, rhT):
            make_hT_a(t, xt, xn, rxt, rxn)
            make_hT_b(t, xn, hT, rxn, rhT)

        junk_s = sb(gst, "junk_s", [128, 8]); rjunk = Res("junk")
        rvalid_dummy = None

        def idle(n):
            for _ in range(n):
                yield

        def run_staggered(gens, lag, max_active=2, gate_key=None):
            pending = list(enumerate(gens))
            active = []
            while pending or active:
                if pending and len(active) < max_active and (not active or active[-1]["steps"] >= lag):
                    k, gen = pending.pop(0)
                    active.append({"k": k, "gen": gen, "steps": 0, "parked": False, "main": False})
                for ent in list(active):
                    if ent["parked"]:
                        key = gate_key(ent["k"])
                        if any(o["main"] and gate_key(o["k"]) == key for o in active if o is not ent):
                            continue
                        ent["parked"] = False
                        ent["main"] = True
                    try:
                        r = next(ent["gen"])
                        ent["steps"] += 1
                        if r == "GATE":
                            ent["parked"] = True
                        elif r == "RELEASE":
                            ent["main"] = False
                    except StopIteration:
                        active.remove(ent)

        def own_index(t):
            blk, r = divmod(t, 4)
            if blk % 2 == 1:
                return (blk // 2) * 4 + r
            return None

        with contextlib.ExitStack() as nsa:
            QT = sb(nsa, "QT", [128, 8, NOWN * 128], BF16)
            KE = sb(nsa, "KE", [128, 2, S_LEN], BF16)
            KW = sb(nsa, "KW", [64, 2, S_LEN], BF16)
            VS = sb(nsa, "VS", [128, NT, 2, 65], BF16); VW = sb(nsa, "VW", [128, NT, 2, 65], BF16)
            GS = sb(nsa, "GS", [128, NOWN, 24])
            mixBf = mixB[:].rearrange("p a b -> p (a b)")
            kcmpT = mixBf[:, 0:S_LEN]; vcmpT = mixBf[:, S_LEN:2 * S_LEN]
            rQT = [Res(f"QT{i}") for i in range(NOWN)]; rQS = [[Res(f"QS{i}_{g}") for g in range(2)] for i in range(NOWN)]
            rKE = Res("KE"); rKW = Res("KW"); rVS = Res("VS"); rVW = Res("VW"); rGS = Res("GS")
            for g in range(2):
                S.dma("pool", KE[64:128, g, :], E_d[:, :], writes=[rKE])
            for V, rV in ((VS, rVS), (VW, rVW)):
                S.op("dve", "tensor_copy", dict(out=V[:, :, :, 64], in_=valid[:].unsqueeze(2).to_broadcast([128, NT, 2])),
                     reads=[rconst], writes=[rV])

            W1k = sb(nsa, "W1k", [128, 32, 128], BF16)
            W2 = [sb(nsa, f"W2{j}", [128, 64], BF16) for j in range(2)]
            posT = sb(nsa, "posT", [128, 32], BF16)
            rW = Res("W1", multi=True)
            with contextlib.ExitStack() as st:
                cos32, sin32, rtab = rope_tables(st, 32, "r32")
                wn = sb(st, "wn", [128, 8, NSA_COLS], BF16); rwn = Res("wn", multi=True)
                w_v = w_in.rearrange("(c p) n -> p c n", p=128)
                W_KC, W_VC, W_KS, W_KW, W_VS, W_VW, W_GL, W_ZN = 512, 640, 768, 896, 1024, 1152, 1280, 1304
                segs = ((0, O_Q, 512), (W_KC, O_KC, 128), (W_VC, O_VC, 128), (W_KS, O_KS, 128), (W_KW, O_KW, 128),
                        (W_VS, O_VS, 128), (W_VW, O_VW, 128), (W_GL, O_GL, 24), (W_ZN, O_ZN, 512))
                for (d0, s0, n_) in segs:
                    S.dma("pool", wn[:, :, d0:d0 + n_], w_v[:, :, s0:s0 + n_], writes=[rwn])
                vk_ = ckw1.rearrange("(l d) j -> d l j", d=64)
                S.dma("pool", W1k[0:64, :, :], vk_, writes=[rW]); S.dma("pool", W1k[64:128, :, :], vk_, writes=[rW])
                S.dma("pool", W2[0][:], ckw2[:, :], writes=[rW]); S.dma("pool", W2[1][:], cvw2[:, :], writes=[rW])
                for hh in range(2):
                    S.dma("pool", posT[hh * 64:(hh + 1) * 64, :], posT_d[:, :], writes=[rW])
                xt = [sb(st, f"xt{i}", [128, D]) for i in range(2)]; rxt = [Res("xt0"), Res("xt1")]
                xn = [sb(st, f"xn{i}", [128, D], BF16) for i in range(2)]; rxn = [Res("xn0"), Res("xn1")]
                hT = [sb(st, f"hT{i}", [128, 8, 512], BF16) for i in range(2)]; rhT = [Res("hT0"), Res("hT1")]
                rtmp = sb(st, "ropetmp", [128, 1024]); ktm = sb(st, "ktm", [128, 256], BF16); rktm = Res("ktm")
                qtm = sb(st, "qtm", [128, 512], BF16); rqtm = Res("qtm")
                for r in range(4):
                    make_hT(r, xt, xn, hT, rxt, rxn, rhT)
                for blk in range(8):
                    hb = blk % 2
                    for j, (off, dstT) in enumerate(((W_KC, kcmpT), (W_VC, vcmpT))):
                        for c in range(8):
                            S.op("pe", "matmul", dict(out=bk(j)[:, :], lhsT=wn[:, c, off:off + 128], rhs=hT[hb][:, c, :],
                                                      start=(c == 0), stop=(c == 7)), reads=[rwn, rhT[hb]], writes=[rbank[j]])
                        S.op("act", "activation", dict(out=dstT[:, blk * 512:(blk + 1) * 512], in_=bk(j)[:, :], func=AF.Copy),
                             reads=[rbank[j]], writes=[rmixB])
                    for r in range(4):
                        t = blk * 4 + r
                        if blk + 1 < 8:
                            make_hT_a((blk + 1) * 4 + r, xt, xn, rxt, rxn)
                        for c in range(8):
                            S.op("pe", "matmul", dict(out=bk(2)[:, :], lhsT=hT[hb][:, c, r * 128:(r + 1) * 128], rhs=wn[:, c, W_KS:W_KS + 512],
                                                      start=(c == 0), stop=(c == 7)), reads=[rwn, rhT[hb]], writes=[rbank[2]])
                        ps = bk(2)
                        i = own_index(t)
                        if i is not None:
                            for c in range(8):
                                S.op("pe", "matmul", dict(out=bk(3)[:, :], lhsT=hT[hb][:, c, r * 128:(r + 1) * 128], rhs=wn[:, c, 0:512],
                                                          start=(c == 0), stop=(c == 7)), reads=[rwn, rhT[hb]], writes=[rbank[3]])
                            for c in range(8):
                                S.op("pe", "matmul", dict(out=bk(4)[:, :], lhsT=hT[hb][:, c, r * 128:(r + 1) * 128], rhs=wn[:, c, W_ZN:W_ZN + 512],
                                                          start=(c == 0), stop=(c == 7)), reads=[rwn, rhT[hb]], writes=[rbank[4]])
                            for c in range(8):
                                S.op("pe", "matmul", dict(out=bk(1)[:, 0:24], lhsT=hT[hb][:, c, r * 128:(r + 1) * 128], rhs=wn[:, c, W_GL:W_GL + 24],
                                                          start=(c == 0), stop=(c == 7)), reads=[rwn, rhT[hb]], writes=[rbank[1]])
                        for (o2, V, rV) in ((256, VS, rVS), (384, VW, rVW)):
                            src = ps[:, o2:o2 + 128].rearrange("p (g d) -> p g d", g=2)
                            if t < 4:
                                S.op("dve", "tensor_scalar", dict(out=V[:, t, :, 0:64], in0=src, scalar1=valid[:, t:t + 1], scalar2=None, op0=ALU.mult),
                                     reads=[rbank[2], rconst], writes=[rV])
                            else:
                                S.op("dve", "tensor_copy", dict(out=V[:, t, :, 0:64], in_=src), reads=[rbank[2]], writes=[rV])
                        rope_tm(ps[:, 0:256].rearrange("p (g d) -> p g d", g=4), ktm[:, :].rearrange("p (g d) -> p g d", g=4),
                                4, 32, cos32[:, t, :], sin32[:, t, :], rtmp, [rbank[2], rtab], [rktm])
                        if i is not None:
                            rope_tm(bk(3)[:, :].rearrange("p (h d) -> p h d", h=8), qtm[:].rearrange("p (h d) -> p h d", h=8),
                                    8, 32, cos32[:, t, :], sin32[:, t, :], rtmp, [rbank[3], rtab], [rqtm])
                            S.op("act", "activation", dict(out=mixA[:, i, :], in_=bk(4)[:, :], func=AF.Silu), reads=[rbank[4]], writes=[rmixA[i]])
                            S.op("act", "activation", dict(out=GS[:, i, :], in_=bk(1)[:, 0:24], func=AF.Sigmoid), reads=[rbank[1]], writes=[rGS])
                        if blk + 1 < 8:
                            make_hT_b((blk + 1) * 4 + r, xn, hT, rxn, rhT, n_dve=0)
                        for idx in range(2):
                            S.op("pe", "transpose", dict(out=bk16(7)[:, idx * 128:(idx + 1) * 128], in_=ktm[:, idx * 128:(idx + 1) * 128], identity=identb[:]),
                                 reads=[rktm, rconst], writes=[rbank[7]])
                        for idx, (KT, rK) in enumerate(((KE, rKE), (KW, rKW))):
                            for g in range(2):
                                S.op("dve", "tensor_copy", dict(out=KT[0:64, g, t * 128:(t + 1) * 128],
                                                                in_=bk16(7)[g * 64:(g + 1) * 64, idx * 128:(idx + 1) * 128]),
                                     reads=[rbank[7]], writes=[rK])
                        if i is None:
                            continue
                        for pr in range(4):
                            S.op("pe", "transpose", dict(out=bk16(3)[:, pr * 128:(pr + 1) * 128], in_=qtm[:, pr * 128:(pr + 1) * 128],
                                                         identity=identb[:]), reads=[rqtm, rconst], writes=[rbank[3]])
                        qsrc = bk16(3)[:, 0:512].rearrange("p (a q) -> p a q", a=4)
                        for hf in range(2):
                            S.op("dve", "tensor_copy", dict(out=QT[0:64, hf:8:2, i * 128:(i + 1) * 128], in_=qsrc[hf * 64:(hf + 1) * 64, :, :]),
                                 reads=[rbank[3]], writes=[rQT[i]])
                dbg("QT", QT[0:64, :, :], [64, 8, NOWN * 128], BF16, reads=rQT)
                dbg("KE", KE[:, :, :], [128, 2, S_LEN], BF16, reads=[rKE])
                dbg("KW", KW[:, :, :], [64, 2, S_LEN], BF16, reads=[rKW])
                dbg("VS", VS[:, :, :, :], [128, NT, 2, 65], BF16, reads=[rVS])
                dbg("kcmpT", kcmpT, [128, S_LEN], BF16, reads=[rmixB])
                dbg("GS", GS[:, :, :], [128, NOWN, 24], reads=[rGS])
                S.flush()
                checkpoint()

            with contextlib.ExitStack() as st:
                W1 = [W1k, sb(st, "W1v", [128, 32, 128], BF16)]
                cst = sb(st, "cst", [128, 2])
                hid = sb(st, "hid", [128, 256], BF16); ctmp = sb(st, "ctmp", [128, 64])
                kcT = sb(st, "kcT", [64, 2, 256], BF16)
                RC = sb(st, "RC", [128, 2, 2, 128], BF16)
                m1 = sb(st, "m1", [128, 2, 64]); vcv = sb(st, "vcv", [128, 2]); cposi = sb(st, "cposi", [128, 2], I32)
                cposf = sb(st, "cposf", [128, 2]); kctm = sb(st, "kctm", [128, 64], BF16)
                rcst = Res("cst"); rhid = Res("hid"); rkcT = Res("kcT"); rRC = Res("RC"); rk2 = Res("kctm")
                vv_ = cvw1.rearrange("(l d) j -> d l j", d=64)
                S.dma("pool", W1[1][0:64, :, :], vv_, writes=[rW]); S.dma("pool", W1[1][64:128, :, :], vv_, writes=[rW])
                S.dma("sp", m1[:, 0, :], m1_d[0:128, :], writes=[rW]); S.dma("sp", m1[:, 1, :], m1_d[128:256, :], writes=[rW])
                v16 = valid_d.rearrange("(n s) -> n s", s=16); p16 = pos_i.rearrange("(n s) -> n s", s=16)
                S.dma("sp", vcv[:, :], validc_pt[:, :], writes=[rW])
                S.dma("sp", cposi[:, :], cpos_pt[:, :], writes=[rW])
                S.op("dve", "tensor_copy", dict(out=cposf[:], in_=cposi[:]), reads=[rW], writes=[rW])
                ccos = sb(st, "ccos", [128, 2, 32]); csin = sb(st, "csin", [128, 2, 32])
                cinv = sb(st, "cinv", [128, 32]); cang = sb(st, "cang", [128, 2, 32]); cki = sb(st, "cki", [128, 2, 32], I32)
                ckf = sb(st, "ckf", [128, 2, 32])
                S.dma("sp", cinv[:], inv32_d.partition_broadcast(128), writes=[rW])
                S.op("dve", "tensor_tensor", dict(out=cang[:], in0=cposf[:].unsqueeze(2).to_broadcast([128, 2, 32]),
                                                      in1=cinv[:].unsqueeze(1).to_broadcast([128, 2, 32]), op=ALU.mult), reads=[rW], writes=[rW])
                for tab, off in ((csin, 0.0), (ccos, PI / 2)):
                    S.op("dve", "tensor_scalar", dict(out=cki[:], in0=cang[:], scalar1=off, scalar2=1.0 / (2 * PI), op0=ALU.add, op1=ALU.mult),
                         reads=[rW], writes=[rW])
                    S.op("dve", "tensor_copy", dict(out=ckf[:], in_=cki[:]), reads=[rW], writes=[rW])
                    S.op("dve", "scalar_tensor_tensor", dict(out=ckf[:], in0=ckf[:], scalar=-2 * PI, in1=cang[:], op0=ALU.mult, op1=ALU.add),
                         reads=[rW], writes=[rW])
                    S.op("dve", "tensor_scalar", dict(out=ckf[:], in0=ckf[:], scalar1=off, scalar2=PI, op0=ALU.add, op1=ALU.min),
                         reads=[rW], writes=[rW])
                    S.op("dve", "tensor_scalar", dict(out=ckf[:], in0=ckf[:], scalar1=-PI, scalar2=None, op0=ALU.max), reads=[rW], writes=[rW])
                    S.op("act", "activation", dict(out=tab[:], in_=ckf[:], func=AF.Sin), reads=[rW], writes=[rW])
                for j in range(2):
                    for l in range(32):
                        S.op("pe", "matmul", dict(out=bk(0)[:, j:j + 1], lhsT=W1[j][0:64, l, :], rhs=posT[0:64, l:l + 1],
                                                             start=(l == 0), stop=(l == 31)), reads=[rW], writes=[rbank[0]])
                S.op("dve", "tensor_copy", dict(out=cst[:], in_=bk(0)[:, 0:2]), reads=[rbank[0]], writes=[rcst])
                S.op("dve", "memset", dict(ap=RC[:], constant=0.0), writes=[rRC])
                S.op("dve", "memset", dict(ap=kcT[:], constant=0.0), writes=[rkcT])
                cmpD = sb(st, "cmpD", [128, 2, 16, 256], BF16); rcmpD = Res("cmpD")
                for j, srcT in enumerate((kcmpT, vcmpT)):
                    S.op("dve", "tensor_copy", dict(out=cmpD[:, j, :, :], in_=srcT.rearrange("p (n s) -> p s n", s=16)), reads=[rmixB], writes=[rcmpD])
                HB = {(0, 0): 1, (0, 1): 3, (1, 0): 4, (1, 1): 5}
                hid2 = [hid, sb(st, "hid_b", [128, 256], BF16)]; rhid2 = [rhid, Res("hid_b")]
                for g in range(2):
                    rows = slice(g * 64, (g + 1) * 64)
                    for j in range(2):
                        hbk = HB[(g, j)]
                        for l in range(32):
                            S.op("pe", "matmul", dict(out=bk(hbk)[:, 0:255], lhsT=W1[j][rows, l, :],
                                                      rhs=cmpD[rows, j, l % 16, l // 16:l // 16 + 255],
                                                      start=(l == 0), stop=(l == 31)),
                                 reads=[rW, rcmpD], writes=[rbank[hbk]])
                for g in range(2):
                    for j in range(2):
                        hbk = HB[(g, j)]
                        hx = (2 * g + j) % 2
                        S.op("act", "activation", dict(out=hid2[hx][:, 0:255], in_=bk(hbk)[:, 0:255], func=AF.Silu, bias=cst[:, j:j + 1]),
                             reads=[rbank[hbk], rcst], writes=[rhid2[hx]])
                        for ch in range(2):
                            nn = 128 if ch == 0 else 127
                            ob = 2 if ch == 0 else 6
                            S.op("pe", "matmul", dict(out=bk(ob)[0:nn, 0:64], lhsT=hid2[hx][:, ch * 128:ch * 128 + nn], rhs=W2[j][:, :],
                                                      start=True, stop=True), reads=[rhid2[hx], rW], writes=[rbank[ob]])
                            if j == 0:
                                rope_tm(bk(ob)[0:nn, 0:64].rearrange("p (h d) -> p h d", h=1), kctm[0:nn, :].rearrange("p (h d) -> p h d", h=1),
                                        1, 32, ccos[0:nn, ch, :], csin[0:nn, ch, :], ctmp[0:nn, :], [rbank[ob], rW], [rk2])
                                S.op("pe", "transpose", dict(out=bk16(7)[0:64, 0:nn], in_=kctm[0:nn, :], identity=identb[0:nn, 0:nn]),
                                     reads=[rk2, rconst], writes=[rbank[7]])
                                S.op("dve", "tensor_copy", dict(out=kcT[:, g, ch * 128:ch * 128 + nn], in_=bk16(7)[0:64, 0:nn]),
                                     reads=[rbank[7]], writes=[rkcT])
                            else:
                                S.op("dve", "tensor_scalar", dict(out=RC[0:nn, ch, g, 0:64], in0=bk(ob)[0:nn, 0:64],
                                                                  scalar1=vcv[0:nn, ch:ch + 1], scalar2=None, op0=ALU.mult),
                                     reads=[rbank[ob], rW], writes=[rRC])
                for g in range(2):
                    for ch in range(2):
                        S.op("dve", "tensor_copy", dict(out=RC[:, ch, g, 64:65], in_=vcv[:, ch:ch + 1]), reads=[rW], writes=[rRC])
                        S.op("dve", "tensor_scalar", dict(out=RC[:, ch, g, 65:128], in0=m1[:, ch, 0:63], scalar1=vcv[:, ch:ch + 1],
                                                                       scalar2=None, op0=ALU.mult), reads=[rW], writes=[rRC])
                dbg("kcT", kcT[:, :, :], [64, 2, 256], BF16, reads=[rkcT])
                dbg("RC", RC[:, :, :, :], [128, 2, 2, 128], BF16, reads=[rRC])
                S.flush()
                checkpoint()

                cmask = sb(st, "cmask", [128, 2, NOWN * 128], BF16)
                iforce = sb(st, "iforce", [128, NOWN, 64])
                rcm = Res("cmask", multi=True)
                for ch in range(2):
                    S.dma("pool", cmask[:, ch, :], cmpmask_d[ch * 128:(ch + 1) * 128, :], writes=[rcm])
                S.dma("sp", iforce[:], impforce_d.rearrange("(i p) j -> p i j", p=128), writes=[rcm])
                Pt = [sb(st, f"Pt{i}", [128, 512], BF16) for i in range(6)]; rPt = [Res(f"Pt{i}") for i in range(6)]
                accS = [sb(st, f"accS{i}", [65, 512]) for i in range(2)]; raccS = [Res("accS0"), Res("accS1")]
                NPS = 4
                cmpS = [sb(st, f"cmpS{i}", [128, 4, 128]) for i in range(NPS)]; rcmpS = [Res(f"cmpS{i}") for i in range(NPS)]
                sm = [sb(st, f"sm{i}", [128, 16]) for i in range(NPS)]; rsm = [Res(f"sm{i}") for i in range(NPS)]
                imp = [sb(st, f"imp{i}", [128, 64]) for i in range(NPS)]; imp2 = [sb(st, f"imp2{i}", [128, 64]) for i in range(NPS)]
                imp3 = [sb(st, f"imp3{i}", [128, 64]) for i in range(NPS)]
                m16 = [sb(st, f"m16{i}", [128, 16]) for i in range(NPS)]
                selb = [sb(st, f"selb{i}", [128, 64], BF16) for i in range(NPS)]; rimp = [Res(f"imp{i}") for i in range(NPS)]
                ocmp = [sb(st, f"ocmp{i}", [128, 4, 64]) for i in range(NPS)]; rocmp = [Res(f"ocmp{i}") for i in range(NPS)]
                oacc = [sb(st, f"oacc{i}", [128, 4, 64]) for i in range(NPS)]; roacc = [Res(f"oacc{i}") for i in range(NPS)]
                wts = [sb(st, f"wts{i}", [128, 3, 4]) for i in range(NPS)]
                Pp = [sb(st, f"Pp{i}", [128, 512], BF16) for i in range(2)]; rPp = [Res("Pp0"), Res("Pp1")]
                for x_ in range(NPS):
                    S.op("dve", "memset", dict(ap=imp[x_][:], constant=0.0), writes=[rimp[x_]])

                def attn_units(units, sx, accb, group_starts=(0,), before_pv=None):
                    n = len(units)
                    before_pv = before_pv or {}

                    def emit_score(u):
                        b = 2 * sx + (u % 2)
                        S.op("pe", "matmul", dict(out=bk(b)[:, :], lhsT=units[u][0], rhs=units[u][1], start=True, stop=True),
                             reads=units[u][2], writes=[rbank[b]])
                    emit_score(0)
                    if n > 1:
                        emit_score(1)
                    for u in range(n):
                        _, _, _, maskt, scale, vl, vrd = units[u]
                        b = 2 * sx + (u % 2)
                        p = 3 * sx + (u % 3)
                        S.op("act", "activation", dict(out=Pt[p][:, :], in_=bk(b)[:, :], func=AF.Exp, scale=scale),
                             reads=[rbank[b]], writes=[rPt[p]])
                        if maskt is not None:
                            pv = Pt[p][:, :].rearrange("p (h q) -> p h q", h=4)
                            S.op("dve", "tensor_tensor", dict(out=pv, in0=pv, in1=maskt[:].unsqueeze(1).to_broadcast([128, 4, 128]), op=ALU.mult),
                                 reads=[rPt[p], rconst], writes=[rPt[p]])
                        if u + 2 < n:
                            emit_score(u + 2)
                        if u in before_pv:
                            before_pv[u]()
                        S.op("pe", "matmul", dict(out=bk(accb)[0:65, :], lhsT=vl, rhs=Pt[p][:, :],
                                                  start=(u in group_starts), stop=(u + 1 in group_starts or u == n - 1)),
                             reads=[rPt[p]] + vrd, writes=[rbank[accb]])
                        yield

                def finalize_T(accb, si):
                    S.op("dve", "tensor_copy", dict(out=accS[si][:, :], in_=bk(accb)[0:65, :]), reads=[rbank[accb]], writes=[raccS[si]])
                    for h in range(4):
                        S.op("pe", "transpose", dict(out=bk(6)[:, h * 65:(h + 1) * 65], in_=accS[si][:, h * 128:(h + 1) * 128],
                                                     identity=ident[0:65, 0:65]), reads=[raccS[si], rconst], writes=[rbank[6]])

                b7lock = {"owner": None}

                def nsa_stream(k, i, g):
                    sx = k % 2
                    ps = k % NPS
                    qt = 8 * (i // 4) + 4 + (i % 4)
                    qs = slice(i * 128, (i + 1) * 128)
                    hs = slice(4 * g, 4 * g + 4)
                    accb = 4 + sx
                    SM, IMP, IMP2, IMP3, M16, SELB = sm[ps], imp[ps], imp2[ps], imp3[ps], m16[ps], selb[ps]
                    OC, OA, WT, CS = ocmp[ps], oacc[ps], wts[ps], cmpS[ps]
                    rSM, rIMP, rOC, rOA, rCS = rsm[ps], rimp[ps], rocmp[ps], roacc[ps], rcmpS[ps]
                    nch = 2 if 8 * qt + 6 >= 128 else 1
                    while b7lock["owner"] not in (None, k):
                        yield
                    b7lock["owner"] = k
                    for ch in range(nch):
                        S.op("pe", "matmul", dict(out=bk(7)[:, :], lhsT=kcT[:, g, ch * 128:(ch + 1) * 128], rhs=QT[0:64, hs, qs],
                                                  start=True, stop=True), reads=[rkcT, rQT[i]], writes=[rbank[7]])
                        yield from idle(3)
                        S.op("act", "activation", dict(out=Pp[ch][:, :], in_=bk(7)[:, :], func=AF.Exp, scale=0.125),
                             reads=[rbank[7]], writes=[rPp[ch]])
                        yield from idle(3)
                        pv = Pp[ch][:, :].rearrange("p (h q) -> p h q", h=4)
                        S.op("dve", "tensor_tensor", dict(out=pv, in0=pv, in1=cmask[:, ch, qs].unsqueeze(1).to_broadcast([128, 4, 128]), op=ALU.mult),
                             reads=[rPp[ch], rcm], writes=[rPp[ch]])
                        yield
                    yield from idle(2)
                    for h in range(4):
                        for ch in range(nch):
                            S.op("pe", "matmul", dict(out=bk(7)[:, h * 128:(h + 1) * 128], lhsT=Pp[ch][:, h * 128:(h + 1) * 128], rhs=RC[:, ch, g, :],
                                                      start=(ch == 0), stop=(ch == nch - 1)), reads=[rPp[ch], rRC], writes=[rbank[7]])
                    yield from idle(3)
                    S.op("dve", "tensor_copy", dict(out=CS[:, :, :], in_=bk(7)[:, :].rearrange("p (h c) -> p h c", h=4)), reads=[rbank[7]], writes=[rCS])
                    b7lock["owner"] = None
                    yield
                    S.op("dve", "tensor_scalar", dict(out=SM[:, 0:4], in0=CS[:, :, 64], scalar1=1e-30, scalar2=None, op0=ALU.max), reads=[rCS], writes=[rSM])
                    S.op("dve", "reciprocal", dict(out=SM[:, 4:8], in_=SM[:, 0:4]), reads=[rSM], writes=[rSM])
                    glv0 = GS[:, i, 12 * g:12 * g + 12].rearrange("p (h b) -> p b h", b=3)[:, 0, :]
                    S.op("dve", "tensor_tensor", dict(out=WT[:, 0, :], in0=glv0, in1=SM[:, 4:8], op=ALU.mult), reads=[rGS, rSM], writes=[rSM])
                    S.op("dve", "tensor_tensor", dict(out=OA[:], in0=CS[:, :, 0:64], in1=WT[:, 0, :].unsqueeze(2).to_broadcast([128, 4, 64]), op=ALU.mult),
                         reads=[rCS, rSM], writes=[rOA])
                    yield
                    for h in range(4):
                        if h == 0:
                            S.op("dve", "tensor_scalar", dict(out=IMP[:, 0:63], in0=CS[:, h, 65:128], scalar1=SM[:, 4:5], scalar2=None, op0=ALU.mult),
                                 reads=[rCS, rSM], writes=[rIMP])
                        else:
                            S.op("dve", "scalar_tensor_tensor", dict(out=IMP[:, 0:63], in0=CS[:, h, 65:128], scalar=SM[:, 4 + h:5 + h], in1=IMP[:, 0:63],
                                                                     op0=ALU.mult, op1=ALU.add), reads=[rCS, rSM, rIMP], writes=[rIMP])
                    yield
                    S.op("dve", "tensor_tensor", dict(out=IMP2[:], in0=IMP[:], in1=iforce[:, i, :], op=ALU.max), reads=[rIMP, rcm], writes=[rIMP])
                    S.op("dve", "max", dict(out=M16[:, 0:8], in_=IMP2[:]), reads=[rIMP], writes=[rIMP])
                    S.op("dve", "match_replace", dict(out=IMP3[:], in_to_replace=M16[:, 0:8], in_values=IMP2[:], imm_value=-1e30),
                         reads=[rIMP], writes=[rIMP])
                    yield
                    S.op("dve", "max", dict(out=M16[:, 8:16], in_=IMP3[:]), reads=[rIMP], writes=[rIMP])
                    S.op("dve", "tensor_scalar", dict(out=SELB[:], in0=IMP2[:], scalar1=M16[:, 15:16], scalar2=-30000.0, op0=ALU.is_lt, op1=ALU.mult),
                         reads=[rIMP], writes=[rIMP])
                    yield from idle(6)
                    while b7lock["owner"] not in (None, k):
                        yield
                    S.op("pe", "transpose", dict(out=bk16(7)[0:64, 0:128], in_=SELB[:, :], identity=identb[:]), reads=[rIMP, rconst], writes=[rbank[7]])
                    S.op("dve", "tensor_copy", dict(out=QT[64:128, hs, qs], in_=bk16(7)[0:64, 0:128].unsqueeze(1).to_broadcast([64, 4, 128])),
                         reads=[rbank[7]], writes=[rQS[i][g]])
                    yield "GATE"
                    units = []
                    for kt in range(qt + 1):
                        ks = slice(kt * 128, (kt + 1) * 128)
                        units.append((KE[:, g, ks], QT[:, hs, qs], [rKE, rQT[i], rQS[i][g]],
                                      trile if kt == qt else None, 0.125, VS[:, kt, g, :], [rVS]))
                    n_slc = len(units)
                    glv = GS[:, i, 12 * g:12 * g + 12].rearrange("p (h b) -> p b h", b=3)

                    def slc_finalize_rest():
                        for h_ in range(4):
                            S.op("pe", "transpose", dict(out=bk(6)[:, h_ * 65:(h_ + 1) * 65], in_=accS[sx][:, h_ * 128:(h_ + 1) * 128],
                                                         identity=ident[0:65, 0:65]), reads=[raccS[sx], rconst], writes=[rbank[6]])
                        S.op("dve", "reciprocal", dict(out=SM[:, 12:16], in_=bk(6)[:, 64:260:65]), reads=[rbank[6]], writes=[rSM])
                        S.op("dve", "tensor_tensor", dict(out=WT[:, 1, :], in0=glv[:, 1, :], in1=SM[:, 12:16], op=ALU.mult), reads=[rGS, rSM], writes=[rSM])
                        S.op("dve", "tensor_tensor", dict(out=OC[:], in0=bk(6)[:, 0:260].rearrange("p (h d) -> p h d", h=4)[:, :, 0:64],
                                                          in1=WT[:, 1, :].unsqueeze(2).to_broadcast([128, 4, 64]), op=ALU.mult),
                             reads=[rbank[6], rSM, rOA], writes=[rOC])
                        S.op("dve", "tensor_tensor", dict(out=OA[:], in0=OA[:], in1=OC[:], op=ALU.add), reads=[rOC, rOA], writes=[rOA])

                    for kt in range(qt - 4, qt + 1):
                        ks = slice(kt * 128, (kt + 1) * 128)
                        mk = trile if kt == qt else (trigt if kt == qt - 4 else None)
                        units.append((KW[:, g, ks], QT[0:64, hs, qs], [rKW, rQT[i]], mk, 0.125, VW[:, kt, g, :], [rVW]))

                    def copy_out():
                        S.op("dve", "tensor_copy", dict(out=accS[sx][:, :], in_=bk(accb)[0:65, :]), reads=[rbank[accb]], writes=[raccS[sx]])

                    for n_, _ in enumerate(attn_units(units, sx, accb, group_starts=(0, n_slc), before_pv={n_slc: copy_out})):
                        yield
                        if n_ == n_slc:
                            slc_finalize_rest()
                            yield
                    S.op("dve", "tensor_copy", dict(out=accS[sx][:, :], in_=bk(accb)[0:65, :]), reads=[rbank[accb]], writes=[raccS[sx]])
                    yield "RELEASE"
                    yield from idle(3)
                    for h_ in range(4):
                        S.op("pe", "transpose", dict(out=bk(6)[:, h_ * 65:(h_ + 1) * 65], in_=accS[sx][:, h_ * 128:(h_ + 1) * 128],
                                                     identity=ident[0:65, 0:65]), reads=[raccS[sx], rconst], writes=[rbank[6]])
                    S.op("dve", "reciprocal", dict(out=SM[:, 12:16], in_=bk(6)[:, 64:260:65]), reads=[rbank[6]], writes=[rSM])
                    S.op("dve", "tensor_tensor", dict(out=WT[:, 2, :], in0=glv[:, 2, :], in1=SM[:, 12:16], op=ALU.mult), reads=[rGS, rSM], writes=[rSM])
                    S.op("dve", "tensor_tensor", dict(out=OC[:], in0=bk(6)[:, 0:260].rearrange("p (h d) -> p h d", h=4)[:, :, 0:64],
                                                      in1=WT[:, 2, :].unsqueeze(2).to_broadcast([128, 4, 64]), op=ALU.mult),
                         reads=[rbank[6], rSM, rOA], writes=[rOC])
                    S.op("dve", "tensor_tensor", dict(out=OA[:], in0=OA[:], in1=OC[:], op=ALU.add), reads=[rOC, rOA], writes=[rOA])
                    mv = mixA[:, i, 256 * g:256 * g + 256].rearrange("p (h d) -> p h d", h=4)
                    S.op("dve", "tensor_tensor", dict(out=mv, in0=mv, in1=OA[:], op=ALU.mult), reads=[rOA, rmixA[i]], writes=[rmixA[i]])
                    yield

                order = [x for j in range(8) for g in range(2) for x in ((j, g), (j + 8, g))]
                run_staggered([nsa_stream(k, i, g) for k, (i, g) in enumerate(order)], 3, max_active=4, gate_key=lambda k: k % 2)
                dbg("mixA", mixA[:, :, :], [128, NOWN, 512], BF16, reads=rmixA)
                S.flush()
                checkpoint()

        wo = sb(gst, "wo", [128, 8, D], BF16); rwo = Res("wo", multi=True)
        wo_v = w_out.rearrange("(c p) n -> p c n", p=128)
        with contextlib.ExitStack() as st:
            cos16, sin16, rtab16 = rope_tables(st, 16, "r16")
            wm = sb(st, "wm", [128, 8, MLA_COLS], BF16); rwm = Res("wm", multi=True)
            w_v = w_in.rearrange("(c p) n -> p c n", p=128)
            for c in range(8):
                S.dma("pool", wm[:, c, :], w_v[:, c, NSA_COLS:IN_W], writes=[rwm])
            for c in range(8):
                S.dma("pool", wo[:, c, :], wo_v[:, c, :], writes=[rwo])
            cqnT = sb(st, "cqnT", [128, 2, NOWN * 128], BF16); ckvnT = sb(st, "ckvnT", [128, S_LEN], BF16)
            KH = [sb(st, f"KH{i}", [96, S_LEN], BF16) for i in range(2)]
            VH = [sb(st, f"VH{i}", [128, NT, 65], BF16) for i in range(2)]
            rcqn = Res("cqnT"); rckv = Res("ckvnT"); rKHrot = [Res("KHrot0"), Res("KHrot1")]; rKH = [Res("KH0"), Res("KH1")]
            rVH = [Res("VH0"), Res("VH1")]
            wq = sb(st, "wq", [128, 2, 768], BF16); wqp = sb(st, "wqp", [128, 2, 8, 96], BF16); wkv = sb(st, "wkv", [128, 1024], BF16)
            wqf = sb(st, "wqf", [128, 2, 768]); wkvf = sb(st, "wkvf", [128, 1024]); gq = sb(st, "gq", [128, 2]); gkv = sb(st, "gkv", [128, 1])
            rwq = Res("wq", multi=True)
            S.dma("sp", wqf[:], w_q_up.rearrange("(c p) n -> p c n", p=128), writes=[rwq])
            S.dma("sp", wkvf[:], w_kv_up[:, :], writes=[rwq])
            S.dma("sp", gq[:], gq_col[:, :], writes=[rwq])
            S.dma("sp", gkv[:], gkv_col[:, :], writes=[rwq])
            for c in range(2):
                S.op("dve", "tensor_scalar", dict(out=wq[:, c, :], in0=wqf[:, c, :], scalar1=gq[:, c:c + 1], scalar2=None, op0=ALU.mult),
                     reads=[rwq], writes=[rwq])
                wv4 = wq[:, c, :].rearrange("p (h d) -> p h d", h=8)
                S.op("dve", "tensor_copy", dict(out=wqp[:, c, :, 0:64], in_=wv4[:, :, 0:64]), reads=[rwq], writes=[rwq])
                S.op("dve", "tensor_copy", dict(out=wqp[:, c, :, 64:80], in_=wv4[:, :, 80:96]), reads=[rwq], writes=[rwq])
                S.op("dve", "tensor_copy", dict(out=wqp[:, c, :, 80:96], in_=wv4[:, :, 64:80]), reads=[rwq], writes=[rwq])
            S.op("dve", "tensor_scalar", dict(out=wkv[:], in0=wkvf[:], scalar1=gkv[:, 0:1], scalar2=None, op0=ALU.mult), reads=[rwq], writes=[rwq])
            for hb in range(2):
                S.op("dve", "tensor_copy", dict(out=VH[hb][:, :, 64], in_=valid[:]), reads=[rconst], writes=[rVH[hb]])

            cosF = sb(st, "cosF", [128, NOWN * 128]); sinF = sb(st, "sinF", [128, NOWN * 128])
            rF = Res("ropeF")
            with contextlib.ExitStack() as t2:
                pFi = sb(t2, "pFi", [128, NOWN * 128], I32); angF = sb(t2, "angF", [128, NOWN * 128])
                kFi = sb(t2, "kFi", [128, NOWN * 128], I32); invF = sb(t2, "invF", [128, 1]); sgnF = sb(t2, "sgnF", [128, 1])
                rq = Res("pF", multi=True)
                for s_ in range(4):
                    S.dma("sp", pFi[64:96, s_ * 512:(s_ + 1) * 512],
                          pos_i[(2 * s_ + 1) * 512:(2 * s_ + 2) * 512].partition_broadcast(32), writes=[rq])
                S.dma("sp", invF[:], invF_d[:, :], writes=[rq]); S.dma("sp", sgnF[:], sgnF_d[:, :], writes=[rq])
                R = slice(64, 96)
                S.op("dve", "tensor_copy", dict(out=angF[R, :], in_=pFi[R, :]), reads=[rq], writes=[rq])
                S.op("dve", "tensor_scalar", dict(out=angF[R, :], in0=angF[R, :], scalar1=invF[R, 0:1], scalar2=None, op0=ALU.mult), reads=[rq], writes=[rq])
                for tab, off in ((sinF, 0.0), (cosF, PI / 2)):
                    S.op("dve", "tensor_scalar", dict(out=kFi[R, :], in0=angF[R, :], scalar1=off, scalar2=1.0 / (2 * PI), op0=ALU.add, op1=ALU.mult),
                         reads=[rq], writes=[rq])
                    S.op("dve", "tensor_copy", dict(out=tab[R, :], in_=kFi[R, :]), reads=[rq], writes=[rq, rF])
                    S.op("dve", "scalar_tensor_tensor", dict(out=tab[R, :], in0=tab[R, :], scalar=-2 * PI, in1=angF[R, :], op0=ALU.mult, op1=ALU.add),
                         reads=[rq], writes=[rq, rF])
                    S.op("dve", "tensor_scalar", dict(out=tab[R, :], in0=tab[R, :], scalar1=off, scalar2=PI, op0=ALU.add, op1=ALU.min),
                         reads=[rq], writes=[rq, rF])
                    S.op("dve", "tensor_scalar", dict(out=tab[R, :], in0=tab[R, :], scalar1=-PI, scalar2=None, op0=ALU.max), reads=[rq], writes=[rq, rF])
                    S.op("act", "activation", dict(out=tab[R, :], in_=tab[R, :], func=AF.Sin), reads=[rq], writes=[rq, rF])
                S.op("dve", "tensor_scalar", dict(out=sinF[R, :], in0=sinF[R, :], scalar1=sgnF[R, 0:1], scalar2=None, op0=ALU.mult), reads=[rq], writes=[rq, rF])
                S.flush()
                checkpoint()

            with contextlib.ExitStack() as t1:
                xt = [sb(t1, f"xt{i}", [128, D]) for i in range(2)]; rxt = [Res("xt0"), Res("xt1")]
                xn = [sb(t1, f"xn{i}", [128, D], BF16) for i in range(2)]; rxn = [Res("xn0"), Res("xn1")]
                hT = [sb(t1, f"hT{i}", [128, 8, 512], BF16) for i in range(2)]; rhT = [Res("hT0"), Res("hT1")]
                junk = sb(t1, "junk", [128, 256], BF16)
                rtmp = sb(t1, "ropetmp", [128, 64]); latn = [sb(t1, f"latn{i}", [128, 416], BF16) for i in range(2)]; rlat = [Res("latn0"), Res("latn1")]
                st3 = [sb(t1, f"st3{i}", [128, 2, 2]) for i in range(2)]; rst3 = [Res("st30"), Res("st31")]
                LBS = ((0, 1), (2, 4))
                for r in range(4):
                    make_hT(r, xt, xn, hT, rxt, rxn, rhT)

                def stage1(hh):
                    sq = st3[hh % 2]; rsq = rst3[hh % 2]
                    for j in range(2):
                        t = 2 * hh + j
                        blk, r = divmod(t, 4)
                        hb = blk % 2
                        i = own_index(t)
                        lbk = LBS[hh % 2][j]
                        if blk + 1 < 8:
                            make_hT_a(t + 4, xt, xn, rxt, rxn)
                        for c in range(8):
                            S.op("pe", "matmul", dict(out=bk(lbk)[:, 0:416], lhsT=hT[hb][:, c, r * 128:(r + 1) * 128], rhs=wm[:, c, 0:416],
                                                      start=(c == 0), stop=(c == 7)), reads=[rwm, rhT[hb]], writes=[rbank[lbk]])
                        for k_, (a_, n_) in enumerate(((0, 256), (256, 128))):
                            S.op("act", "activation", dict(out=junk[:, 0:n_], in_=bk(lbk)[:, a_:a_ + n_], func=AF.Square, accum_out=sq[:, k_, j:j + 1]),
                                 reads=[rbank[lbk]], writes=[rsq])
                        if i is not None:
                            for c in range(8):
                                S.op("pe", "matmul", dict(out=bk(3)[:, :], lhsT=hT[hb][:, c, r * 128:(r + 1) * 128], rhs=wm[:, c, 416:928],
                                                          start=(c == 0), stop=(c == 7)), reads=[rwm, rhT[hb]], writes=[rbank[3]])
                            S.op("act", "activation", dict(out=mixB[:, i, :], in_=bk(3)[:, :], func=AF.Silu), reads=[rbank[3]], writes=[rmixB, rmixBt[i]])
                        if blk + 1 < 8:
                            make_hT_b(t + 4, xn, hT, rxn, rhT, n_dve=6)

                def stats(hh):
                    sq = st3[hh % 2]; rsq = rst3[hh % 2]
                    for k_, n_ in enumerate((256, 128)):
                        S.op("dve", "tensor_scalar", dict(out=sq[:, k_, :], in0=sq[:, k_, :], scalar1=1.0 / n_, scalar2=1e-6, op0=ALU.mult, op1=ALU.add),
                             reads=[rsq], writes=[rsq])
                    S.op("act", "activation", dict(out=sq[:, :, :], in_=sq[:, :, :], func=AF.Sqrt), reads=[rsq], writes=[rsq])
                    S.op("dve", "reciprocal", dict(out=sq[:, :, :], in_=sq[:, :, :]), reads=[rsq], writes=[rsq])

                def stage2(hh):
                    sq = st3[hh % 2]; rsq = rst3[hh % 2]
                    for j in range(2):
                        t = 2 * hh + j
                        i = own_index(t)
                        lb = j
                        lbk = LBS[hh % 2][j]
                        ps = bk(lbk)
                        if i is not None:
                            S.op("dve", "tensor_scalar", dict(out=latn[lb][:, 0:256], in0=ps[:, 0:256], scalar1=sq[:, 0, j:j + 1], scalar2=None, op0=ALU.mult),
                                 reads=[rbank[lbk], rsq], writes=[rlat[lb]])
                        S.op("dve", "tensor_scalar", dict(out=latn[lb][:, 256:384], in0=ps[:, 256:384], scalar1=sq[:, 1, j:j + 1], scalar2=None, op0=ALU.mult),
                             reads=[rbank[lbk], rsq], writes=[rlat[lb]])
                        rope_tm(ps[:, 384:416].rearrange("p (h d) -> p h d", h=1), latn[lb][:, 384:416].rearrange("p (h d) -> p h d", h=1),
                                1, 16, cos16[:, t, :], sin16[:, t, :], rtmp, [rbank[lbk], rtab16], [rlat[lb]])
                        S.op("pe", "transpose", dict(out=bk16(7)[:, 0:128], in_=latn[lb][:, 256:384], identity=identb[:]), reads=[rlat[lb], rconst], writes=[rbank[7]])
                        S.op("pe", "transpose", dict(out=bk16(7)[0:32, 128:256], in_=latn[lb][:, 384:416], identity=identb[:]), reads=[rlat[lb], rconst], writes=[rbank[7]])
                        if i is not None:
                            for c in range(2):
                                S.op("pe", "transpose", dict(out=bk16(7)[:, 256 + c * 128:384 + c * 128], in_=latn[lb][:, c * 128:(c + 1) * 128], identity=identb[:]),
                                     reads=[rlat[lb], rconst], writes=[rbank[7]])
                        S.op("dve", "tensor_copy", dict(out=ckvnT[:, t * 128:(t + 1) * 128], in_=bk16(7)[:, 0:128]), reads=[rbank[7]], writes=[rckv])
                        for hb2 in range(2):
                            S.op("dve", "tensor_copy", dict(out=KH[hb2][64:96, t * 128:(t + 1) * 128], in_=bk16(7)[0:32, 128:256]),
                                 reads=[rbank[7]], writes=[rKHrot[hb2]])
                        if i is not None:
                            S.op("dve", "tensor_copy", dict(out=cqnT[:, :, i * 128:(i + 1) * 128], in_=bk16(7)[:, 256:512].rearrange("p (c q) -> p c q", c=2)),
                                 reads=[rbank[7]], writes=[rcqn])

                for hh in range(16):
                    stage1(hh)
                    if hh >= 1:
                        stage2(hh - 1)
                    stats(hh)
                stage2(15)
                dbg("ckvnT", ckvnT[:, :], [128, S_LEN], BF16, reads=[rckv])
                dbg("cqnT", cqnT[:, :, :], [128, 2, NOWN * 128], BF16, reads=[rcqn])
                dbg("krot", KH[0][64:96, :], [32, S_LEN], BF16, reads=[rKHrot[0]])
                S.flush()
                checkpoint()

            with contextlib.ExitStack() as t1:
                Pt = [sb(t1, f"Pt{i}", [128, 512], BF16) for i in range(6)]; rPt = [Res(f"Pt{i}") for i in range(6)]
                accS = [sb(t1, f"accS{i}", [65, 512]) for i in range(2)]; raccS = [Res("accS0"), Res("accS1")]
                qh2 = [[sb(t1, f"qh{i}_{j}", [96, 512], BF16) for j in range(2)] for i in range(2)]; rqh2 = [[Res(f"qh{i}_{j}") for j in range(2)] for i in range(2)]
                qtmp = [sb(t1, f"qtmp{i}", [128, 512]) for i in range(2)]; rqtmp = [Res("qtmp0"), Res("qtmp1")]
                qtmp2 = [sb(t1, f"qtmpb{i}", [128, 512]) for i in range(2)]; rqtmp2 = [Res("qtmpb0"), Res("qtmpb1")]
                sm = [sb(t1, f"sm{i}", [128, 16]) for i in range(2)]; rsm = [Res("sm0"), Res("sm1")]
                otmp = [sb(t1, f"otmp{i}", [128, 4, 64]) for i in range(2)]; rot = [Res("otmp0"), Res("otmp1")]
                SC = float(96 ** -0.5)
                R = slice(64, 96)

                def mla_stream(h):
                    hb = h % 2
                    accb = 4 + hb
                    for blk in range(8):
                        gb = 7 - (blk % 2)
                        S.op("pe", "matmul", dict(out=bk(gb)[0:64, :], lhsT=wkv[:, h * 128:h * 128 + 64], rhs=ckvnT[:, blk * 512:(blk + 1) * 512],
                                                  start=True, stop=True), reads=[rwq, rckv], writes=[rbank[gb]])
                        S.op("dve", "tensor_copy", dict(out=KH[hb][0:64, blk * 512:(blk + 1) * 512], in_=bk(gb)[0:64, :]),
                             reads=[rbank[gb]], writes=[rKH[hb]])
                        yield
                    for t8 in range(4):
                        gb = 7 - (t8 % 2)
                        for tt in range(8):
                            t = t8 * 8 + tt
                            S.op("pe", "matmul", dict(out=bk(gb)[:, tt * 64:(tt + 1) * 64], lhsT=ckvnT[:, t * 128:(t + 1) * 128],
                                                      rhs=wkv[:, h * 128 + 64:h * 128 + 128], start=True, stop=True),
                                 reads=[rwq, rckv], writes=[rbank[gb]])
                        if t8 == 0:
                            for tt in range(4):
                                S.op("dve", "tensor_scalar", dict(out=VH[hb][:, tt, 0:64], in0=bk(gb)[:, tt * 64:(tt + 1) * 64], scalar1=valid[:, tt:tt + 1],
                                                                  scalar2=None, op0=ALU.mult), reads=[rbank[gb], rconst], writes=[rVH[hb]])
                            S.op("dve", "tensor_copy", dict(out=VH[hb][:, 4:8, 0:64], in_=bk(gb)[:, 256:512].rearrange("p (t d) -> p t d", t=4)),
                                 reads=[rbank[gb]], writes=[rVH[hb]])
                        else:
                            S.op("dve", "tensor_copy", dict(out=VH[hb][:, t8 * 8:(t8 + 1) * 8, 0:64], in_=bk(gb)[:, :].rearrange("p (t d) -> p t d", t=8)),
                                 reads=[rbank[gb]], writes=[rVH[hb]])
                        yield
                    def qgen(s_):
                        cs = slice(s_ * 512, (s_ + 1) * 512)
                        qd = qh2[hb][s_ % 2]; rqd = rqh2[hb][s_ % 2]
                        for c in range(2):
                            S.op("pe", "matmul", dict(out=bk(7)[0:96, :], lhsT=wq[:, c, h * 96:(h + 1) * 96], rhs=cqnT[:, c, cs],
                                                      start=(c == 0), stop=(c == 1)), reads=[rwq, rcqn], writes=[rbank[7]])
                        for c in range(2):
                            S.op("pe", "matmul", dict(out=bk(6)[0:96, :], lhsT=wqp[:, c, h, :], rhs=cqnT[:, c, cs],
                                                      start=(c == 0), stop=(c == 1)), reads=[rwq, rcqn], writes=[rbank[6]])
                        S.op("dve", "tensor_copy", dict(out=qd[0:64, :], in_=bk(7)[0:64, :]), reads=[rbank[7]], writes=[rqd])
                        S.op("dve", "tensor_tensor", dict(out=qtmp[hb][R, :], in0=bk(7)[R, :], in1=cosF[R, cs], op=ALU.mult), reads=[rbank[7], rF], writes=[rqtmp[hb]])
                        S.op("dve", "tensor_tensor", dict(out=qtmp2[hb][R, :], in0=bk(6)[R, :], in1=sinF[R, cs], op=ALU.mult), reads=[rbank[6], rF], writes=[rqtmp2[hb]])
                        S.op("dve", "tensor_tensor", dict(out=qd[R, :], in0=qtmp[hb][R, :], in1=qtmp2[hb][R, :], op=ALU.add),
                             reads=[rqtmp[hb], rqtmp2[hb]], writes=[rqd])

                    def fin_rest(s_):
                        for r in range(4):
                            S.op("pe", "transpose", dict(out=bk(6)[:, r * 65:(r + 1) * 65], in_=accS[hb][:, r * 128:(r + 1) * 128], identity=ident[0:65, 0:65]),
                                 reads=[raccS[hb], rconst], writes=[rbank[6]])
                        S.op("dve", "reciprocal", dict(out=sm[hb][:, 4:8], in_=bk(6)[:, 64:260:65]), reads=[rbank[6]], writes=[rsm[hb]])
                        S.op("dve", "tensor_tensor", dict(out=otmp[hb][:], in0=bk(6)[:, 0:260].rearrange("p (r d) -> p r d", r=4)[:, :, 0:64],
                                                          in1=sm[hb][:, 4:8].unsqueeze(2).to_broadcast([128, 4, 64]), op=ALU.mult),
                             reads=[rbank[6], rsm[hb]], writes=[rot[hb]])
                        mvb = mixB[:, s_ * 4:(s_ + 1) * 4, h * 64:(h + 1) * 64]
                        rmv = [rmixBt[s_ * 4 + r] for r in range(4)]
                        S.op("dve", "tensor_tensor", dict(out=mvb, in0=mvb, in1=otmp[hb][:, :, :], op=ALU.mult), reads=[rot[hb]] + rmv, writes=rmv)

                    qgen(0)
                    yield from idle(4)
                    flat = [(s_, u) for s_ in range(4) for u in range(8 * s_ + 8)]
                    nflat = len(flat)

                    def emit_score(gi):
                        s_, u = flat[gi]
                        nk = 8 * s_ + 8
                        b = 2 * hb + (gi % 2)
                        c0 = max(0, u - (nk - 4)) * 128
                        qd = qh2[hb][s_ % 2]; rqd = rqh2[hb][s_ % 2]
                        S.op("pe", "matmul", dict(out=bk(b)[:, c0:512], lhsT=KH[hb][:, u * 128:(u + 1) * 128], rhs=qd[:, c0:512],
                                                  start=True, stop=True), reads=[rKH[hb], rKHrot[hb], rqd], writes=[rbank[b]])
                    emit_score(0)
                    emit_score(1)
                    for gi in range(nflat):
                        s_, u = flat[gi]
                        nk = 8 * s_ + 8
                        b = 2 * hb + (gi % 2)
                        p = 3 * hb + (gi % 3)
                        c0 = max(0, u - (nk - 4)) * 128
                        S.op("act", "activation", dict(out=Pt[p][:, c0:512], in_=bk(b)[:, c0:512], func=AF.Exp, scale=SC),
                             reads=[rbank[b]], writes=[rPt[p]])
                        if u >= nk - 4:
                            S.op("dve", "tensor_tensor", dict(out=Pt[p][:, c0:c0 + 128], in0=Pt[p][:, c0:c0 + 128], in1=trile[:], op=ALU.mult),
                                 reads=[rPt[p], rconst], writes=[rPt[p]])
                        if gi + 2 < nflat:
                            emit_score(gi + 2)
                        if u == 0 and s_ > 0:
                            S.op("dve", "tensor_copy", dict(out=accS[hb][:, :], in_=bk(accb)[0:65, :]), reads=[rbank[accb]], writes=[raccS[hb]])
                        S.op("pe", "matmul", dict(out=bk(accb)[0:65, c0:512], lhsT=VH[hb][:, u, :], rhs=Pt[p][:, c0:512],
                                                  start=(u == 0), stop=(u == nk - 1)), reads=[rPt[p], rVH[hb]], writes=[rbank[accb]])
                        yield
                        if u == 1 and s_ > 0:
                            fin_rest(s_ - 1)
                            yield
                        if u == 3 and s_ + 1 < 4:
                            qgen(s_ + 1)
                            yield
                    S.op("dve", "tensor_copy", dict(out=accS[hb][:, :], in_=bk(accb)[0:65, :]), reads=[rbank[accb]], writes=[raccS[hb]])
                    yield from idle(3)
                    fin_rest(3)
                    yield

                run_staggered([mla_stream(h) for h in range(8)], 44, max_active=2, gate_key=lambda k: k)
                dbg("mixB", mixB[:, :, :], [128, NOWN, 512], BF16, reads=rmixBt)
                S.flush()
                checkpoint()

        with contextlib.ExitStack() as st:
            xt = [sb(st, f"xo{i}", [128, D]) for i in range(2)]; rxt = [Res("xo0"), Res("xo1")]
            mT = [sb(st, f"mT{i}", [128, 8, 128], BF16) for i in range(2)]; rmT = [Res("mT0"), Res("mT1")]
            yo = [sb(st, f"yo{i}", [128, D]) for i in range(2)]; ryo = [Res("yo0"), Res("yo1")]
            junk = sb(st, "junk5", [128, D], BF16); st5 = [sb(st, f"st5{i}", [128, 4]) for i in range(2)]; rst5 = [Res("st50"), Res("st51")]

            def p5_front(i):
                b = i % 2
                t = 8 * (i // 4) + 4 + (i % 4)
                tb = 7 - b
                S.dma("sp", xt[b][:], xs[t * 128:(t + 1) * 128, :], writes=[rxt[b]])
                for c in range(8):
                    src = mixA[:, i, c * 128:(c + 1) * 128] if c < 4 else mixB[:, i, (c - 4) * 128:(c - 3) * 128]
                    S.op("pe", "transpose", dict(out=bk16(tb)[:, c * 128:(c + 1) * 128], in_=src, identity=identb[:]),
                         reads=[rmixA[i], rmixBt[i], rconst], writes=[rbank[tb]])
                S.op("act", "activation", dict(out=mT[b][:, :, :], in_=bk16(tb)[:, :].rearrange("p (c q) -> p c q", c=8), func=AF.Copy),
                     reads=[rbank[tb]], writes=[rmT[b]])

            p5_front(0)
            for i in range(NOWN):
                b = i % 2
                if i + 1 < NOWN:
                    p5_front(i + 1)
                for hh in range(2):
                    ob = 2 * b + hh
                    for c in range(8):
                        S.op("pe", "matmul", dict(out=bk(ob)[:, :], lhsT=mT[b][:, c, :], rhs=wo[:, c, hh * 512:(hh + 1) * 512],
                                                  start=(c == 0), stop=(c == 7)), reads=[rmT[b], rwo], writes=[rbank[ob]])
                    hsl = slice(hh * 512, (hh + 1) * 512)
                    S.op("dve", "tensor_tensor", dict(out=yo[b][:, hsl], in0=bk(ob)[:, :], in1=gate_bc[:, hsl], op=ALU.mult),
                         reads=[rbank[ob], rmod], writes=[ryo[b]])
                S.op("dve", "tensor_tensor", dict(out=yo[b][:, :], in0=yo[b][:, :], in1=xt[b][:, :], op=ALU.add), reads=[ryo[b], rxt[b]], writes=[ryo[b]])
                S.op("act", "activation", dict(out=junk[:], in_=yo[b][:], func=AF.Square, accum_out=st5[b][:, 0:1]), reads=[ryo[b]], writes=[rst5[b]])
                S.op("dve", "tensor_scalar", dict(out=st5[b][:, 1:2], in0=st5[b][:, 0:1], scalar1=1.0 / D, scalar2=1e-6, op0=ALU.mult, op1=ALU.add), reads=[rst5[b]], writes=[rst5[b]])
                S.op("act", "activation", dict(out=st5[b][:, 2:3], in_=st5[b][:, 1:2], func=AF.Sqrt), reads=[rst5[b]], writes=[rst5[b]])
                S.op("dve", "reciprocal", dict(out=st5[b][:, 2:3], in_=st5[b][:, 2:3]), reads=[rst5[b]], writes=[rst5[b]])
                S.op("dve", "scalar_tensor_tensor", dict(out=yo[b][:, :], in0=yo[b][:, :], scalar=st5[b][:, 2:3], in1=fing_bc[:, :], op0=ALU.mult, op1=ALU.mult),
                     reads=[ryo[b], rst5[b], rconst], writes=[ryo[b]])
                S.dma("sp", out_d[i * 128:(i + 1) * 128, :], yo[b][:, :], reads=[ryo[b]])
            S.wait_all_dma("sp")
            S.flush()
            checkpoint()
    return nc, dbg_out


def _constants(par):
    delta = 1 - par
    c = {}
    c["inv32"] = (10000.0 ** (-np.arange(32, dtype=np.float32) / 32)).astype(np.float32)
    c["inv16"] = (10000.0 ** (-np.arange(16, dtype=np.float32) / 16)).astype(np.float32)
    invF = np.zeros((128, 1), np.float32); sgnF = np.zeros((128, 1), np.float32)
    for r in range(64, 96):
        invF[r, 0] = c["inv16"][(r - 64) % 16]
        sgnF[r, 0] = -1.0 if r < 80 else 1.0
    c["invF"] = invF; c["sgnF"] = sgnF
    k = np.arange(S_LEN)
    c["E_aug"] = (k[None, :] // 64 == np.arange(64)[:, None]).astype(np.float32)
    c["ident"] = np.eye(128, dtype=np.float32)
    kk = np.arange(128)[:, None]; qq = np.arange(128)[None, :]
    c["tri_le"] = (kk <= qq).astype(np.float32)
    c["tri_gt"] = (kk > qq).astype(np.float32)
    own_t = np.concatenate([np.arange((2 * s + 1) * 512, (2 * s + 2) * 512) for s in range(4)])
    n = np.arange(256)[:, None]
    c["cmpmask"] = ((16 * n + 31 <= own_t[None, :]) & (n < 255)).astype(np.float32)
    j = np.arange(64)[None, :]
    cur = (own_t // 64)[:, None]
    jr = j - 8 * delta; curr = cur - 8 * delta
    forced = (jr == 0) | (jr == curr) | (jr == curr - 1)
    keep = np.ones((NOWN * 128, 64), np.float32); force = np.zeros((NOWN * 128, 64), np.float32)
    fut = jr > curr
    dummy = jr < 0
    force[forced & ~dummy & ~fut] = 1.0e4
    c["impkeep"] = keep; c["impforce"] = force
    start = np.arange(256)[:, None] * 16; bstart = np.arange(64)[None, :] * 64
    ov = np.minimum(start + 32, bstart + 64) - np.maximum(start, bstart)
    m1 = (np.clip(ov, 0, None) / 32).astype(np.float32); m1[255] = 0.0
    c["m1"] = m1
    return c


_CACHE = {}


def kernel(x, c, positions, ada_w, ada_b, norm_g, w_in, cmp_pos, cmp_k_w1, cmp_k_w2, cmp_v_w1, cmp_v_w2,
           q_norm_g, w_q_up, kv_norm_g, w_kv_up, w_out, final_norm_g, _dbg=(), _stop=99):
    x = np.asarray(x, np.float32); c = np.asarray(c, np.float32); positions = np.asarray(positions, np.int32)
    key = (tuple(_dbg), _stop)
    if key not in _CACHE:
        _CACHE[key] = build(_dbg, _stop)
    nc, dbg_out = _CACHE[key]
    shared = {
        "ada_w": np.asarray(ada_w, np.float32)[0], "ada_b": np.asarray(ada_b, np.float32)[0], "norm_g": np.asarray(norm_g, np.float32)[0],
        "w_in": np.asarray(w_in, np.float32)[0], "cmp_pos": np.asarray(cmp_pos, np.float32)[0],
        "cmp_k_w1": np.asarray(cmp_k_w1, np.float32)[0], "cmp_k_w2": np.asarray(cmp_k_w2, np.float32)[0],
        "cmp_v_w1": np.asarray(cmp_v_w1, np.float32)[0], "cmp_v_w2": np.asarray(cmp_v_w2, np.float32)[0],
        "q_norm_g": np.asarray(q_norm_g, np.float32)[0], "w_q_up": np.asarray(w_q_up, np.float32)[0],
        "kv_norm_g": np.asarray(kv_norm_g, np.float32)[0], "w_kv_up": np.asarray(w_kv_up, np.float32)[0],
        "w_out": np.asarray(w_out, np.float32)[0], "final_norm_g": np.asarray(final_norm_g, np.float32),
    }
    shared["g_col"] = np.ascontiguousarray(shared["norm_g"].reshape(8, 128).T)
    shared["gq_col"] = np.ascontiguousarray(shared["q_norm_g"].reshape(2, 128).T)
    shared["gkv_col"] = np.ascontiguousarray(shared["kv_norm_g"].reshape(1, 128).T)
    shared["cmp_posT"] = np.ascontiguousarray(shared["cmp_pos"].T)
    consts = [_constants(0), _constants(1)]
    in_maps = []
    for core in range(8):
        b, par = divmod(core, 2)
        if par == 1:
            xs = x[b]; ps = positions[b]; valid = np.ones(S_LEN, np.float32)
        else:
            xs = np.concatenate([np.zeros((512, D), np.float32), x[b, :S_LEN - 512]], axis=0)
            ps = np.concatenate([np.zeros(512, np.int32), positions[b, :S_LEN - 512]])
            valid = np.concatenate([np.zeros(512, np.float32), np.ones(S_LEN - 512, np.float32)])
        m = {"xs": np.ascontiguousarray(xs), "pos_i": np.ascontiguousarray(ps), "valid": valid, "c_b": np.ascontiguousarray(c[b])}
        m["valid_pt"] = np.ascontiguousarray(valid.reshape(NT, 128).T)
        m["pos_pt"] = np.ascontiguousarray(ps.reshape(NT, 128).T)
        m["c_col"] = np.ascontiguousarray(c[b].reshape(8, 128).T)
        nidx = np.minimum(np.arange(256), 254)
        vc_ = valid[16 * nidx].copy(); vc_[255] = 0.0
        cp_ = ps[16 * nidx + 31].copy(); cp_[255] = 0
        m["validc_pt"] = np.ascontiguousarray(vc_.reshape(2, 128).T.astype(np.float32))
        m["cpos_pt"] = np.ascontiguousarray(cp_.reshape(2, 128).T.astype(np.int32))
        m.update(shared)
        m.update(consts[par])
        in_maps.append(m)
    res = run_bass_kernel_spmd(nc, in_maps, core_ids=list(range(8)))
    out = np.zeros((4, S_LEN, D), np.float32)
    for core in range(8):
        b, par = divmod(core, 2)
        o = np.asarray(res.results[core]["out"], np.float32)
        for s in range(4):
            qb = 2 * s + par
            out[b, qb * 512:(qb + 1) * 512, :] = o[s * 512:(s + 1) * 512, :]
    if _dbg:
        kernel.last_dbg = [{k: np.asarray(res.results[core]["dbg_" + k]) for k in dbg_out} for core in range(8)]
    return out
```

```python
import contextlib
import numpy as np
import ml_dtypes
import concourse.bass as bass
import concourse.mybir as mybir
from concourse.bass_utils import run_bass_kernel_spmd

F32 = mybir.dt.float32
BF16 = mybir.dt.bfloat16
I32 = mybir.dt.int32
ALU = mybir.AluOpType
AF = mybir.ActivationFunctionType

S_LEN = 4096
D = 1024
NT = 32
NOWN = 16
IN_W = 2744
PI = float(np.pi)

ENGS = ("pe", "act", "dve", "pool", "sp")
SAME_ENG_SYNC = {"pe": False, "act": True, "dve": True, "pool": True, "sp": False}
N_DMA_SLOTS = 10


class Res:
    __slots__ = ("name", "w", "r", "excl", "multi", "ws")

    def __init__(self, name, excl=False, multi=False):
        self.name = name
        self.w = None
        self.r = {}
        self.excl = excl
        self.multi = multi
        self.ws = {}


class Sched:
    def __init__(self, nc, stack):
        self.nc = nc
        self.sems = {}
        for e in ENGS:
            self.sems[e] = stack.enter_context(nc.semaphore("s_" + e))
        self.dq = ("sp", "pool")
        for q in self.dq:
            for i in range(N_DMA_SLOTS):
                self.sems[("d", q, i)] = stack.enter_context(nc.semaphore(f"d_{q}{i}"))
        self.cnt = {k: 0 for k in self.sems}
        self.ops = {e: [] for e in ENGS}
        self.seen = {e: {} for e in ENGS}
        self.dslot = {q: 0 for q in self.dq}
        self.dead = False

    def _wait(self, eng, key, val):
        if self.dead:
            return
        if self.seen[eng].get(key, 0) >= val:
            return
        self.seen[eng][key] = val
        sem = self.sems[key]
        self.ops[eng].append(lambda e, sem=sem, val=val: e.wait_ge(sem, val))

    def _deps(self, eng, reads, writes, extra=()):
        deps = {}

        def add(tok):
            if tok is None:
                return
            k, v = tok
            if deps.get(k, 0) < v:
                deps[k] = v
        for r in reads:
            add(r.w)
            if r.multi:
                for k, v in r.ws.items():
                    add((k, v))
            if r.excl:
                for k, v in r.r.items():
                    if k != eng:
                        add((k, v))
        for w in writes:
            if w.multi:
                continue
            add(w.w)
            for k, v in w.r.items():
                add((k, v))
        for t in extra:
            add(t)
        for k, v in deps.items():
            if k == eng and not SAME_ENG_SYNC[eng]:
                continue
            self._wait(eng, k, v)

    def _mark(self, tok, reads, writes):
        k, v = tok
        for r in reads:
            if r.r.get(k, 0) < v:
                r.r[k] = v
        for w in writes:
            if w.multi:
                if w.ws.get(k, 0) < v:
                    w.ws[k] = v
                continue
            w.w = tok
            w.r = {}

    def op(self, eng, meth, kw, reads=(), writes=()):
        if self.dead:
            return None
        self._deps(eng, reads, writes)
        sem = self.sems[eng]
        self.cnt[eng] += 1
        tok = (eng, self.cnt[eng])
        self.ops[eng].append(lambda e, meth=meth, kw=kw, sem=sem: getattr(e, meth)(**kw).then_inc(sem, 1))
        self._mark(tok, reads, writes)
        return tok

    def dma(self, q, out, in_, reads=(), writes=(), **kw):
        if self.dead:
            return None
        slot = self.dslot[q]
        self.dslot[q] = (slot + 1) % N_DMA_SLOTS
        key = ("d", q, slot)
        prev = (key, self.cnt[key]) if self.cnt[key] > 0 else None
        self._deps(q, reads, writes, extra=(prev,))
        self.cnt[key] += 16
        tok = (key, self.cnt[key])
        sem = self.sems[key]
        self.ops[q].append(lambda e, out=out, in_=in_, sem=sem, kw=kw:
                           e.dma_start(out=out, in_=in_, **kw).then_inc(sem, 16))
        self._mark(tok, reads, writes)
        return tok

    def wait_all_dma(self, eng):
        for q in self.dq:
            for i in range(N_DMA_SLOTS):
                k = ("d", q, i)
                if self.cnt[k]:
                    self._wait(eng, k, self.cnt[k])

    def flush(self):
        if not any(self.ops[e] for e in ENGS):
            return
        nc = self.nc
        ops = self.ops
        with nc.Block() as block:
            @block.tensor
            def _(e):
                for f in ops["pe"]:
                    f(e)

            @block.scalar
            def _(e):
                for f in ops["act"]:
                    f(e)

            @block.vector
            def _(e):
                for f in ops["dve"]:
                    f(e)

            @block.gpsimd
            def _(e):
                for f in ops["pool"]:
                    f(e)

            @block.sync
            def _(e):
                for f in ops["sp"]:
                    f(e)
        self.ops = {e: [] for e in ENGS}


O_Q, O_KC, O_VC, O_KS, O_VS, O_KW, O_VW, O_GL, O_ZN, O_CQ, O_CKV, O_KR, O_ZM = (
    0, 512, 640, 768, 896, 1024, 1152, 1280, 1304, 1816, 2072, 2200, 2232)
NSA_COLS = 1816
MLA_COLS = IN_W - NSA_COLS


class _Stop(Exception):
    pass


def build(dbg_names=(), stop=99):
    nc = bass.Bass("TRN2", target_bir_lowering=False)
    dram = {}

    def din(name, shape, dt=F32):
        dram[name] = nc.dram_tensor(name, list(shape), dt, kind="ExternalInput").ap()
        return dram[name]

    xs = din("xs", [S_LEN, D])
    pos_i = din("pos_i", [S_LEN], I32)
    valid_d = din("valid", [S_LEN])
    c_b = din("c_b", [D])
    ada_w = din("ada_w", [D, 3 * D]); ada_b = din("ada_b", [3 * D]); norm_g = din("norm_g", [D])
    w_in = din("w_in", [D, IN_W]); cmp_pos = din("cmp_pos", [32, 64])
    ckw1 = din("cmp_k_w1", [2048, 128]); ckw2 = din("cmp_k_w2", [128, 64])
    cvw1 = din("cmp_v_w1", [2048, 128]); cvw2 = din("cmp_v_w2", [128, 64])
    q_norm_g = din("q_norm_g", [256]); w_q_up = din("w_q_up", [256, 768])
    kv_norm_g = din("kv_norm_g", [128]); w_kv_up = din("w_kv_up", [128, 1024])
    w_out = din("w_out", [D, D]); fin_g = din("final_norm_g", [D])
    inv32_d = din("inv32", [32]); inv16_d = din("inv16", [16])
    invF_d = din("invF", [128, 1]); sgnF_d = din("sgnF", [128, 1])
    E_d = din("E_aug", [64, S_LEN]); ident_d = din("ident", [128, 128])
    trile_d = din("tri_le", [128, 128]); trigt_d = din("tri_gt", [128, 128])
    cmpmask_d = din("cmpmask", [256, NOWN * 128])
    impkeep_d = din("impkeep", [NOWN * 128, 64]); impforce_d = din("impforce", [NOWN * 128, 64])
    m1_d = din("m1", [256, 64])
    valid_pt = din("valid_pt", [128, NT]); pos_pt = din("pos_pt", [128, NT], I32)
    c_col = din("c_col", [128, 8]); g_col = din("g_col", [128, 8]); gq_col = din("gq_col", [128, 2]); gkv_col = din("gkv_col", [128, 1])
    posT_d = din("cmp_posT", [64, 32]); validc_pt = din("validc_pt", [128, 2]); cpos_pt = din("cpos_pt", [128, 2], I32)
    out_d = nc.dram_tensor("out", [NOWN * 128, D], F32, kind="ExternalOutput").ap()
    dbg_out = {}

    with contextlib.ExitStack() as gst:
        S = Sched(nc, gst)
        ckpt_n = [0]

        def checkpoint():
            ckpt_n[0] += 1
            if ckpt_n[0] == stop:
                S.wait_all_dma("sp")
                S.flush()
                S.dead = True

        uniq = [0]

        def sb(st, name, shape, dt=F32):
            uniq[0] += 1
            return st.enter_context(nc.sbuf_tensor(f"t{uniq[0]}_{name}", list(shape), dt))

        def dbg(name, ap, shape, dt=F32, reads=()):
            if name in dbg_names:
                d = nc.dram_tensor("dbg_" + name, list(shape), dt, kind="ExternalOutput").ap()
                dbg_out[name] = d
                S.dma("sp", d, ap, reads=reads)

        banks = [gst.enter_context(nc.psum_tensor(f"bank{i}", [128, 512], F32)) for i in range(8)]
        rbank = [Res(f"bank{i}", excl=True) for i in range(8)]

        def bk(i):
            return banks[i]

        def bk16(i):
            return banks[i][:].bitcast(BF16)

        ident = sb(gst, "ident", [128, 128]); identb = sb(gst, "identb", [128, 128], BF16)
        trile = sb(gst, "trile", [128, 128], BF16); trigt = sb(gst, "trigt", [128, 128], BF16)
        ones_f = sb(gst, "ones_f", [128, 128])
        scl1 = sb(gst, "scl1", [128, 8]); shf = sb(gst, "shf", [128, 8])
        gate_bc = sb(gst, "gate_bc", [128, D]); fing_bc = sb(gst, "fing_bc", [128, D])
        valid = sb(gst, "validc", [128, NT]); posf = sb(gst, "posf", [128, NT])
        rstd_all = sb(gst, "rstd_all", [128, NT]); rrstd = Res("rstd")
        mixA = sb(gst, "mixA", [128, NOWN, 512], BF16); mixB = sb(gst, "mixB", [128, NOWN, 512], BF16)
        rconst = Res("const", multi=True); rmod = Res("mod"); rmixA = [Res(f"mixA{i}") for i in range(NOWN)]
        rmixB = Res("mixB"); rmixBt = [Res(f"mixBt{i}") for i in range(NOWN)]

        S.dma("sp", ident[:], ident_d[:, :], writes=[rconst])
        S.dma("pool", identb[:], ident_d[:, :], writes=[rconst])
        S.dma("pool", trile[:], trile_d[:, :], writes=[rconst])
        S.dma("pool", trigt[:], trigt_d[:, :], writes=[rconst])
        S.dma("sp", valid[:], valid_pt[:, :], writes=[rconst])
        S.dma("sp", fing_bc[:], fin_g.partition_broadcast(128), writes=[rconst])
        S.op("dve", "memset", dict(ap=ones_f[:], constant=1.0), writes=[rconst])

        with contextlib.ExitStack() as st:
            ccol = sb(st, "ccol", [128, 8]); scol = sb(st, "scol", [128, 8], BF16)
            gcol = sb(st, "gcol", [128, 8]); posi = sb(st, "posi", [128, NT], I32)
            adab = sb(st, "adab", [1, 3 * D]); modrow = sb(st, "modrow", [1, 3 * D])
            awb = [sb(st, f"awb{i}", [128, 8, 512], BF16) for i in range(2)]
            rawb = [Res("awb0"), Res("awb1")]; rc = Res("ccol", multi=True); rrow = Res("modrow")
            S.dma("sp", ccol[:], c_col[:, :], writes=[rc])
            S.dma("sp", gcol[:], g_col[:, :], writes=[rc])
            S.dma("sp", posi[:], pos_pt[:, :], writes=[rc])
            S.dma("sp", adab[:], ada_b.rearrange("(o n) -> o n", o=1), writes=[rc])
            S.op("act", "activation", dict(out=scol[:], in_=ccol[:], func=AF.Silu), reads=[rc], writes=[rc])
            S.op("dve", "tensor_copy", dict(out=posf[:], in_=posi[:]), reads=[rc], writes=[rconst])
            aw_v = ada_w.rearrange("(c p) n -> p c n", p=128)
            for n in range(6):
                b = n % 2
                S.dma("pool", awb[b][:], aw_v[:, :, n * 512:(n + 1) * 512], writes=[rawb[b]])
                for c in range(8):
                    S.op("pe", "matmul", dict(out=bk(n % 2)[0:1, :], lhsT=scol[:, c:c + 1], rhs=awb[b][:, c, :],
                                                              start=(c == 0), stop=(c == 7)),
                         reads=[rc, rawb[b]], writes=[rbank[n % 2]])
                S.op("dve", "tensor_tensor", dict(out=modrow[:, n * 512:(n + 1) * 512], in0=bk(n % 2)[0:1, :],
                                                        in1=adab[:, n * 512:(n + 1) * 512], op=ALU.add),
                     reads=[rbank[n % 2], rc], writes=[rrow])
            for j in range(16):
                S.op("pe", "matmul", dict(out=bk(2)[:, j:j + 1], lhsT=modrow[:, j * 128:(j + 1) * 128], rhs=ones_f[0:1, 0:1],
                                                start=True, stop=True), reads=[rrow, rconst], writes=[rbank[2]])
            S.op("dve", "tensor_copy", dict(out=shf[:], in_=bk(2)[:, 0:8]), reads=[rbank[2]], writes=[rmod])
            S.op("dve", "scalar_tensor_tensor", dict(out=scl1[:], in0=bk(2)[:, 8:16], scalar=1.0, in1=gcol[:],
                                                         op0=ALU.add, op1=ALU.mult), reads=[rbank[2], rc], writes=[rmod])
            for hh in range(2):
                S.op("pe", "matmul", dict(out=bk(3)[:, :], lhsT=ones_f[0:1, :], rhs=modrow[:, 2048 + hh * 512:2048 + (hh + 1) * 512],
                                                  start=True, stop=True), reads=[rrow, rconst], writes=[rbank[3]])
                S.op("dve", "tensor_copy", dict(out=gate_bc[:, hh * 512:(hh + 1) * 512], in_=bk(3)[:, :]),
                     reads=[rbank[3]], writes=[rmod])
            xpre = [sb(st, f"xpre{i}", [128, D]) for i in range(3)]; rxpre = [Res(f"xpre{i}") for i in range(3)]
            jpre = sb(st, "jpre", [128, D], BF16)
            for t in range(NT):
                b3 = t % 3
                S.dma("sp", xpre[b3][:], xs[t * 128:(t + 1) * 128, :], writes=[rxpre[b3]])
                S.op("act", "activation", dict(out=jpre[:], in_=xpre[b3][:], func=AF.Square, accum_out=rstd_all[:, t:t + 1]),
                     reads=[rxpre[b3]], writes=[rrstd])
            S.op("dve", "tensor_scalar", dict(out=rstd_all[:], in0=rstd_all[:], scalar1=1.0 / D, scalar2=1e-6, op0=ALU.mult, op1=ALU.add),
                 reads=[rrstd], writes=[rrstd])
            S.op("act", "activation", dict(out=rstd_all[:], in_=rstd_all[:], func=AF.Sqrt), reads=[rrstd], writes=[rrstd])
            S.op("dve", "reciprocal", dict(out=rstd_all[:], in_=rstd_all[:]), reads=[rrstd], writes=[rrstd])
            dbg("scl1", scl1[:], [128, 8], reads=[rmod]); dbg("shf", shf[:], [128, 8], reads=[rmod])
            dbg("gate", gate_bc[0:1, :], [1, D], reads=[rmod])
            S.flush()
            checkpoint()

        def rope_tables(st, half, name):
            cosT = sb(st, name + "cos", [128, NT, half]); sinT = sb(st, name + "sin", [128, NT, half])
            with contextlib.ExitStack() as t2:
                invb = sb(t2, name + "inv", [128, half]); ang = sb(t2, name + "ang", [128, NT, half])
                ki = sb(t2, name + "ki", [128, NT, half], I32); kf = sb(t2, name + "kf", [128, NT, half])
                rr = Res(name + "tmp"); rt = Res(name + "tab")
                S.dma("sp", invb[:], (inv32_d if half == 32 else inv16_d).partition_broadcast(128), writes=[rr])
                S.op("dve", "tensor_tensor", dict(out=ang[:], in0=posf[:].unsqueeze(2).to_broadcast([128, NT, half]),
                                                      in1=invb[:].unsqueeze(1).to_broadcast([128, NT, half]), op=ALU.mult),
                     reads=[rr, rconst], writes=[rr])
                for tab, off in ((sinT, 0.0), (cosT, PI / 2)):
                    S.op("dve", "tensor_scalar", dict(out=ki[:], in0=ang[:], scalar1=off, scalar2=1.0 / (2 * PI),
                                                                 op0=ALU.add, op1=ALU.mult), reads=[rr], writes=[rr])
                    S.op("dve", "tensor_copy", dict(out=kf[:], in_=ki[:]), reads=[rr], writes=[rr])
                    S.op("dve", "scalar_tensor_tensor", dict(out=kf[:], in0=kf[:], scalar=-2 * PI, in1=ang[:],
                                                                 op0=ALU.mult, op1=ALU.add), reads=[rr], writes=[rr])
                    S.op("dve", "tensor_scalar", dict(out=kf[:], in0=kf[:], scalar1=off, scalar2=PI,
                                                                 op0=ALU.add, op1=ALU.min), reads=[rr], writes=[rr])
                    S.op("dve", "tensor_scalar", dict(out=kf[:], in0=kf[:], scalar1=-PI, scalar2=None, op0=ALU.max),
                         reads=[rr], writes=[rr])
                    S.op("act", "activation", dict(out=tab[:], in_=kf[:], func=AF.Sin), reads=[rr], writes=[rr, rt])
                S.flush()
                checkpoint()
            return cosT, sinT, rt

        def rope_tm(src, dst, nh, half, cosv, sinv, tmp, reads, writes):
            n = nh * half
            npart = src.shape[0]
            cb = cosv.unsqueeze(1).to_broadcast([npart, nh, half]); sbv = sinv.unsqueeze(1).to_broadcast([npart, nh, half])
            x1 = src[:, :, 0:half]; x2 = src[:, :, half:2 * half]
            t1 = tmp[:, 0:n].rearrange("p (h d) -> p h d", h=nh); t2 = tmp[:, n:2 * n].rearrange("p (h d) -> p h d", h=nh)
            S.op("dve", "tensor_tensor", dict(out=t1, in0=x1, in1=cb, op=ALU.mult), reads=reads, writes=[rtmp_rope])
            S.op("dve", "tensor_tensor", dict(out=t2, in0=x2, in1=sbv, op=ALU.mult), reads=reads, writes=[rtmp_rope])
            S.op("dve", "tensor_tensor", dict(out=dst[:, :, 0:half], in0=t1, in1=t2, op=ALU.subtract),
                 reads=[rtmp_rope], writes=writes)
            S.op("dve", "tensor_tensor", dict(out=t1, in0=x2, in1=cb, op=ALU.mult), reads=reads, writes=[rtmp_rope])
            S.op("dve", "tensor_tensor", dict(out=t2, in0=x1, in1=sbv, op=ALU.mult), reads=reads, writes=[rtmp_rope])
            S.op("dve", "tensor_tensor", dict(out=dst[:, :, half:2 * half], in0=t1, in1=t2, op=ALU.add),
                 reads=[rtmp_rope], writes=writes)

        rtmp_rope = Res("ropetmp")

        def make_hT_a(t, xt, xn, rxt, rxn):
            b = t % 2
            S.dma("sp", xt[b][:], xs[t * 128:(t + 1) * 128, :], writes=[rxt[b]])
            S.op("act", "activation", dict(out=xn[b][:], in_=xt[b][:], func=AF.Copy, scale=rstd_all[:, t:t + 1]),
                 reads=[rxt[b], rrstd], writes=[rxn[b]])

        def make_hT_b(t, xn, hT, rxn, rhT, n_dve=4):
            b = t % 2
            hb = (t // 4) % 2
            for c in range(8):
                tb = 5 + c // 4
                S.op("pe", "transpose", dict(out=bk16(tb)[:, (c % 4) * 128:(c % 4 + 1) * 128], in_=xn[b][:, c * 128:(c + 1) * 128],
                                             identity=identb[:]), reads=[rxn[b], rconst], writes=[rbank[tb]])
            col = (t % 4) * 128
            for c in range(8):
                tb = 5 + c // 4
                src = bk16(tb)[:, (c % 4) * 128:(c % 4 + 1) * 128]
                if c < n_dve:
                    S.op("dve", "tensor_scalar", dict(out=hT[hb][:, c, col:col + 128], in0=src,
                                                      scalar1=scl1[:, c:c + 1], scalar2=shf[:, c:c + 1], op0=ALU.mult, op1=ALU.add),
                         reads=[rbank[tb], rmod], writes=[rhT[hb][t % 4]])
                else:
                    S.op("act", "activation", dict(out=hT[hb][:, c, col:col + 128], in_=src,
                                                   func=AF.Identity, bias=shf[:, c:c + 1], scale=scl1[:, c:c + 1]),
                         reads=[rbank[tb], rmod], writes=[rhT[hb][t % 4]])

        def make_hT(t, xt, xn, hT, rxt, rxn, rhT):
            make_hT_a(t, xt, xn, rxt, rxn)
            make_hT_b(t, xn, hT, rxn, rhT)

        junk_s = sb(gst, "junk_s", [128, 8]); rjunk = Res("junk")
        rvalid_dummy = None

        def idle(n):
            for _ in range(n):
                yield

        def run_staggered(gens, lag, max_active=2, gate_key=None):
            pending = list(enumerate(gens))
            active = []
            while pending or active:
                if pending and len(active) < max_active and (not active or active[-1]["steps"] >= lag):
                    k, gen = pending.pop(0)
                    active.append({"k": k, "gen": gen, "steps": 0, "parked": False, "main": False})
                for ent in list(active):
                    if ent["parked"]:
                        key = gate_key(ent["k"])
                        if any(o["main"] and gate_key(o["k"]) == key for o in active if o is not ent):
                            continue
                        ent["parked"] = False
                        ent["main"] = True
                    try:
                        r = next(ent["gen"])
                        ent["steps"] += 1
                        if r == "GATE":
                            ent["parked"] = True
                        elif r == "RELEASE":
                            ent["main"] = False
                    except StopIteration:
                        active.remove(ent)

        def own_index(t):
            blk, r = divmod(t, 4)
            if blk % 2 == 1:
                return (blk // 2) * 4 + r
            return None

        with contextlib.ExitStack() as nsa:
            QT = sb(nsa, "QT", [128, 8, NOWN * 128], BF16)
            KE = sb(nsa, "KE", [128, 2, S_LEN], BF16)
            KW = sb(nsa, "KW", [64, 2, S_LEN], BF16)
            VS = sb(nsa, "VS", [128, NT, 2, 65], BF16); VW = sb(nsa, "VW", [128, NT, 2, 65], BF16)
            GS = sb(nsa, "GS", [128, NOWN, 24])
            mixBf = mixB[:].rearrange("p a b -> p (a b)")
            kcmpT = mixBf[:, 0:S_LEN]; vcmpT = mixBf[:, S_LEN:2 * S_LEN]
            rQT = [Res(f"QT{i}") for i in range(NOWN)]; rQS = [[Res(f"QS{i}_{g}") for g in range(2)] for i in range(NOWN)]
            rKE = Res("KE"); rKW = Res("KW"); rVS = Res("VS"); rVW = Res("VW"); rGS = Res("GS")
            for g in range(2):
                S.dma("pool", KE[64:128, g, :], E_d[:, :], writes=[rKE])
            for V, rV in ((VS, rVS), (VW, rVW)):
                S.op("dve", "tensor_copy", dict(out=V[:, :, :, 64], in_=valid[:].unsqueeze(2).to_broadcast([128, NT, 2])),
                     reads=[rconst], writes=[rV])

            W1k = sb(nsa, "W1k", [128, 32, 128], BF16)
            W2 = [sb(nsa, f"W2{j}", [128, 64], BF16) for j in range(2)]
            posT = sb(nsa, "posT", [128, 32], BF16)
            rW = Res("W1", multi=True)
            with contextlib.ExitStack() as st:
                cos32, sin32, rtab = rope_tables(st, 32, "r32")
                wn = sb(st, "wn", [128, 8, NSA_COLS], BF16); rwn = Res("wn", multi=True)
                w_v = w_in.rearrange("(c p) n -> p c n", p=128)
                W_KC, W_VC, W_KS, W_KW, W_VS, W_VW, W_GL, W_ZN = 512, 640, 768, 896, 1024, 1152, 1280, 1304
                segs = ((0, O_Q, 512), (W_KC, O_KC, 128), (W_VC, O_VC, 128), (W_KS, O_KS, 128), (W_KW, O_KW, 128),
                        (W_VS, O_VS, 128), (W_VW, O_VW, 128), (W_GL, O_GL, 24), (W_ZN, O_ZN, 512))
                for (d0, s0, n_) in segs:
                    S.dma("pool", wn[:, :, d0:d0 + n_], w_v[:, :, s0:s0 + n_], writes=[rwn])
                vk_ = ckw1.rearrange("(l d) j -> d l j", d=64)
                S.dma("pool", W1k[0:64, :, :], vk_, writes=[rW]); S.dma("pool", W1k[64:128, :, :], vk_, writes=[rW])
                S.dma("pool", W2[0][:], ckw2[:, :], writes=[rW]); S.dma("pool", W2[1][:], cvw2[:, :], writes=[rW])
                for hh in range(2):
                    S.dma("pool", posT[hh * 64:(hh + 1) * 64, :], posT_d[:, :], writes=[rW])
                xt = [sb(st, f"xt{i}", [128, D]) for i in range(2)]; rxt = [Res("xt0"), Res("xt1")]
                xn = [sb(st, f"xn{i}", [128, D], BF16) for i in range(2)]; rxn = [Res("xn0"), Res("xn1")]
                hT = [sb(st, f"hT{i}", [128, 8, 512], BF16) for i in range(2)]; rhT = [[Res(f"hT{i}_{r}") for r in range(4)] for i in range(2)]
                rtmp = sb(st, "ropetmp", [128, 1024]); ktm = sb(st, "ktm", [128, 256], BF16); rktm = Res("ktm")
                qtm = sb(st, "qtm", [128, 512], BF16); rqtm = Res("qtm")
                for r in range(4):
                    make_hT(r, xt, xn, hT, rxt, rxn, rhT)
                for blk in range(8):
                    hb = blk % 2
                    def emit_kcvc(blk=blk, hb=hb):
                        for j, (off, dstT) in enumerate(((W_KC, kcmpT), (W_VC, vcmpT))):
                            for c in range(8):
                                S.op("pe", "matmul", dict(out=bk(j)[:, :], lhsT=wn[:, c, off:off + 128], rhs=hT[hb][:, c, :],
                                                          start=(c == 0), stop=(c == 7)), reads=[rwn] + rhT[hb], writes=[rbank[j]])
                            S.op("act", "activation", dict(out=dstT[:, blk * 512:(blk + 1) * 512], in_=bk(j)[:, :], func=AF.Copy),
                                 reads=[rbank[j]], writes=[rmixB])
                    for r in range(4):
                        t = blk * 4 + r
                        if blk + 1 < 8:
                            make_hT_a((blk + 1) * 4 + r, xt, xn, rxt, rxn)
                        for c in range(8):
                            S.op("pe", "matmul", dict(out=bk(2)[:, :], lhsT=hT[hb][:, c, r * 128:(r + 1) * 128], rhs=wn[:, c, W_KS:W_KS + 512],
                                                      start=(c == 0), stop=(c == 7)), reads=[rwn, rhT[hb][r]], writes=[rbank[2]])
                        ps = bk(2)
                        i = own_index(t)
                        if i is not None:
                            for c in range(8):
                                S.op("pe", "matmul", dict(out=bk(3)[:, :], lhsT=hT[hb][:, c, r * 128:(r + 1) * 128], rhs=wn[:, c, 0:512],
                                                          start=(c == 0), stop=(c == 7)), reads=[rwn, rhT[hb][r]], writes=[rbank[3]])
                            for c in range(8):
                                S.op("pe", "matmul", dict(out=bk(4)[:, :], lhsT=hT[hb][:, c, r * 128:(r + 1) * 128], rhs=wn[:, c, W_ZN:W_ZN + 512],
                                                          start=(c == 0), stop=(c == 7)), reads=[rwn, rhT[hb][r]], writes=[rbank[4]])
                            for c in range(8):
                                S.op("pe", "matmul", dict(out=bk(1)[:, 0:24], lhsT=hT[hb][:, c, r * 128:(r + 1) * 128], rhs=wn[:, c, W_GL:W_GL + 24],
                                                          start=(c == 0), stop=(c == 7)), reads=[rwn, rhT[hb][r]], writes=[rbank[1]])
                        for (o2, V, rV) in ((256, VS, rVS), (384, VW, rVW)):
                            src = ps[:, o2:o2 + 128].rearrange("p (g d) -> p g d", g=2)
                            if t < 4:
                                S.op("dve", "tensor_scalar", dict(out=V[:, t, :, 0:64], in0=src, scalar1=valid[:, t:t + 1], scalar2=None, op0=ALU.mult),
                                     reads=[rbank[2], rconst], writes=[rV])
                            else:
                                S.op("dve", "tensor_copy", dict(out=V[:, t, :, 0:64], in_=src), reads=[rbank[2]], writes=[rV])
                        rope_tm(ps[:, 0:256].rearrange("p (g d) -> p g d", g=4), ktm[:, :].rearrange("p (g d) -> p g d", g=4),
                                4, 32, cos32[:, t, :], sin32[:, t, :], rtmp, [rbank[2], rtab], [rktm])
                        if i is not None:
                            rope_tm(bk(3)[:, :].rearrange("p (h d) -> p h d", h=8), qtm[:].rearrange("p (h d) -> p h d", h=8),
                                    8, 32, cos32[:, t, :], sin32[:, t, :], rtmp, [rbank[3], rtab], [rqtm])
                            S.op("act", "activation", dict(out=mixA[:, i, :], in_=bk(4)[:, :], func=AF.Silu), reads=[rbank[4]], writes=[rmixA[i]])
                            S.op("act", "activation", dict(out=GS[:, i, :], in_=bk(1)[:, 0:24], func=AF.Sigmoid), reads=[rbank[1]], writes=[rGS])
                        if r == 1:
                            emit_kcvc()
                        if blk + 1 < 8:
                            make_hT_b((blk + 1) * 4 + r, xn, hT, rxn, rhT, n_dve=0)
                        for idx in range(2):
                            S.op("pe", "transpose", dict(out=bk16(7)[:, idx * 128:(idx + 1) * 128], in_=ktm[:, idx * 128:(idx + 1) * 128], identity=identb[:]),
                                 reads=[rktm, rconst], writes=[rbank[7]])
                        for idx, (KT, rK) in enumerate(((KE, rKE), (KW, rKW))):
                            for g in range(2):
                                S.op("dve", "tensor_copy", dict(out=KT[0:64, g, t * 128:(t + 1) * 128],
                                                                in_=bk16(7)[g * 64:(g + 1) * 64, idx * 128:(idx + 1) * 128]),
                                     reads=[rbank[7]], writes=[rK])
                        if i is None:
                            continue
                        for pr in range(4):
                            S.op("pe", "transpose", dict(out=bk16(3)[:, pr * 128:(pr + 1) * 128], in_=qtm[:, pr * 128:(pr + 1) * 128],
                                                         identity=identb[:]), reads=[rqtm, rconst], writes=[rbank[3]])
                        qsrc = bk16(3)[:, 0:512].rearrange("p (a q) -> p a q", a=4)
                        for hf in range(2):
                            S.op("dve", "tensor_copy", dict(out=QT[0:64, hf:8:2, i * 128:(i + 1) * 128], in_=qsrc[hf * 64:(hf + 1) * 64, :, :]),
                                 reads=[rbank[3]], writes=[rQT[i]])
                dbg("QT", QT[0:64, :, :], [64, 8, NOWN * 128], BF16, reads=rQT)
                dbg("KE", KE[:, :, :], [128, 2, S_LEN], BF16, reads=[rKE])
                dbg("KW", KW[:, :, :], [64, 2, S_LEN], BF16, reads=[rKW])
                dbg("VS", VS[:, :, :, :], [128, NT, 2, 65], BF16, reads=[rVS])
                dbg("kcmpT", kcmpT, [128, S_LEN], BF16, reads=[rmixB])
                dbg("GS", GS[:, :, :], [128, NOWN, 24], reads=[rGS])
                S.flush()
                checkpoint()

            with contextlib.ExitStack() as st:
                W1 = [W1k, sb(st, "W1v", [128, 32, 128], BF16)]
                cst = sb(st, "cst", [128, 2])
                hid = sb(st, "hid", [128, 256], BF16); ctmp = sb(st, "ctmp", [128, 64])
                kcT = sb(st, "kcT", [64, 2, 256], BF16)
                RC = sb(st, "RC", [128, 2, 2, 128], BF16)
                m1 = sb(st, "m1", [128, 2, 64]); vcv = sb(st, "vcv", [128, 2]); cposi = sb(st, "cposi", [128, 2], I32)
                cposf = sb(st, "cposf", [128, 2]); kctm = sb(st, "kctm", [128, 64], BF16)
                rcst = Res("cst"); rhid = Res("hid"); rkcT = Res("kcT"); rRC = Res("RC"); rk2 = Res("kctm")
                vv_ = cvw1.rearrange("(l d) j -> d l j", d=64)
                S.dma("pool", W1[1][0:64, :, :], vv_, writes=[rW]); S.dma("pool", W1[1][64:128, :, :], vv_, writes=[rW])
                S.dma("sp", m1[:, 0, :], m1_d[0:128, :], writes=[rW]); S.dma("sp", m1[:, 1, :], m1_d[128:256, :], writes=[rW])
                v16 = valid_d.rearrange("(n s) -> n s", s=16); p16 = pos_i.rearrange("(n s) -> n s", s=16)
                S.dma("sp", vcv[:, :], validc_pt[:, :], writes=[rW])
                S.dma("sp", cposi[:, :], cpos_pt[:, :], writes=[rW])
                S.op("dve", "tensor_copy", dict(out=cposf[:], in_=cposi[:]), reads=[rW], writes=[rW])
                ccos = sb(st, "ccos", [128, 2, 32]); csin = sb(st, "csin", [128, 2, 32])
                cinv = sb(st, "cinv", [128, 32]); cang = sb(st, "cang", [128, 2, 32]); cki = sb(st, "cki", [128, 2, 32], I32)
                ckf = sb(st, "ckf", [128, 2, 32])
                S.dma("sp", cinv[:], inv32_d.partition_broadcast(128), writes=[rW])
                S.op("dve", "tensor_tensor", dict(out=cang[:], in0=cposf[:].unsqueeze(2).to_broadcast([128, 2, 32]),
                                                      in1=cinv[:].unsqueeze(1).to_broadcast([128, 2, 32]), op=ALU.mult), reads=[rW], writes=[rW])
                for tab, off in ((csin, 0.0), (ccos, PI / 2)):
                    S.op("dve", "tensor_scalar", dict(out=cki[:], in0=cang[:], scalar1=off, scalar2=1.0 / (2 * PI), op0=ALU.add, op1=ALU.mult),
                         reads=[rW], writes=[rW])
                    S.op("dve", "tensor_copy", dict(out=ckf[:], in_=cki[:]), reads=[rW], writes=[rW])
                    S.op("dve", "scalar_tensor_tensor", dict(out=ckf[:], in0=ckf[:], scalar=-2 * PI, in1=cang[:], op0=ALU.mult, op1=ALU.add),
                         reads=[rW], writes=[rW])
                    S.op("dve", "tensor_scalar", dict(out=ckf[:], in0=ckf[:], scalar1=off, scalar2=PI, op0=ALU.add, op1=ALU.min),
                         reads=[rW], writes=[rW])
                    S.op("dve", "tensor_scalar", dict(out=ckf[:], in0=ckf[:], scalar1=-PI, scalar2=None, op0=ALU.max), reads=[rW], writes=[rW])
                    S.op("act", "activation", dict(out=tab[:], in_=ckf[:], func=AF.Sin), reads=[rW], writes=[rW])
                for j in range(2):
                    for l in range(32):
                        S.op("pe", "matmul", dict(out=bk(0)[:, j:j + 1], lhsT=W1[j][0:64, l, :], rhs=posT[0:64, l:l + 1],
                                                             start=(l == 0), stop=(l == 31)), reads=[rW], writes=[rbank[0]])
                S.op("dve", "tensor_copy", dict(out=cst[:], in_=bk(0)[:, 0:2]), reads=[rbank[0]], writes=[rcst])
                S.op("dve", "memset", dict(ap=RC[:], constant=0.0), writes=[rRC])
                S.op("dve", "memset", dict(ap=kcT[:], constant=0.0), writes=[rkcT])
                cmpD = sb(st, "cmpD", [128, 2, 16, 256], BF16); rcmpD = Res("cmpD")
                for j, srcT in enumerate((kcmpT, vcmpT)):
                    S.op("dve", "tensor_copy", dict(out=cmpD[:, j, :, :], in_=srcT.rearrange("p (n s) -> p s n", s=16)), reads=[rmixB], writes=[rcmpD])
                HB = {(0, 0): 1, (0, 1): 3, (1, 0): 4, (1, 1): 5}
                hid2 = [hid, sb(st, "hid_b", [128, 256], BF16)]; rhid2 = [rhid, Res("hid_b")]
                for g in range(2):
                    rows = slice(g * 64, (g + 1) * 64)
                    for j in range(2):
                        hbk = HB[(g, j)]
                        for l in range(32):
                            S.op("pe", "matmul", dict(out=bk(hbk)[:, 0:255], lhsT=W1[j][rows, l, :],
                                                      rhs=cmpD[rows, j, l % 16, l // 16:l // 16 + 255],
                                                      start=(l == 0), stop=(l == 31)),
                                 reads=[rW, rcmpD], writes=[rbank[hbk]])
                for g in range(2):
                    for j in range(2):
                        hbk = HB[(g, j)]
                        hx = (2 * g + j) % 2
                        S.op("act", "activation", dict(out=hid2[hx][:, 0:255], in_=bk(hbk)[:, 0:255], func=AF.Silu, bias=cst[:, j:j + 1]),
                             reads=[rbank[hbk], rcst], writes=[rhid2[hx]])
                        for ch in range(2):
                            nn = 128 if ch == 0 else 127
                            ob = 2 if ch == 0 else 6
                            S.op("pe", "matmul", dict(out=bk(ob)[0:nn, 0:64], lhsT=hid2[hx][:, ch * 128:ch * 128 + nn], rhs=W2[j][:, :],
                                                      start=True, stop=True), reads=[rhid2[hx], rW], writes=[rbank[ob]])
                            if j == 0:
                                rope_tm(bk(ob)[0:nn, 0:64].rearrange("p (h d) -> p h d", h=1), kctm[0:nn, :].rearrange("p (h d) -> p h d", h=1),
                                        1, 32, ccos[0:nn, ch, :], csin[0:nn, ch, :], ctmp[0:nn, :], [rbank[ob], rW], [rk2])
                                S.op("pe", "transpose", dict(out=bk16(7)[0:64, 0:nn], in_=kctm[0:nn, :], identity=identb[0:nn, 0:nn]),
                                     reads=[rk2, rconst], writes=[rbank[7]])
                                S.op("dve", "tensor_copy", dict(out=kcT[:, g, ch * 128:ch * 128 + nn], in_=bk16(7)[0:64, 0:nn]),
                                     reads=[rbank[7]], writes=[rkcT])
                            else:
                                S.op("dve", "tensor_scalar", dict(out=RC[0:nn, ch, g, 0:64], in0=bk(ob)[0:nn, 0:64],
                                                                  scalar1=vcv[0:nn, ch:ch + 1], scalar2=None, op0=ALU.mult),
                                     reads=[rbank[ob], rW], writes=[rRC])
                for g in range(2):
                    for ch in range(2):
                        S.op("dve", "tensor_copy", dict(out=RC[:, ch, g, 64:65], in_=vcv[:, ch:ch + 1]), reads=[rW], writes=[rRC])
                        S.op("dve", "tensor_scalar", dict(out=RC[:, ch, g, 65:128], in0=m1[:, ch, 0:63], scalar1=vcv[:, ch:ch + 1],
                                                                       scalar2=None, op0=ALU.mult), reads=[rW], writes=[rRC])
                dbg("kcT", kcT[:, :, :], [64, 2, 256], BF16, reads=[rkcT])
                dbg("RC", RC[:, :, :, :], [128, 2, 2, 128], BF16, reads=[rRC])
                S.flush()
                checkpoint()

                cmask = sb(st, "cmask", [128, 2, NOWN * 128], BF16)
                iforce = sb(st, "iforce", [128, NOWN, 64])
                rcm = Res("cmask", multi=True)
                for ch in range(2):
                    S.dma("pool", cmask[:, ch, :], cmpmask_d[ch * 128:(ch + 1) * 128, :], writes=[rcm])
                S.dma("sp", iforce[:], impforce_d.rearrange("(i p) j -> p i j", p=128), writes=[rcm])
                Pt = [sb(st, f"Pt{i}", [128, 512], BF16) for i in range(6)]; rPt = [Res(f"Pt{i}") for i in range(6)]
                accS = [sb(st, f"accS{i}", [65, 512]) for i in range(2)]; raccS = [Res("accS0"), Res("accS1")]
                NPS = 4
                cmpS = [sb(st, f"cmpS{i}", [128, 4, 128]) for i in range(NPS)]; rcmpS = [Res(f"cmpS{i}") for i in range(NPS)]
                sm = [sb(st, f"sm{i}", [128, 16]) for i in range(NPS)]; rsm = [Res(f"sm{i}") for i in range(NPS)]
                imp = [sb(st, f"imp{i}", [128, 64]) for i in range(NPS)]; imp2 = [sb(st, f"imp2{i}", [128, 64]) for i in range(NPS)]
                imp3 = [sb(st, f"imp3{i}", [128, 64]) for i in range(NPS)]
                m16 = [sb(st, f"m16{i}", [128, 16]) for i in range(NPS)]
                selb = [sb(st, f"selb{i}", [128, 64], BF16) for i in range(NPS)]; rimp = [Res(f"imp{i}") for i in range(NPS)]
                ocmp = [sb(st, f"ocmp{i}", [128, 4, 64]) for i in range(NPS)]; rocmp = [Res(f"ocmp{i}") for i in range(NPS)]
                oacc = [sb(st, f"oacc{i}", [128, 4, 64]) for i in range(NPS)]; roacc = [Res(f"oacc{i}") for i in range(NPS)]
                wts = [sb(st, f"wts{i}", [128, 3, 4]) for i in range(NPS)]
                Pp = [sb(st, f"Pp{i}", [128, 512], BF16) for i in range(2)]; rPp = [Res("Pp0"), Res("Pp1")]
                for x_ in range(NPS):
                    S.op("dve", "memset", dict(ap=imp[x_][:], constant=0.0), writes=[rimp[x_]])

                def attn_units(units, sx, accb, group_starts=(0,), before_pv=None):
                    n = len(units)
                    before_pv = before_pv or {}

                    def emit_score(u):
                        b = 2 * sx + (u % 2)
                        S.op("pe", "matmul", dict(out=bk(b)[:, :], lhsT=units[u][0], rhs=units[u][1], start=True, stop=True),
                             reads=units[u][2], writes=[rbank[b]])
                    emit_score(0)
                    if n > 1:
                        emit_score(1)
                    for u in range(n):
                        _, _, _, maskt, scale, vl, vrd = units[u]
                        b = 2 * sx + (u % 2)
                        p = 3 * sx + (u % 3)
                        S.op("act", "activation", dict(out=Pt[p][:, :], in_=bk(b)[:, :], func=AF.Exp, scale=scale),
                             reads=[rbank[b]], writes=[rPt[p]])
                        if maskt is not None:
                            pv = Pt[p][:, :].rearrange("p (h q) -> p h q", h=4)
                            S.op("dve", "tensor_tensor", dict(out=pv, in0=pv, in1=maskt[:].unsqueeze(1).to_broadcast([128, 4, 128]), op=ALU.mult),
                                 reads=[rPt[p], rconst], writes=[rPt[p]])
                        if u + 2 < n:
                            emit_score(u + 2)
                        if u in before_pv:
                            before_pv[u]()
                        S.op("pe", "matmul", dict(out=bk(accb)[0:65, :], lhsT=vl, rhs=Pt[p][:, :],
                                                  start=(u in group_starts), stop=(u + 1 in group_starts or u == n - 1)),
                             reads=[rPt[p]] + vrd, writes=[rbank[accb]])
                        yield

                def finalize_T(accb, si):
                    S.op("dve", "tensor_copy", dict(out=accS[si][:, :], in_=bk(accb)[0:65, :]), reads=[rbank[accb]], writes=[raccS[si]])
                    for h in range(4):
                        S.op("pe", "transpose", dict(out=bk(6)[:, h * 65:(h + 1) * 65], in_=accS[si][:, h * 128:(h + 1) * 128],
                                                     identity=ident[0:65, 0:65]), reads=[raccS[si], rconst], writes=[rbank[6]])

                b7lock = {"owner": None}

                def nsa_stream(k, i, g):
                    sx = k % 2
                    ps = k % NPS
                    qt = 8 * (i // 4) + 4 + (i % 4)
                    qs = slice(i * 128, (i + 1) * 128)
                    hs = slice(4 * g, 4 * g + 4)
                    accb = 4 + sx
                    SM, IMP, IMP2, IMP3, M16, SELB = sm[ps], imp[ps], imp2[ps], imp3[ps], m16[ps], selb[ps]
                    OC, OA, WT, CS = ocmp[ps], oacc[ps], wts[ps], cmpS[ps]
                    rSM, rIMP, rOC, rOA, rCS = rsm[ps], rimp[ps], rocmp[ps], roacc[ps], rcmpS[ps]
                    nch = 2 if 8 * qt + 6 >= 128 else 1
                    while b7lock["owner"] not in (None, k):
                        yield
                    b7lock["owner"] = k
                    for ch in range(nch):
                        S.op("pe", "matmul", dict(out=bk(7)[:, :], lhsT=kcT[:, g, ch * 128:(ch + 1) * 128], rhs=QT[0:64, hs, qs],
                                                  start=True, stop=True), reads=[rkcT, rQT[i]], writes=[rbank[7]])
                        yield from idle(3)
                        S.op("act", "activation", dict(out=Pp[ch][:, :], in_=bk(7)[:, :], func=AF.Exp, scale=0.125),
                             reads=[rbank[7]], writes=[rPp[ch]])
                        yield from idle(3)
                        pv = Pp[ch][:, :].rearrange("p (h q) -> p h q", h=4)
                        S.op("dve", "tensor_tensor", dict(out=pv, in0=pv, in1=cmask[:, ch, qs].unsqueeze(1).to_broadcast([128, 4, 128]), op=ALU.mult),
                             reads=[rPp[ch], rcm], writes=[rPp[ch]])
                        yield
                    yield from idle(2)
                    for h in range(4):
                        for ch in range(nch):
                            S.op("pe", "matmul", dict(out=bk(7)[:, h * 128:(h + 1) * 128], lhsT=Pp[ch][:, h * 128:(h + 1) * 128], rhs=RC[:, ch, g, :],
                                                      start=(ch == 0), stop=(ch == nch - 1)), reads=[rPp[ch], rRC], writes=[rbank[7]])
                    yield from idle(3)
                    S.op("dve", "tensor_copy", dict(out=CS[:, :, :], in_=bk(7)[:, :].rearrange("p (h c) -> p h c", h=4)), reads=[rbank[7]], writes=[rCS])
                    b7lock["owner"] = None
                    yield
                    S.op("dve", "tensor_scalar", dict(out=SM[:, 0:4], in0=CS[:, :, 64], scalar1=1e-30, scalar2=None, op0=ALU.max), reads=[rCS], writes=[rSM])
                    S.op("dve", "reciprocal", dict(out=SM[:, 4:8], in_=SM[:, 0:4]), reads=[rSM], writes=[rSM])
                    glv0 = GS[:, i, 12 * g:12 * g + 12].rearrange("p (h b) -> p b h", b=3)[:, 0, :]
                    S.op("dve", "tensor_tensor", dict(out=WT[:, 0, :], in0=glv0, in1=SM[:, 4:8], op=ALU.mult), reads=[rGS, rSM], writes=[rSM])
                    S.op("dve", "tensor_tensor", dict(out=OA[:], in0=CS[:, :, 0:64], in1=WT[:, 0, :].unsqueeze(2).to_broadcast([128, 4, 64]), op=ALU.mult),
                         reads=[rCS, rSM], writes=[rOA])
                    yield
                    for h in range(4):
                        if h == 0:
                            S.op("dve", "tensor_scalar", dict(out=IMP[:, 0:63], in0=CS[:, h, 65:128], scalar1=SM[:, 4:5], scalar2=None, op0=ALU.mult),
                                 reads=[rCS, rSM], writes=[rIMP])
                        else:
                            S.op("dve", "scalar_tensor_tensor", dict(out=IMP[:, 0:63], in0=CS[:, h, 65:128], scalar=SM[:, 4 + h:5 + h], in1=IMP[:, 0:63],
                                                                     op0=ALU.mult, op1=ALU.add), reads=[rCS, rSM, rIMP], writes=[rIMP])
                    yield
                    S.op("dve", "tensor_tensor", dict(out=IMP2[:], in0=IMP[:], in1=iforce[:, i, :], op=ALU.max), reads=[rIMP, rcm], writes=[rIMP])
                    S.op("dve", "max", dict(out=M16[:, 0:8], in_=IMP2[:]), reads=[rIMP], writes=[rIMP])
                    S.op("dve", "match_replace", dict(out=IMP3[:], in_to_replace=M16[:, 0:8], in_values=IMP2[:], imm_value=-1e30),
                         reads=[rIMP], writes=[rIMP])
                    yield
                    S.op("dve", "max", dict(out=M16[:, 8:16], in_=IMP3[:]), reads=[rIMP], writes=[rIMP])
                    S.op("dve", "tensor_scalar", dict(out=SELB[:], in0=IMP2[:], scalar1=M16[:, 15:16], scalar2=-30000.0, op0=ALU.is_lt, op1=ALU.mult),
                         reads=[rIMP], writes=[rIMP])
                    yield from idle(6)
                    while b7lock["owner"] not in (None, k):
                        yield
                    S.op("pe", "transpose", dict(out=bk16(7)[0:64, 0:128], in_=SELB[:, :], identity=identb[:]), reads=[rIMP, rconst], writes=[rbank[7]])
                    S.op("dve", "tensor_copy", dict(out=QT[64:128, hs, qs], in_=bk16(7)[0:64, 0:128].unsqueeze(1).to_broadcast([64, 4, 128])),
                         reads=[rbank[7]], writes=[rQS[i][g]])
                    yield "GATE"
                    units = []
                    for kt in range(qt + 1):
                        ks = slice(kt * 128, (kt + 1) * 128)
                        units.append((KE[:, g, ks], QT[:, hs, qs], [rKE, rQT[i], rQS[i][g]],
                                      trile if kt == qt else None, 0.125, VS[:, kt, g, :], [rVS]))
                    n_slc = len(units)
                    glv = GS[:, i, 12 * g:12 * g + 12].rearrange("p (h b) -> p b h", b=3)

                    def slc_finalize_rest():
                        for h_ in range(4):
                            S.op("pe", "transpose", dict(out=bk(6)[:, h_ * 65:(h_ + 1) * 65], in_=accS[sx][:, h_ * 128:(h_ + 1) * 128],
                                                         identity=ident[0:65, 0:65]), reads=[raccS[sx], rconst], writes=[rbank[6]])
                        S.op("dve", "reciprocal", dict(out=SM[:, 12:16], in_=bk(6)[:, 64:260:65]), reads=[rbank[6]], writes=[rSM])
                        S.op("dve", "tensor_tensor", dict(out=WT[:, 1, :], in0=glv[:, 1, :], in1=SM[:, 12:16], op=ALU.mult), reads=[rGS, rSM], writes=[rSM])
                        S.op("dve", "tensor_tensor", dict(out=OC[:], in0=bk(6)[:, 0:260].rearrange("p (h d) -> p h d", h=4)[:, :, 0:64],
                                                          in1=WT[:, 1, :].unsqueeze(2).to_broadcast([128, 4, 64]), op=ALU.mult),
                             reads=[rbank[6], rSM, rOA], writes=[rOC])
                        S.op("dve", "tensor_tensor", dict(out=OA[:], in0=OA[:], in1=OC[:], op=ALU.add), reads=[rOC, rOA], writes=[rOA])

                    for kt in range(qt - 4, qt + 1):
                        ks = slice(kt * 128, (kt + 1) * 128)
                        mk = trile if kt == qt else (trigt if kt == qt - 4 else None)
                        units.append((KW[:, g, ks], QT[0:64, hs, qs], [rKW, rQT[i]], mk, 0.125, VW[:, kt, g, :], [rVW]))

                    def copy_out():
                        S.op("dve", "tensor_copy", dict(out=accS[sx][:, :], in_=bk(accb)[0:65, :]), reads=[rbank[accb]], writes=[raccS[sx]])

                    for n_, _ in enumerate(attn_units(units, sx, accb, group_starts=(0, n_slc), before_pv={n_slc: copy_out})):
                        yield
                        if n_ == n_slc:
                            slc_finalize_rest()
                            yield
                    S.op("dve", "tensor_copy", dict(out=accS[sx][:, :], in_=bk(accb)[0:65, :]), reads=[rbank[accb]], writes=[raccS[sx]])
                    yield "RELEASE"
                    yield from idle(3)
                    for h_ in range(4):
                        S.op("pe", "transpose", dict(out=bk(6)[:, h_ * 65:(h_ + 1) * 65], in_=accS[sx][:, h_ * 128:(h_ + 1) * 128],
                                                     identity=ident[0:65, 0:65]), reads=[raccS[sx], rconst], writes=[rbank[6]])
                    S.op("dve", "reciprocal", dict(out=SM[:, 12:16], in_=bk(6)[:, 64:260:65]), reads=[rbank[6]], writes=[rSM])
                    S.op("dve", "tensor_tensor", dict(out=WT[:, 2, :], in0=glv[:, 2, :], in1=SM[:, 12:16], op=ALU.mult), reads=[rGS, rSM], writes=[rSM])
                    S.op("dve", "tensor_tensor", dict(out=OC[:], in0=bk(6)[:, 0:260].rearrange("p (h d) -> p h d", h=4)[:, :, 0:64],
                                                      in1=WT[:, 2, :].unsqueeze(2).to_broadcast([128, 4, 64]), op=ALU.mult),
                         reads=[rbank[6], rSM, rOA], writes=[rOC])
                    S.op("dve", "tensor_tensor", dict(out=OA[:], in0=OA[:], in1=OC[:], op=ALU.add), reads=[rOC, rOA], writes=[rOA])
                    mv = mixA[:, i, 256 * g:256 * g + 256].rearrange("p (h d) -> p h d", h=4)
                    S.op("dve", "tensor_tensor", dict(out=mv, in0=mv, in1=OA[:], op=ALU.mult), reads=[rOA, rmixA[i]], writes=[rmixA[i]])
                    yield

                order = [x for j in range(8) for g in range(2) for x in ((j, g), (j + 8, g))]
                run_staggered([nsa_stream(k, i, g) for k, (i, g) in enumerate(order)], 3, max_active=4, gate_key=lambda k: k % 2)
                dbg("mixA", mixA[:, :, :], [128, NOWN, 512], BF16, reads=rmixA)
                S.flush()
                checkpoint()

        wo = sb(gst, "wo", [128, 8, D], BF16); rwo = Res("wo", multi=True)
        wo_v = w_out.rearrange("(c p) n -> p c n", p=128)
        with contextlib.ExitStack() as st:
            cos16, sin16, rtab16 = rope_tables(st, 16, "r16")
            wm = sb(st, "wm", [128, 8, MLA_COLS], BF16); rwm = Res("wm", multi=True)
            w_v = w_in.rearrange("(c p) n -> p c n", p=128)
            for c in range(8):
                S.dma("pool", wm[:, c, :], w_v[:, c, NSA_COLS:IN_W], writes=[rwm])
            for c in range(8):
                S.dma("pool", wo[:, c, :], wo_v[:, c, :], writes=[rwo])
            cqnT = sb(st, "cqnT", [128, 2, NOWN * 128], BF16); ckvnT = sb(st, "ckvnT", [128, S_LEN], BF16)
            KH = [sb(st, f"KH{i}", [96, S_LEN], BF16) for i in range(2)]
            VH = [sb(st, f"VH{i}", [128, NT, 65], BF16) for i in range(2)]
            rcqn = Res("cqnT"); rckv = Res("ckvnT"); rKHrot = [Res("KHrot0"), Res("KHrot1")]; rKH = [Res("KH0"), Res("KH1")]
            rVH = [Res("VH0"), Res("VH1")]
            wq = sb(st, "wq", [128, 2, 768], BF16); wqp = sb(st, "wqp", [128, 2, 8, 96], BF16); wkv = sb(st, "wkv", [128, 1024], BF16)
            wqf = sb(st, "wqf", [128, 2, 768]); wkvf = sb(st, "wkvf", [128, 1024]); gq = sb(st, "gq", [128, 2]); gkv = sb(st, "gkv", [128, 1])
            rwq = Res("wq", multi=True)
            S.dma("sp", wqf[:], w_q_up.rearrange("(c p) n -> p c n", p=128), writes=[rwq])
            S.dma("sp", wkvf[:], w_kv_up[:, :], writes=[rwq])
            S.dma("sp", gq[:], gq_col[:, :], writes=[rwq])
            S.dma("sp", gkv[:], gkv_col[:, :], writes=[rwq])
            for c in range(2):
                S.op("dve", "tensor_scalar", dict(out=wq[:, c, :], in0=wqf[:, c, :], scalar1=gq[:, c:c + 1], scalar2=None, op0=ALU.mult),
                     reads=[rwq], writes=[rwq])
                wv4 = wq[:, c, :].rearrange("p (h d) -> p h d", h=8)
                S.op("dve", "tensor_copy", dict(out=wqp[:, c, :, 0:64], in_=wv4[:, :, 0:64]), reads=[rwq], writes=[rwq])
                S.op("dve", "tensor_copy", dict(out=wqp[:, c, :, 64:80], in_=wv4[:, :, 80:96]), reads=[rwq], writes=[rwq])
                S.op("dve", "tensor_copy", dict(out=wqp[:, c, :, 80:96], in_=wv4[:, :, 64:80]), reads=[rwq], writes=[rwq])
            S.op("dve", "tensor_scalar", dict(out=wkv[:], in0=wkvf[:], scalar1=gkv[:, 0:1], scalar2=None, op0=ALU.mult), reads=[rwq], writes=[rwq])
            for hb in range(2):
                S.op("dve", "tensor_copy", dict(out=VH[hb][:, :, 64], in_=valid[:]), reads=[rconst], writes=[rVH[hb]])

            cosF = sb(st, "cosF", [128, NOWN * 128]); sinF = sb(st, "sinF", [128, NOWN * 128])
            rF = Res("ropeF")
            with contextlib.ExitStack() as t2:
                pFi = sb(t2, "pFi", [128, NOWN * 128], I32); angF = sb(t2, "angF", [128, NOWN * 128])
                kFi = sb(t2, "kFi", [128, NOWN * 128], I32); invF = sb(t2, "invF", [128, 1]); sgnF = sb(t2, "sgnF", [128, 1])
                rq = Res("pF", multi=True)
                for s_ in range(4):
                    S.dma("sp", pFi[64:96, s_ * 512:(s_ + 1) * 512],
                          pos_i[(2 * s_ + 1) * 512:(2 * s_ + 2) * 512].partition_broadcast(32), writes=[rq])
                S.dma("sp", invF[:], invF_d[:, :], writes=[rq]); S.dma("sp", sgnF[:], sgnF_d[:, :], writes=[rq])
                R = slice(64, 96)
                S.op("dve", "tensor_copy", dict(out=angF[R, :], in_=pFi[R, :]), reads=[rq], writes=[rq])
                S.op("dve", "tensor_scalar", dict(out=angF[R, :], in0=angF[R, :], scalar1=invF[R, 0:1], scalar2=None, op0=ALU.mult), reads=[rq], writes=[rq])
                for tab, off in ((sinF, 0.0), (cosF, PI / 2)):
                    S.op("dve", "tensor_scalar", dict(out=kFi[R, :], in0=angF[R, :], scalar1=off, scalar2=1.0 / (2 * PI), op0=ALU.add, op1=ALU.mult),
                         reads=[rq], writes=[rq])
                    S.op("dve", "tensor_copy", dict(out=tab[R, :], in_=kFi[R, :]), reads=[rq], writes=[rq, rF])
                    S.op("dve", "scalar_tensor_tensor", dict(out=tab[R, :], in0=tab[R, :], scalar=-2 * PI, in1=angF[R, :], op0=ALU.mult, op1=ALU.add),
                         reads=[rq], writes=[rq, rF])
                    S.op("dve", "tensor_scalar", dict(out=tab[R, :], in0=tab[R, :], scalar1=off, scalar2=PI, op0=ALU.add, op1=ALU.min),
                         reads=[rq], writes=[rq, rF])
                    S.op("dve", "tensor_scalar", dict(out=tab[R, :], in0=tab[R, :], scalar1=-PI, scalar2=None, op0=ALU.max), reads=[rq], writes=[rq, rF])
                    S.op("act", "activation", dict(out=tab[R, :], in_=tab[R, :], func=AF.Sin), reads=[rq], writes=[rq, rF])
                S.op("dve", "tensor_scalar", dict(out=sinF[R, :], in0=sinF[R, :], scalar1=sgnF[R, 0:1], scalar2=None, op0=ALU.mult), reads=[rq], writes=[rq, rF])
                S.flush()
                checkpoint()

            with contextlib.ExitStack() as t1:
                xt = [sb(t1, f"xt{i}", [128, D]) for i in range(2)]; rxt = [Res("xt0"), Res("xt1")]
                xn = [sb(t1, f"xn{i}", [128, D], BF16) for i in range(2)]; rxn = [Res("xn0"), Res("xn1")]
                hT = [sb(t1, f"hT{i}", [128, 8, 512], BF16) for i in range(2)]; rhT = [[Res(f"hT{i}_{r}") for r in range(4)] for i in range(2)]
                junk = sb(t1, "junk", [128, 256], BF16)
                rtmp = sb(t1, "ropetmp", [128, 64]); latn = [sb(t1, f"latn{i}", [128, 416], BF16) for i in range(2)]; rlat = [Res("latn0"), Res("latn1")]
                st3 = [sb(t1, f"st3{i}", [128, 2, 2]) for i in range(2)]; rst3 = [Res("st30"), Res("st31")]
                LBS = ((0, 1), (2, 4))
                for r in range(4):
                    make_hT(r, xt, xn, hT, rxt, rxn, rhT)

                def stage1(hh):
                    sq = st3[hh % 2]; rsq = rst3[hh % 2]
                    for j in range(2):
                        t = 2 * hh + j
                        blk, r = divmod(t, 4)
                        hb = blk % 2
                        i = own_index(t)
                        lbk = LBS[hh % 2][j]
                        if blk + 1 < 8:
                            make_hT_a(t + 4, xt, xn, rxt, rxn)
                        for c in range(8):
                            S.op("pe", "matmul", dict(out=bk(lbk)[:, 0:416], lhsT=hT[hb][:, c, r * 128:(r + 1) * 128], rhs=wm[:, c, 0:416],
                                                      start=(c == 0), stop=(c == 7)), reads=[rwm, rhT[hb][r]], writes=[rbank[lbk]])
                        for k_, (a_, n_) in enumerate(((0, 256), (256, 128))):
                            S.op("act", "activation", dict(out=junk[:, 0:n_], in_=bk(lbk)[:, a_:a_ + n_], func=AF.Square, accum_out=sq[:, k_, j:j + 1]),
                                 reads=[rbank[lbk]], writes=[rsq])
                        if i is not None:
                            for c in range(8):
                                S.op("pe", "matmul", dict(out=bk(3)[:, :], lhsT=hT[hb][:, c, r * 128:(r + 1) * 128], rhs=wm[:, c, 416:928],
                                                          start=(c == 0), stop=(c == 7)), reads=[rwm, rhT[hb][r]], writes=[rbank[3]])
                            S.op("act", "activation", dict(out=mixB[:, i, :], in_=bk(3)[:, :], func=AF.Silu), reads=[rbank[3]], writes=[rmixB, rmixBt[i]])
                        if blk + 1 < 8:
                            make_hT_b(t + 4, xn, hT, rxn, rhT, n_dve=6)

                def stats(hh):
                    sq = st3[hh % 2]; rsq = rst3[hh % 2]
                    for k_, n_ in enumerate((256, 128)):
                        S.op("dve", "tensor_scalar", dict(out=sq[:, k_, :], in0=sq[:, k_, :], scalar1=1.0 / n_, scalar2=1e-6, op0=ALU.mult, op1=ALU.add),
                             reads=[rsq], writes=[rsq])
                    S.op("act", "activation", dict(out=sq[:, :, :], in_=sq[:, :, :], func=AF.Sqrt), reads=[rsq], writes=[rsq])
                    S.op("dve", "reciprocal", dict(out=sq[:, :, :], in_=sq[:, :, :]), reads=[rsq], writes=[rsq])

                def stage2(hh):
                    sq = st3[hh % 2]; rsq = rst3[hh % 2]
                    for j in range(2):
                        t = 2 * hh + j
                        i = own_index(t)
                        lb = j
                        lbk = LBS[hh % 2][j]
                        ps = bk(lbk)
                        if i is not None:
                            S.op("dve", "tensor_scalar", dict(out=latn[lb][:, 0:256], in0=ps[:, 0:256], scalar1=sq[:, 0, j:j + 1], scalar2=None, op0=ALU.mult),
                                 reads=[rbank[lbk], rsq], writes=[rlat[lb]])
                        S.op("dve", "tensor_scalar", dict(out=latn[lb][:, 256:384], in0=ps[:, 256:384], scalar1=sq[:, 1, j:j + 1], scalar2=None, op0=ALU.mult),
                             reads=[rbank[lbk], rsq], writes=[rlat[lb]])
                        rope_tm(ps[:, 384:416].rearrange("p (h d) -> p h d", h=1), latn[lb][:, 384:416].rearrange("p (h d) -> p h d", h=1),
                                1, 16, cos16[:, t, :], sin16[:, t, :], rtmp, [rbank[lbk], rtab16], [rlat[lb]])
                        S.op("pe", "transpose", dict(out=bk16(7)[:, 0:128], in_=latn[lb][:, 256:384], identity=identb[:]), reads=[rlat[lb], rconst], writes=[rbank[7]])
                        S.op("pe", "transpose", dict(out=bk16(7)[0:32, 128:256], in_=latn[lb][:, 384:416], identity=identb[:]), reads=[rlat[lb], rconst], writes=[rbank[7]])
                        if i is not None:
                            for c in range(2):
                                S.op("pe", "transpose", dict(out=bk16(7)[:, 256 + c * 128:384 + c * 128], in_=latn[lb][:, c * 128:(c + 1) * 128], identity=identb[:]),
                                     reads=[rlat[lb], rconst], writes=[rbank[7]])
                        S.op("dve", "tensor_copy", dict(out=ckvnT[:, t * 128:(t + 1) * 128], in_=bk16(7)[:, 0:128]), reads=[rbank[7]], writes=[rckv])
                        for hb2 in range(2):
                            S.op("dve", "tensor_copy", dict(out=KH[hb2][64:96, t * 128:(t + 1) * 128], in_=bk16(7)[0:32, 128:256]),
                                 reads=[rbank[7]], writes=[rKHrot[hb2]])
                        if i is not None:
                            S.op("dve", "tensor_copy", dict(out=cqnT[:, :, i * 128:(i + 1) * 128], in_=bk16(7)[:, 256:512].rearrange("p (c q) -> p c q", c=2)),
                                 reads=[rbank[7]], writes=[rcqn])

                for hh in range(16):
                    stage1(hh)
                    if hh >= 1:
                        stage2(hh - 1)
                    stats(hh)
                stage2(15)
                dbg("ckvnT", ckvnT[:, :], [128, S_LEN], BF16, reads=[rckv])
                dbg("cqnT", cqnT[:, :, :], [128, 2, NOWN * 128], BF16, reads=[rcqn])
                dbg("krot", KH[0][64:96, :], [32, S_LEN], BF16, reads=[rKHrot[0]])
                S.flush()
                checkpoint()

            with contextlib.ExitStack() as t1:
                Pt = [sb(t1, f"Pt{i}", [128, 512], BF16) for i in range(6)]; rPt = [Res(f"Pt{i}") for i in range(6)]
                accS = [sb(t1, f"accS{i}", [65, 512]) for i in range(2)]; raccS = [Res("accS0"), Res("accS1")]
                qh2 = [[sb(t1, f"qh{i}_{j}", [96, 512], BF16) for j in range(2)] for i in range(2)]; rqh2 = [[Res(f"qh{i}_{j}") for j in range(2)] for i in range(2)]
                qtmp = [sb(t1, f"qtmp{i}", [128, 512]) for i in range(2)]; rqtmp = [Res("qtmp0"), Res("qtmp1")]
                qtmp2 = [sb(t1, f"qtmpb{i}", [128, 512]) for i in range(2)]; rqtmp2 = [Res("qtmpb0"), Res("qtmpb1")]
                sm = [sb(t1, f"sm{i}", [128, 16]) for i in range(2)]; rsm = [Res("sm0"), Res("sm1")]
                otmp = [sb(t1, f"otmp{i}", [128, 4, 64]) for i in range(2)]; rot = [Res("otmp0"), Res("otmp1")]
                SC = float(96 ** -0.5)
                R = slice(64, 96)

                def mla_stream(h):
                    hb = h % 2
                    accb = 4 + hb
                    for blk in range(8):
                        gb = 7 - (blk % 2)
                        S.op("pe", "matmul", dict(out=bk(gb)[0:64, :], lhsT=wkv[:, h * 128:h * 128 + 64], rhs=ckvnT[:, blk * 512:(blk + 1) * 512],
                                                  start=True, stop=True), reads=[rwq, rckv], writes=[rbank[gb]])
                        S.op("dve", "tensor_copy", dict(out=KH[hb][0:64, blk * 512:(blk + 1) * 512], in_=bk(gb)[0:64, :]),
                             reads=[rbank[gb]], writes=[rKH[hb]])
                        yield
                    for t8 in range(4):
                        gb = 7 - (t8 % 2)
                        for tt in range(8):
                            t = t8 * 8 + tt
                            S.op("pe", "matmul", dict(out=bk(gb)[:, tt * 64:(tt + 1) * 64], lhsT=ckvnT[:, t * 128:(t + 1) * 128],
                                                      rhs=wkv[:, h * 128 + 64:h * 128 + 128], start=True, stop=True),
                                 reads=[rwq, rckv], writes=[rbank[gb]])
                        if t8 == 0:
                            for tt in range(4):
                                S.op("dve", "tensor_scalar", dict(out=VH[hb][:, tt, 0:64], in0=bk(gb)[:, tt * 64:(tt + 1) * 64], scalar1=valid[:, tt:tt + 1],
                                                                  scalar2=None, op0=ALU.mult), reads=[rbank[gb], rconst], writes=[rVH[hb]])
                            S.op("dve", "tensor_copy", dict(out=VH[hb][:, 4:8, 0:64], in_=bk(gb)[:, 256:512].rearrange("p (t d) -> p t d", t=4)),
                                 reads=[rbank[gb]], writes=[rVH[hb]])
                        else:
                            S.op("dve", "tensor_copy", dict(out=VH[hb][:, t8 * 8:(t8 + 1) * 8, 0:64], in_=bk(gb)[:, :].rearrange("p (t d) -> p t d", t=8)),
                                 reads=[rbank[gb]], writes=[rVH[hb]])
                        yield
                    def qgen(s_):
                        cs = slice(s_ * 512, (s_ + 1) * 512)
                        qd = qh2[hb][s_ % 2]; rqd = rqh2[hb][s_ % 2]
                        for c in range(2):
                            S.op("pe", "matmul", dict(out=bk(7)[0:96, :], lhsT=wq[:, c, h * 96:(h + 1) * 96], rhs=cqnT[:, c, cs],
                                                      start=(c == 0), stop=(c == 1)), reads=[rwq, rcqn], writes=[rbank[7]])
                        for c in range(2):
                            S.op("pe", "matmul", dict(out=bk(6)[0:96, :], lhsT=wqp[:, c, h, :], rhs=cqnT[:, c, cs],
                                                      start=(c == 0), stop=(c == 1)), reads=[rwq, rcqn], writes=[rbank[6]])
                        S.op("dve", "tensor_copy", dict(out=qd[0:64, :], in_=bk(7)[0:64, :]), reads=[rbank[7]], writes=[rqd])
                        S.op("dve", "tensor_tensor", dict(out=qtmp[hb][R, :], in0=bk(7)[R, :], in1=cosF[R, cs], op=ALU.mult), reads=[rbank[7], rF], writes=[rqtmp[hb]])
                        S.op("dve", "tensor_tensor", dict(out=qtmp2[hb][R, :], in0=bk(6)[R, :], in1=sinF[R, cs], op=ALU.mult), reads=[rbank[6], rF], writes=[rqtmp2[hb]])
                        S.op("dve", "tensor_tensor", dict(out=qd[R, :], in0=qtmp[hb][R, :], in1=qtmp2[hb][R, :], op=ALU.add),
                             reads=[rqtmp[hb], rqtmp2[hb]], writes=[rqd])

                    def fin_rest(s_):
                        for r in range(4):
                            S.op("pe", "transpose", dict(out=bk(6)[:, r * 65:(r + 1) * 65], in_=accS[hb][:, r * 128:(r + 1) * 128], identity=ident[0:65, 0:65]),
                                 reads=[raccS[hb], rconst], writes=[rbank[6]])
                        S.op("dve", "reciprocal", dict(out=sm[hb][:, 4:8], in_=bk(6)[:, 64:260:65]), reads=[rbank[6]], writes=[rsm[hb]])
                        S.op("dve", "tensor_tensor", dict(out=otmp[hb][:], in0=bk(6)[:, 0:260].rearrange("p (r d) -> p r d", r=4)[:, :, 0:64],
                                                          in1=sm[hb][:, 4:8].unsqueeze(2).to_broadcast([128, 4, 64]), op=ALU.mult),
                             reads=[rbank[6], rsm[hb]], writes=[rot[hb]])
                        mvb = mixB[:, s_ * 4:(s_ + 1) * 4, h * 64:(h + 1) * 64]
                        rmv = [rmixBt[s_ * 4 + r] for r in range(4)]
                        S.op("dve", "tensor_tensor", dict(out=mvb, in0=mvb, in1=otmp[hb][:, :, :], op=ALU.mult), reads=[rot[hb]] + rmv, writes=rmv)

                    qgen(0)
                    yield from idle(4)
                    flat = [(s_, u) for s_ in range(4) for u in range(8 * s_ + 8)]
                    nflat = len(flat)

                    def emit_score(gi):
                        s_, u = flat[gi]
                        nk = 8 * s_ + 8
                        b = 2 * hb + (gi % 2)
                        c0 = max(0, u - (nk - 4)) * 128
                        qd = qh2[hb][s_ % 2]; rqd = rqh2[hb][s_ % 2]
                        S.op("pe", "matmul", dict(out=bk(b)[:, c0:512], lhsT=KH[hb][:, u * 128:(u + 1) * 128], rhs=qd[:, c0:512],
                                                  start=True, stop=True), reads=[rKH[hb], rKHrot[hb], rqd], writes=[rbank[b]])
                    emit_score(0)
                    emit_score(1)
                    for gi in range(nflat):
                        s_, u = flat[gi]
                        nk = 8 * s_ + 8
                        b = 2 * hb + (gi % 2)
                        p = 3 * hb + (gi % 3)
                        c0 = max(0, u - (nk - 4)) * 128
                        S.op("act", "activation", dict(out=Pt[p][:, c0:512], in_=bk(b)[:, c0:512], func=AF.Exp, scale=SC),
                             reads=[rbank[b]], writes=[rPt[p]])
                        if u >= nk - 4:
                            S.op("dve", "tensor_tensor", dict(out=Pt[p][:, c0:c0 + 128], in0=Pt[p][:, c0:c0 + 128], in1=trile[:], op=ALU.mult),
                                 reads=[rPt[p], rconst], writes=[rPt[p]])
                        if gi + 2 < nflat:
                            emit_score(gi + 2)
                        if u == 0 and s_ > 0:
                            S.op("dve", "tensor_copy", dict(out=accS[hb][:, :], in_=bk(accb)[0:65, :]), reads=[rbank[accb]], writes=[raccS[hb]])
                        S.op("pe", "matmul", dict(out=bk(accb)[0:65, c0:512], lhsT=VH[hb][:, u, :], rhs=Pt[p][:, c0:512],
                                                  start=(u == 0), stop=(u == nk - 1)), reads=[rPt[p], rVH[hb]], writes=[rbank[accb]])
                        yield
                        if u == 1 and s_ > 0:
                            fin_rest(s_ - 1)
                            yield
                        if u == 3 and s_ + 1 < 4:
                            qgen(s_ + 1)
                            yield
                    S.op("dve", "tensor_copy", dict(out=accS[hb][:, :], in_=bk(accb)[0:65, :]), reads=[rbank[accb]], writes=[raccS[hb]])
                    yield from idle(3)
                    fin_rest(3)
                    yield

                run_staggered([mla_stream(h) for h in range(8)], 44, max_active=2, gate_key=lambda k: k)
                dbg("mixB", mixB[:, :, :], [128, NOWN, 512], BF16, reads=rmixBt)
                S.flush()
                checkpoint()

        with contextlib.ExitStack() as st:
            xt = [sb(st, f"xo{i}", [128, D]) for i in range(2)]; rxt = [Res("xo0"), Res("xo1")]
            mT = [sb(st, f"mT{i}", [128, 8, 128], BF16) for i in range(2)]; rmT = [Res("mT0"), Res("mT1")]
            yo = [sb(st, f"yo{i}", [128, D]) for i in range(2)]; ryo = [Res("yo0"), Res("yo1")]
            junk = sb(st, "junk5", [128, D], BF16); st5 = [sb(st, f"st5{i}", [128, 4]) for i in range(2)]; rst5 = [Res("st50"), Res("st51")]

            def p5_front(i):
                b = i % 2
                t = 8 * (i // 4) + 4 + (i % 4)
                tb = 7 - b
                S.dma("sp", xt[b][:], xs[t * 128:(t + 1) * 128, :], writes=[rxt[b]])
                for c in range(8):
                    src = mixA[:, i, c * 128:(c + 1) * 128] if c < 4 else mixB[:, i, (c - 4) * 128:(c - 3) * 128]
                    S.op("pe", "transpose", dict(out=bk16(tb)[:, c * 128:(c + 1) * 128], in_=src, identity=identb[:]),
                         reads=[rmixA[i], rmixBt[i], rconst], writes=[rbank[tb]])
                S.op("act", "activation", dict(out=mT[b][:, :, :], in_=bk16(tb)[:, :].rearrange("p (c q) -> p c q", c=8), func=AF.Copy),
                     reads=[rbank[tb]], writes=[rmT[b]])

            p5_front(0)
            for i in range(NOWN):
                b = i % 2
                if i + 1 < NOWN:
                    p5_front(i + 1)
                for hh in range(2):
                    ob = 2 * b + hh
                    for c in range(8):
                        S.op("pe", "matmul", dict(out=bk(ob)[:, :], lhsT=mT[b][:, c, :], rhs=wo[:, c, hh * 512:(hh + 1) * 512],
                                                  start=(c == 0), stop=(c == 7)), reads=[rmT[b], rwo], writes=[rbank[ob]])
                    hsl = slice(hh * 512, (hh + 1) * 512)
                    S.op("dve", "tensor_tensor", dict(out=yo[b][:, hsl], in0=bk(ob)[:, :], in1=gate_bc[:, hsl], op=ALU.mult),
                         reads=[rbank[ob], rmod], writes=[ryo[b]])
                S.op("dve", "tensor_tensor", dict(out=yo[b][:, :], in0=yo[b][:, :], in1=xt[b][:, :], op=ALU.add), reads=[ryo[b], rxt[b]], writes=[ryo[b]])
                S.op("act", "activation", dict(out=junk[:], in_=yo[b][:], func=AF.Square, accum_out=st5[b][:, 0:1]), reads=[ryo[b]], writes=[rst5[b]])
                S.op("dve", "tensor_scalar", dict(out=st5[b][:, 1:2], in0=st5[b][:, 0:1], scalar1=1.0 / D, scalar2=1e-6, op0=ALU.mult, op1=ALU.add), reads=[rst5[b]], writes=[rst5[b]])
                S.op("act", "activation", dict(out=st5[b][:, 2:3], in_=st5[b][:, 1:2], func=AF.Sqrt), reads=[rst5[b]], writes=[rst5[b]])
                S.op("dve", "reciprocal", dict(out=st5[b][:, 2:3], in_=st5[b][:, 2:3]), reads=[rst5[b]], writes=[rst5[b]])
                S.op("dve", "scalar_tensor_tensor", dict(out=yo[b][:, :], in0=yo[b][:, :], scalar=st5[b][:, 2:3], in1=fing_bc[:, :], op0=ALU.mult, op1=ALU.mult),
                     reads=[ryo[b], rst5[b], rconst], writes=[ryo[b]])
                S.dma("sp", out_d[i * 128:(i + 1) * 128, :], yo[b][:, :], reads=[ryo[b]])
            S.wait_all_dma("sp")
            S.flush()
            checkpoint()
    return nc, dbg_out


def _constants(par):
    delta = 1 - par
    c = {}
    c["inv32"] = (10000.0 ** (-np.arange(32, dtype=np.float32) / 32)).astype(np.float32)
    c["inv16"] = (10000.0 ** (-np.arange(16, dtype=np.float32) / 16)).astype(np.float32)
    invF = np.zeros((128, 1), np.float32); sgnF = np.zeros((128, 1), np.float32)
    for r in range(64, 96):
        invF[r, 0] = c["inv16"][(r - 64) % 16]
        sgnF[r, 0] = -1.0 if r < 80 else 1.0
    c["invF"] = invF; c["sgnF"] = sgnF
    k = np.arange(S_LEN)
    c["E_aug"] = (k[None, :] // 64 == np.arange(64)[:, None]).astype(np.float32)
    c["ident"] = np.eye(128, dtype=np.float32)
    kk = np.arange(128)[:, None]; qq = np.arange(128)[None, :]
    c["tri_le"] = (kk <= qq).astype(np.float32)
    c["tri_gt"] = (kk > qq).astype(np.float32)
    own_t = np.concatenate([np.arange((2 * s + 1) * 512, (2 * s + 2) * 512) for s in range(4)])
    n = np.arange(256)[:, None]
    c["cmpmask"] = ((16 * n + 31 <= own_t[None, :]) & (n < 255)).astype(np.float32)
    j = np.arange(64)[None, :]
    cur = (own_t // 64)[:, None]
    jr = j - 8 * delta; curr = cur - 8 * delta
    forced = (jr == 0) | (jr == curr) | (jr == curr - 1)
    keep = np.ones((NOWN * 128, 64), np.float32); force = np.zeros((NOWN * 128, 64), np.float32)
    fut = jr > curr
    dummy = jr < 0
    force[forced & ~dummy & ~fut] = 1.0e4
    c["impkeep"] = keep; c["impforce"] = force
    start = np.arange(256)[:, None] * 16; bstart = np.arange(64)[None, :] * 64
    ov = np.minimum(start + 32, bstart + 64) - np.maximum(start, bstart)
    m1 = (np.clip(ov, 0, None) / 32).astype(np.float32); m1[255] = 0.0
    c["m1"] = m1
    return c


_CACHE = {}


def kernel(x, c, positions, ada_w, ada_b, norm_g, w_in, cmp_pos, cmp_k_w1, cmp_k_w2, cmp_v_w1, cmp_v_w2,
           q_norm_g, w_q_up, kv_norm_g, w_kv_up, w_out, final_norm_g, _dbg=(), _stop=99):
    x = np.asarray(x, np.float32); c = np.asarray(c, np.float32); positions = np.asarray(positions, np.int32)
    key = (tuple(_dbg), _stop)
    if key not in _CACHE:
        _CACHE[key] = build(_dbg, _stop)
    nc, dbg_out = _CACHE[key]
    shared = {
        "ada_w": np.asarray(ada_w, np.float32)[0], "ada_b": np.asarray(ada_b, np.float32)[0], "norm_g": np.asarray(norm_g, np.float32)[0],
        "w_in": np.asarray(w_in, np.float32)[0], "cmp_pos": np.asarray(cmp_pos, np.float32)[0],
        "cmp_k_w1": np.asarray(cmp_k_w1, np.float32)[0], "cmp_k_w2": np.asarray(cmp_k_w2, np.float32)[0],
        "cmp_v_w1": np.asarray(cmp_v_w1, np.float32)[0], "cmp_v_w2": np.asarray(cmp_v_w2, np.float32)[0],
        "q_norm_g": np.asarray(q_norm_g, np.float32)[0], "w_q_up": np.asarray(w_q_up, np.float32)[0],
        "kv_norm_g": np.asarray(kv_norm_g, np.float32)[0], "w_kv_up": np.asarray(w_kv_up, np.float32)[0],
        "w_out": np.asarray(w_out, np.float32)[0], "final_norm_g": np.asarray(final_norm_g, np.float32),
    }
    shared["g_col"] = np.ascontiguousarray(shared["norm_g"].reshape(8, 128).T)
    shared["gq_col"] = np.ascontiguousarray(shared["q_norm_g"].reshape(2, 128).T)
    shared["gkv_col"] = np.ascontiguousarray(shared["kv_norm_g"].reshape(1, 128).T)
    shared["cmp_posT"] = np.ascontiguousarray(shared["cmp_pos"].T)
    consts = [_constants(0), _constants(1)]
    in_maps = []
    for core in range(8):
        b, par = divmod(core, 2)
        if par == 1:
            xs = x[b]; ps = positions[b]; valid = np.ones(S_LEN, np.float32)
        else:
            xs = np.concatenate([np.zeros((512, D), np.float32), x[b, :S_LEN - 512]], axis=0)
            ps = np.concatenate([np.zeros(512, np.int32), positions[b, :S_LEN - 512]])
            valid = np.concatenate([np.zeros(512, np.float32), np.ones(S_LEN - 512, np.float32)])
        m = {"xs": np.ascontiguousarray(xs), "pos_i": np.ascontiguousarray(ps), "valid": valid, "c_b": np.ascontiguousarray(c[b])}
        m["valid_pt"] = np.ascontiguousarray(valid.reshape(NT, 128).T)
        m["pos_pt"] = np.ascontiguousarray(ps.reshape(NT, 128).T)
        m["c_col"] = np.ascontiguousarray(c[b].reshape(8, 128).T)
        nidx = np.minimum(np.arange(256), 254)
        vc_ = valid[16 * nidx].copy(); vc_[255] = 0.0
        cp_ = ps[16 * nidx + 31].copy(); cp_[255] = 0
        m["validc_pt"] = np.ascontiguousarray(vc_.reshape(2, 128).T.astype(np.float32))
        m["cpos_pt"] = np.ascontiguousarray(cp_.reshape(2, 128).T.astype(np.int32))
        m.update(shared)
        m.update(consts[par])
        in_maps.append(m)
    res = run_bass_kernel_spmd(nc, in_maps, core_ids=list(range(8)))
    out = np.zeros((4, S_LEN, D), np.float32)
    for core in range(8):
        b, par = divmod(core, 2)
        o = np.asarray(res.results[core]["out"], np.float32)
        for s in range(4):
            qb = 2 * s + par
            out[b, qb * 512:(qb + 1) * 512, :] = o[s * 512:(s + 1) * 512, :]
    if _dbg:
        kernel.last_dbg = [{k: np.asarray(res.results[core]["dbg_" + k]) for k in dbg_out} for core in range(8)]
    return out
```
